# Optimizing a Trainium2 kernel written in Bass

```python
import jax, jax.numpy as jnp
from jax import lax
import numpy as np

D_MODEL = 2048
BATCH = 4
SEQ = 4096
DEPTH = 1

A_HEADS = 8
A_HEAD_DIM = 128
A_WIDTH = A_HEADS * A_HEAD_DIM
CONV_WIDTH = 4
CHUNK_A = 64
B_GROUPS = 8
B_GROUP_DIM = 128
B_WIDTH = B_GROUPS * B_GROUP_DIM
CHUNK_B = 128
MIX_WIDTH = A_WIDTH + B_WIDTH
IN_SIZES = (3 * A_WIDTH, A_WIDTH, A_HEADS, A_HEADS, B_WIDTH, B_WIDTH, B_WIDTH)
IN_WIDTH = int(sum(IN_SIZES))
IN_SPLITS = tuple(int(s) for s in np.cumsum(IN_SIZES)[:-1])
EPS = 1e-6

kernel_name = "hymba_style_gdn_gmlp_hybrid"


def rms_norm(x, w):
    xf = x.astype(jnp.float32)
    y = xf * lax.rsqrt(jnp.mean(xf * xf, axis=-1, keepdims=True) + EPS)
    return (y * w.astype(jnp.float32)).astype(x.dtype)


def layer_norm(x, w, b):
    xf = x.astype(jnp.float32)
    mu = jnp.mean(xf, axis=-1, keepdims=True)
    var = jnp.mean(jnp.square(xf - mu), axis=-1, keepdims=True)
    y = (xf - mu) * lax.rsqrt(var + EPS)
    return (y * w.astype(jnp.float32) + b.astype(jnp.float32)).astype(x.dtype)


def l2_normalize(x):
    return x * lax.rsqrt(jnp.sum(x * x, axis=-1, keepdims=True) + EPS)


def causal_depthwise_conv(x, w):
    C = x.shape[-1]
    return lax.conv_general_dilated(
        x, w[:, None, :].astype(x.dtype), window_strides=(1,),
        padding=[(CONV_WIDTH - 1, 0)],
        dimension_numbers=("NWC", "WIO", "NWC"), feature_group_count=C)


def chunk_gated_delta_rule(q, k, v, g, beta):
    B, T, H, D = q.shape
    N, C = T // CHUNK_A, CHUNK_A
    chunk4 = lambda t: t.reshape(B, N, C, H, D).transpose(0, 3, 1, 2, 4)
    chunk3 = lambda t: t.reshape(B, N, C, H).transpose(0, 3, 1, 2)
    q = chunk4(q) * (D ** -0.5)
    k, v = chunk4(k), chunk4(v)
    beta, g = chunk3(beta), chunk3(g)
    kb = k * beta[..., None]
    vb = v * beta[..., None]
    gc = jnp.cumsum(g, axis=-1)
    incl = jnp.tril(jnp.ones((C, C), dtype=bool))
    strict = jnp.tril(jnp.ones((C, C), dtype=bool), k=-1)
    diff = gc[..., :, None] - gc[..., None, :]
    decay = jnp.where(incl, jnp.exp(jnp.where(incl, diff, 0.0)), 0.0)
    L = jnp.where(strict, jnp.einsum('bhnid,bhnjd->bhnij', kb, k) * decay, 0.0)
    rhs = jnp.concatenate([vb, kb * jnp.exp(gc)[..., None]], axis=-1)
    sol = lax.linalg.triangular_solve(L + jnp.eye(C, dtype=L.dtype), rhs,
                                      left_side=True, lower=True, unit_diagonal=True)
    u, w = sol[..., :D], sol[..., D:]
    attn = jnp.where(incl, jnp.einsum('bhnid,bhnjd->bhnij', q, k) * decay, 0.0)
    xs = tuple(jnp.moveaxis(t, 2, 0) for t in (q, k, u, w, attn, gc))

    def step(S, inp):
        qi, ki, ui, wi, ai, gi = inp
        v_new = ui - jnp.einsum('bhcd,bhde->bhce', wi, S)
        o = (jnp.einsum('bhcd,bhde->bhce', qi * jnp.exp(gi)[..., None], S)
             + jnp.einsum('bhij,bhje->bhie', ai, v_new))
        g_last = gi[..., -1]
        k_dec = ki * jnp.exp(g_last[..., None] - gi)[..., None]
        S = S * jnp.exp(g_last)[..., None, None] + jnp.einsum('bhcd,bhce->bhde', k_dec, v_new)
        return S, o

    S0 = jnp.zeros((B, H, D, D), dtype=jnp.float32)
    _, o = lax.scan(step, S0, xs)
    return o.transpose(1, 0, 3, 2, 4).reshape(B, T, H, D)


def chunked_causal_sgu(v, w_s, b_s):
    Bn, T, _ = v.shape
    N = T // CHUNK_B
    vc = v.reshape(Bn, N, CHUNK_B, B_GROUPS, B_GROUP_DIM)
    mask = jnp.tril(jnp.ones((CHUNK_B, CHUNK_B), dtype=bool))
    w_m = jnp.where(mask[None], w_s, 0.0).astype(v.dtype)
    s = jnp.einsum('gts,bnsgc->bntgc', w_m, vc) + b_s.T.astype(v.dtype)[None, None, :, :, None]
    return s.reshape(Bn, T, B_WIDTH)


def setup_inputs(seed: int = 0) -> dict:
    key = jax.random.key(seed)
    ks = jax.random.split(key, 14)
    f32 = jnp.float32
    x = jax.random.normal(ks[0], (BATCH, SEQ, D_MODEL), f32)
    norm_w = 1.0 + 0.02 * jax.random.normal(ks[1], (DEPTH, D_MODEL), f32)
    w_in = jax.random.normal(ks[2], (DEPTH, D_MODEL, IN_WIDTH), f32) * D_MODEL ** -0.5
    conv_w = jax.random.normal(ks[3], (DEPTH, CONV_WIDTH, 3 * A_WIDTH), f32) * CONV_WIDTH ** -0.5
    a_log = jnp.log(jax.random.uniform(ks[4], (DEPTH, A_HEADS), f32, 1.0, 16.0))
    dt = jnp.exp(jax.random.uniform(ks[5], (DEPTH, A_HEADS), f32,
                                    np.log(1e-3).astype(np.float32), np.log(1e-1).astype(np.float32)))
    dt_bias = dt + jnp.log(-jnp.expm1(-dt))
    head_norm_w = 1.0 + 0.02 * jax.random.normal(ks[6], (DEPTH, A_HEAD_DIM), f32)
    sgu_ln_w = 1.0 + 0.02 * jax.random.normal(ks[7], (DEPTH, B_WIDTH), f32)
    sgu_ln_b = 0.02 * jax.random.normal(ks[8], (DEPTH, B_WIDTH), f32)
    w_spatial = jax.random.normal(ks[9], (DEPTH, B_GROUPS, CHUNK_B, CHUNK_B), f32) * CHUNK_B ** -0.5
    b_spatial = 1.0 + 0.02 * jax.random.normal(ks[10], (DEPTH, B_GROUPS, CHUNK_B), f32)
    w_out = jax.random.normal(ks[11], (DEPTH, MIX_WIDTH, D_MODEL), f32) * MIX_WIDTH ** -0.5
    final_norm_w = 1.0 + 0.02 * jax.random.normal(ks[12], (D_MODEL,), f32)
    return {"x": x, "norm_w": norm_w, "w_in": w_in, "conv_w": conv_w, "a_log": a_log,
            "dt_bias": dt_bias, "head_norm_w": head_norm_w, "sgu_ln_w": sgu_ln_w,
            "sgu_ln_b": sgu_ln_b, "w_spatial": w_spatial, "b_spatial": b_spatial,
            "w_out": w_out, "final_norm_w": final_norm_w}


def reference(x, norm_w, w_in, conv_w, a_log, dt_bias, head_norm_w, sgu_ln_w,
              sgu_ln_b, w_spatial, b_spatial, w_out, final_norm_w):
    Bn, T, _ = x.shape
    h = x
    for l in range(DEPTH):
        xn = rms_norm(h, norm_w[l])
        proj = xn @ w_in[l].astype(x.dtype)
        qkv, z_a, b_raw, a_raw, u_b, v_b, z_b = jnp.split(proj, IN_SPLITS, axis=-1)

        qkv = jax.nn.silu(causal_depthwise_conv(qkv, conv_w[l])).astype(jnp.float32)
        q, k, v = jnp.split(qkv, 3, axis=-1)
        q = l2_normalize(q.reshape(Bn, T, A_HEADS, A_HEAD_DIM))
        k = l2_normalize(k.reshape(Bn, T, A_HEADS, A_HEAD_DIM))
        v = v.reshape(Bn, T, A_HEADS, A_HEAD_DIM)
        beta = jax.nn.sigmoid(b_raw.astype(jnp.float32))
        g = -jnp.exp(a_log[l].astype(jnp.float32)) * jax.nn.softplus(
            a_raw.astype(jnp.float32) + dt_bias[l].astype(jnp.float32))
        o_a = chunk_gated_delta_rule(q, k, v, g, beta)
        o_a = rms_norm(o_a, head_norm_w[l]).astype(x.dtype)
        o_a = (o_a * jax.nn.silu(z_a.reshape(Bn, T, A_HEADS, A_HEAD_DIM))).reshape(Bn, T, A_WIDTH)

        v_n = layer_norm(v_b, sgu_ln_w[l], sgu_ln_b[l])
        o_b = u_b * chunked_causal_sgu(v_n, w_spatial[l], b_spatial[l]) * jax.nn.silu(z_b)

        mix = jnp.concatenate([o_a, o_b], axis=-1) @ w_out[l].astype(x.dtype)
        h = h + mix
    return rms_norm(h, final_norm_w)
```

```python
import numpy as np
from contextlib import ExitStack
import concourse.bass as bass
import concourse.mybir as mybir
from concourse.bass_utils import run_bass_kernel_spmd

F32 = mybir.dt.float32
BF16 = mybir.dt.bfloat16
ALU = mybir.AluOpType
AF = mybir.ActivationFunctionType

D = 2048
NCH = 16
NH = 8
EPS = 1e-6
SAME_SYNC = True
NDS = 8
GCH = 5
RRES = 8

C_ID, C_LOW, C_UP, C_BD16, C_C32, C_C64, C_C128, C_ONES, C_TRIL, C_PLOW, C_PUP = range(11)
NCST = 11


def make_consts():
    p = np.arange(128)[:, None]
    f = np.arange(128)[None, :]
    c = np.zeros((128, NCST, 128), np.float32)
    c[:, C_ID] = (p == f)
    c[:, C_LOW] = (f < p)
    c[:, C_UP] = (f >= p)
    bd = lambda b: ((p // b) == (f // b)).astype(np.float32)
    c[:, C_BD16] = bd(16)
    c[:, C_C32] = bd(32) - bd(16)
    c[:, C_C64] = bd(64) - bd(32)
    c[:, C_C128] = 1.0 - bd(64)
    c[:, C_ONES] = 1.0
    c[:, C_TRIL] = (f <= p)
    c[:, C_PLOW] = 30000.0 * (f >= p)
    c[:, C_PUP] = 30000.0 * (f < p)
    return c


class StopBuild(Exception):
    pass


STOP = [0]


class Buf:
    __slots__ = ("name", "lw", "rd", "excl")

    def __init__(self, name, excl=False):
        self.name = name
        self.lw = None
        self.rd = {}
        self.excl = excl


class Ctx:
    def __init__(self, nc, es):
        self.nc = nc
        self.es = es
        self.eng = {'pe': nc.tensor, 'act': nc.scalar, 'dve': nc.vector, 'pool': nc.gpsimd, 'sp': nc.sync}
        self.sem = {k: es.enter_context(nc.semaphore("s_" + k)) for k in ['pe', 'act', 'dve', 'pool']}
        self.cnt = {k: 0 for k in self.sem}
        self.dsem = [es.enter_context(nc.semaphore(f"dq{i}")) for i in range(NDS)]
        self.dcnt = [0] * NDS
        self.dn = 0
        self.waited = {}
        self.nins = 0
        self.cur = es
        self.nsb = 0

    def sb(self, name, shape, dt):
        self.nsb += 1
        return self.cur.enter_context(self.nc.sbuf_tensor(f"{name}_{self.nsb}", shape, dt))

    def barrier(self):
        for e in ['pe', 'act', 'dve', 'pool', 'sp']:
            for k in self.sem:
                if k != e and self.cnt[k] > 0:
                    self._wait(e, k, self.cnt[k])
            for i in range(NDS):
                if self.dcnt[i] > 0:
                    self._wait(e, ('d', i), self.dcnt[i])

    def _wait(self, e, key, val):
        if self.waited.get((e, key), 0) >= val:
            return
        sem = self.sem[key] if isinstance(key, str) else self.dsem[key[1]]
        self.eng[e].wait_ge(sem, val)
        self.waited[(e, key)] = val

    def _deps(self, e, reads, writes):
        need = {}
        for b in reads:
            if b.lw is not None and need.get(b.lw[0], 0) < b.lw[1]:
                need[b.lw[0]] = b.lw[1]
        same = need.get(e, 0)
        for b in writes:
            if b.lw is not None and b.lw[0] != e and need.get(b.lw[0], 0) < b.lw[1]:
                need[b.lw[0]] = b.lw[1]
            for k, v in b.rd.items():
                if k != e and need.get(k, 0) < v:
                    need[k] = v
        for k, v in need.items():
            if k == e and (e == 'pe' or not SAME_SYNC):
                continue
            self._wait(e, k, v)

    def op(self, e, fn, reads=(), writes=(), signal=True):
        ex = [b for b in reads if b.excl]
        if ex:
            reads = [b for b in reads if not b.excl]
            writes = list(writes) + ex
        self._deps(e, reads, writes)
        ins = fn(self.eng[e])
        self.nins += 1
        if signal:
            self.cnt[e] += 1
            ins.then_inc(self.sem[e], 1)
            v = self.cnt[e]
        else:
            v = self.cnt[e] + 1
        for b in reads:
            if b.rd.get(e, 0) < v:
                b.rd[e] = v
        for b in writes:
            b.lw = (e, v)
            b.rd = {}

    def dma(self, out_ap, in_ap, reads=(), writes=(), q='sp'):
        i = self.dn % NDS
        self.dn += 1
        key = ('d', i)
        if self.dcnt[i] > 0:
            self._wait(q, key, self.dcnt[i])
        self._deps(q, reads, writes)
        self.dcnt[i] += 16
        self.eng[q].dma_start(out=out_ap, in_=in_ap).then_inc(self.dsem[i], 16)
        self.nins += 1
        v = self.dcnt[i]
        for b in reads:
            if b.rd.get(key, 0) < v:
                b.rd[key] = v
        for b in writes:
            b.lw = (key, v)
            b.rd = {}

    def finish(self, q='sp'):
        for i in range(NDS):
            if self.dcnt[i] > 0:
                self.eng[q].wait_ge(self.dsem[i], self.dcnt[i])


class Ring:
    def __init__(self, items):
        self.items = items
        self.i = 0

    def next(self):
        it = self.items[self.i % len(self.items)]
        self.i += 1
        return it


class WRing:
    def __init__(self, cx, nws, nwb):
        self.cx = cx
        self.f = [cx.sb(f"wst_f{i}", [128, NCH, 128], F32) for i in range(nws)]
        self.bf = [Buf(f"wstf{i}") for i in range(nws)]
        self.w = [cx.sb(f"wst_b{i}", [128, NCH, 128], BF16) for i in range(nwb)]
        self.bw = [Buf(f"wstb{i}") for i in range(nwb)]
        self.i = 0


def build_program(NT, dbg=False):
    TM = NT * 128
    TB = min(512, TM)
    NTB = TM // TB
    nc = bass.Bass("TRN2", target_bir_lowering=False)
    dt_in = lambda n, s: nc.dram_tensor(n, s, F32, kind="ExternalInput").ap()
    xcat = dt_in("xcat", [2 * TM, D])
    w_fm = dt_in("w_fm", [48, 128, NCH, 128])
    w_vb = dt_in("w_vb", [128, NCH, 1024])
    w_ba = dt_in("w_ba", [128, NCH, 16])
    w_o = dt_in("w_o", [128, NCH, D])
    convw = dt_in("convw", [128, 24, 4])
    normw = dt_in("normw", [128, NCH])
    alog = dt_in("alog", [1, 8])
    dtb = dt_in("dtb", [1, 8])
    hnw = dt_in("hnw", [1, 128])
    lnw = dt_in("lnw", [1, 1024])
    lnb = dt_in("lnb", [1, 1024])
    wsp = dt_in("wsp", [128, 8, 128])
    bsp = dt_in("bsp", [1, 1024])
    fnw = dt_in("fnw", [1, D])
    cst = dt_in("cst", [128, NCST, 128])
    out = nc.dram_tensor("out", [TM, D], F32, kind="ExternalOutput").ap()
    oT_d = nc.dram_tensor("oT_d", [NCH, 128, TM], BF16, kind="Internal").ap()
    B_oTd = [Buf(f"oTd{c}") for c in range(NCH)]
    dbg_out = {}
    if dbg:
        for nm, shp in [("d_xnT", [128, NCH, TM]), ("d_q", [128, TM]), ("d_k", [128, TM]), ("d_v", [128, TM]),
                        ("d_beta", [128, NT, 8]), ("d_g", [128, NT, 8]), ("d_U", [128, 128]), ("d_N", [128, 128]),
                        ("d_S", [128, 8, 128]), ("d_oT", [128, NCH, TM]), ("d_E", [128, 128]),
                        ("d_opre", [128, 128]), ("d_vnew", [128, 128])]:
            dbg_out[nm] = nc.dram_tensor(nm, shp, F32, kind="ExternalOutput").ap()

    try:
      with ExitStack() as es:
        cx = Ctx(nc, es)
        sb = cx.sb

        def chk(n):
            if STOP[0] == n:
                cx.barrier()
                cx.finish()
                raise StopBuild()
        big = [es.enter_context(nc.psum_tensor(f"pbig{i}", [128, 512], F32)) for i in range(2)]
        bigR = Ring([(t, Buf(f"pbig{i}", True)) for i, t in enumerate(big)])
        NSM = 6
        smt = [es.enter_context(nc.psum_tensor(f"psm{i}", [128, 4, 128], F32)) for i in range(NSM)]
        smB = [Buf(f"psm{i}", True) for i in range(NSM)]
        smR = Ring([(smt[i][:, 0, :], smB[i]) for i in range(NSM)])

        def sm_full():
            i = smR.i % NSM
            smR.i += 1
            return smt[i], smB[i]

        cst_f = sb("cst_f", [128, NCST, 128], F32)
        cst_b = sb("cst_b", [128, 2, 128], BF16)
        B_cst = Buf("cst")
        cx.dma(cst_f[:], cst, writes=[B_cst])
        cx.op('dve', lambda e: e.tensor_copy(out=cst_b[:, 0, :], in_=cst_f[:, C_ID, :]), reads=[B_cst], writes=[B_cst])
        cx.op('dve', lambda e: e.tensor_copy(out=cst_b[:, 1, :], in_=cst_f[:, C_ONES, :]), reads=[B_cst], writes=[B_cst])
        idb = cst_b[:, 0, :]
        idf = cst_f[:, C_ID, :]
        onesb = cst_b[:, 1, :]
        onesf = cst_f[:, C_ONES, :]

        def mk(i):
            return cst_f[:, i, :]

        normw_t = sb("normw_t", [128, NCH], F32)
        B_nw = Buf("nw")
        cx.dma(normw_t[:], normw, writes=[B_nw])
        convw_t = sb("convw_t", [128, 24, 4], F32)
        B_cw = Buf("cw")
        cx.dma(convw_t[:], convw, writes=[B_cw])
        small = sb("small", [128, 64], F32)
        B_small = Buf("small")
        cx.dma(small[:, 0:8], alog.partition_broadcast(128), writes=[B_small])
        cx.dma(small[:, 8:16], dtb.partition_broadcast(128), writes=[B_small])
        negA = small[:, 16:24]
        cx.op('act', lambda e: e.activation(out=negA, in_=small[:, 0:8], func=AF.Exp), reads=[B_small], writes=[B_small])
        cx.op('dve', lambda e: e.tensor_scalar(out=negA, in0=negA, scalar1=-1.0, scalar2=None, op0=ALU.mult),
              reads=[B_small], writes=[B_small])
        dtb_t = small[:, 8:16]
        eps_c = small[:, 24:25]
        one_c = small[:, 25:26]
        cx.op('pool', lambda e: e.memset(small[:, 24:25], EPS), writes=[B_small])
        cx.op('pool', lambda e: e.memset(small[:, 25:26], 1.0), writes=[B_small])

        def rsqrt(out_ap, in_ap, scale, rd, wr):
            cx.op('act', lambda e: e.activation(out=out_ap, in_=in_ap, func=AF.Ln, scale=scale, bias=eps_c),
                  reads=list(rd) + [B_small], writes=wr)
            cx.op('act', lambda e: e.activation(out=out_ap, in_=out_ap, func=AF.Exp, scale=-0.5), reads=wr, writes=wr)
        hnw_t = sb("hnw_t", [128, 128], F32)
        B_hnw = Buf("hnw")
        cx.dma(hnw_t[:], hnw.partition_broadcast(128), writes=[B_hnw])
        wba_f = sb("wba_f", [128, NCH, 16], F32)
        wba_b = sb("wba_b", [128, NCH, 16], BF16)
        B_wba = Buf("wba")
        cx.dma(wba_f[:], w_ba, writes=[B_wba])
        cx.op('dve', lambda e: e.tensor_tensor(out=wba_b[:], in0=wba_f[:],
                                               in1=normw_t[:].unsqueeze(2).to_broadcast([128, NCH, 16]), op=ALU.mult),
              reads=[B_wba, B_nw], writes=[B_wba])
        xnT = sb("xnT", [128, NCH, TM], BF16)
        B_xnT = [Buf(f"xnT{t}") for t in range(NT)]
        xh = sb("xh", [128, NCH, 4], BF16)
        B_xh = Buf("xh")
        st16 = sb("st16", [128, 64], F32)
        B_st = [Buf(f"st{i}") for i in range(4)]
        beta_all = sb("beta_all", [128, NT, 8], F32)
        g_all = sb("g_all", [128, NT, 8], F32)
        gc_all = sb("gc_all", [128, NT, 8], F32)
        egc_all = sb("egc_all", [128, NT, 8], F32)
        kds_all = sb("kds_all", [128, NT, 8], F32)
        edec_all = sb("edec_all", [128, NT, 8], F32)
        ngc_all = sb("ngc_all", [128, NT, 8], F32)
        B_tok = Buf("tokscal")
        S_f = sb("S_f", [128, NH, 128], F32)
        S_b = sb("S_b", [128, NH, 128], BF16)
        B_S = [Buf(f"S{h}") for h in range(NH)]
        cx.op('pool', lambda e: e.memset(S_f[:], 0.0), writes=B_S)
        cx.op('pool', lambda e: e.memset(S_b[:], 0.0), writes=B_S)
        cx.op('pool', lambda e: e.memset(xh[:], 0.0), writes=[B_xh])
        fl = lambda t: t[:].rearrange("p a b -> p (a b)")
        chk(1)

        def load_w(wr, src_ap, fold=True):
            i = wr.i
            wr.i += 1
            f, bfb = wr.f[i % len(wr.f)], wr.bf[i % len(wr.f)]
            w, bwb = wr.w[i % len(wr.w)], wr.bw[i % len(wr.w)]
            cx.dma(f[:], src_ap, writes=[bfb])
            eng = 'pool' if (i % 2 == 0) else 'dve'
            if fold:
                cx.op(eng, lambda e: e.tensor_tensor(out=w[:], in0=f[:],
                                                     in1=normw_t[:].unsqueeze(2).to_broadcast([128, NCH, 128]), op=ALU.mult),
                      reads=[bfb, B_nw], writes=[bwb])
            else:
                cx.op(eng, lambda e: e.tensor_copy(out=w[:], in_=f[:]), reads=[bfb], writes=[bwb])
            return w, bwb

        def inproj_fm(w, bw, dst_fn, bdst, func):
            for tb in range(NTB):
                ps, bps = bigR.next()
                for c in range(NCH):
                    cx.op('pe', lambda e: e.matmul(ps[:, 0:TB], lhsT=w[:, c, :], rhs=xnT[:, c, tb * TB:(tb + 1) * TB],
                                                   start=(c == 0), stop=(c == NCH - 1)),
                          reads=[bw] + B_xnT[tb * TB // 128:(tb + 1) * TB // 128], writes=[bps],
                          signal=(c == NCH - 1))
                cx.op('act', lambda e: e.activation(out=dst_fn(tb), in_=ps[:, 0:TB], func=func), reads=[bps], writes=[bdst])

        def phase(ph):
            main = (ph == 'M')
            tok0 = TM if main else 0
            with ExitStack() as sx:
                cx.cur = sx
                xin = [sb(f"xin{i}", [128, D], F32) for i in range(2)]
                B_xin = [Buf("xin0"), Buf("xin1")]
                xs = [sb(f"xs{i}", [128, D], BF16) for i in range(2)]
                B_xs = [Buf("xs0"), Buf("xs1")]
                for ti in range(NT):
                    xi, bxi = xin[ti % 2], B_xin[ti % 2]
                    xsi, bxs = xs[ti % 2], B_xs[ti % 2]
                    cx.dma(xi[:], xcat[tok0 + ti * 128: tok0 + (ti + 1) * 128, :], writes=[bxi])
                    ssq = st16[:, ti % 2: ti % 2 + 1]
                    rstd = st16[:, 2 + ti % 2: 3 + ti % 2]
                    bs = B_st[ti % 2]
                    cx.op('act', lambda e: e.activation(out=xsi[:], in_=xi[:], func=AF.Square, accum_out=ssq),
                          reads=[bxi], writes=[bxs, bs])
                    rsqrt(rstd, ssq, 1.0 / D, [bs], [bs])
                    cx.op('act', lambda e: e.activation(out=xsi[:], in_=xi[:], func=AF.Copy, scale=rstd),
                          reads=[bxi, bs], writes=[bxs])
                    for q4 in range(4):
                        pst, bpst = sm_full()
                        for j in range(4):
                            c = q4 * 4 + j
                            cx.op('pe', lambda e: e.matmul(pst[:, j, :], lhsT=xsi[:, c * 128:(c + 1) * 128], rhs=idb,
                                                           start=True, stop=True),
                                  reads=[bxs, B_cst], writes=[bpst], signal=(j == 3))
                        cx.op('dve' if q4 % 2 == 0 else 'act',
                              (lambda e: e.tensor_copy(out=xnT[:, q4 * 4:(q4 + 1) * 4, ti * 128:(ti + 1) * 128], in_=pst[:]))
                              if q4 % 2 == 0 else
                              (lambda e: e.activation(out=xnT[:, q4 * 4:(q4 + 1) * 4, ti * 128:(ti + 1) * 128], in_=pst[:], func=AF.Copy)),
                              reads=[bpst], writes=[B_xnT[ti]])
                cx.barrier()
                if not main:
                    chk(2)
            cx.cur = es
            for ti in range(NT):
                ps, bps = smR.next()
                for c in range(NCH):
                    cx.op('pe', lambda e: e.matmul(ps[:, 0:16], lhsT=xnT[:, c, ti * 128:(ti + 1) * 128],
                                                   rhs=wba_b[:, c, :], start=(c == 0), stop=(c == NCH - 1)),
                          reads=[B_xnT[ti], B_wba], writes=[bps], signal=(c == NCH - 1))
                cx.op('act', lambda e: e.activation(out=beta_all[:, ti, :], in_=ps[:, 0:8], func=AF.Exp, scale=-1.0),
                      reads=[bps], writes=[B_tok])
                cx.op('dve', lambda e: e.tensor_copy(out=g_all[:, ti, :], in_=ps[:, 8:16]), reads=[bps], writes=[B_tok])
            TA = NT * 8
            cx.op('dve', lambda e: e.tensor_scalar(out=fl(beta_all), in0=fl(beta_all), scalar1=1.0, scalar2=None,
                                                   op0=ALU.add), reads=[B_tok], writes=[B_tok])
            cx.op('dve', lambda e: e.reciprocal(out=fl(beta_all), in_=fl(beta_all)), reads=[B_tok], writes=[B_tok])
            cx.op('dve', lambda e: e.tensor_tensor(out=g_all[:], in0=g_all[:],
                                                   in1=dtb_t.unsqueeze(1).to_broadcast([128, NT, 8]), op=ALU.add),
                  reads=[B_tok, B_small], writes=[B_tok])
            cx.op('act', lambda e: e.activation(out=fl(g_all), in_=fl(g_all), func=AF.Exp), reads=[B_tok], writes=[B_tok])
            cx.op('act', lambda e: e.activation(out=fl(g_all), in_=fl(g_all), func=AF.Ln, bias=one_c),
                  reads=[B_tok, B_small], writes=[B_tok])
            cx.op('dve', lambda e: e.tensor_tensor(out=g_all[:], in0=g_all[:],
                                                   in1=negA.unsqueeze(1).to_broadcast([128, NT, 8]), op=ALU.mult),
                  reads=[B_tok, B_small], writes=[B_tok])
            for a0 in range(0, TA, 128):
                a1 = min(TA, a0 + 128)
                ps, bps = smR.next()
                cx.op('pe', lambda e: e.matmul(ps[:, 0:a1 - a0], lhsT=mk(C_UP), rhs=fl(g_all)[:, a0:a1],
                                               start=True, stop=True), reads=[B_tok, B_cst], writes=[bps])
                cx.op('dve', lambda e: e.tensor_copy(out=fl(gc_all)[:, a0:a1], in_=ps[:, 0:a1 - a0]),
                      reads=[bps], writes=[B_tok])
                ps2, bps2 = smR.next()
                cx.op('pe', lambda e: e.matmul(ps2[:, 0:a1 - a0], lhsT=onesf, rhs=fl(g_all)[:, a0:a1],
                                               start=True, stop=True), reads=[B_tok, B_cst], writes=[bps2])
                cx.op('dve', lambda e: e.tensor_tensor(out=fl(kds_all)[:, a0:a1], in0=ps2[:, 0:a1 - a0],
                                                       in1=fl(gc_all)[:, a0:a1], op=ALU.subtract),
                      reads=[bps2, B_tok], writes=[B_tok])
                cx.op('dve', lambda e: e.tensor_copy(out=fl(edec_all)[:, a0:a1], in_=ps2[:, 0:a1 - a0]),
                      reads=[bps2, B_tok], writes=[B_tok])
                cx.op('act', lambda e: e.activation(out=fl(edec_all)[:, a0:a1], in_=fl(edec_all)[:, a0:a1], func=AF.Exp),
                      reads=[B_tok], writes=[B_tok])
            cx.op('act', lambda e: e.activation(out=fl(kds_all), in_=fl(kds_all), func=AF.Exp), reads=[B_tok], writes=[B_tok])
            cx.op('act', lambda e: e.activation(out=fl(egc_all), in_=fl(gc_all), func=AF.Exp), reads=[B_tok], writes=[B_tok])
            cx.op('dve', lambda e: e.tensor_scalar(out=fl(ngc_all), in0=fl(gc_all), scalar1=-1.0, scalar2=None, op0=ALU.mult),
                  reads=[B_tok], writes=[B_tok])
            if not main:
                chk(3)
            if dbg and main:
                cx.dma(dbg_out["d_beta"], beta_all[:], reads=[B_tok])
                cx.dma(dbg_out["d_g"], g_all[:], reads=[B_tok])
                with ExitStack() as sd:
                    cx.cur = sd
                    dtmp = sb("dbg_xnT", [128, NCH, TM], F32)
                    bt = Buf("dbgx")
                    cx.op('dve', lambda e: e.tensor_copy(out=dtmp[:], in_=xnT[:]), reads=B_xnT, writes=[bt])
                    cx.dma(dbg_out["d_xnT"], dtmp[:], reads=[bt])
                    cx.barrier()
                cx.cur = es

            with ExitStack() as shd:
                cx.cur = shd
                wr = WRing(cx, 2, 2)
                pre = [sb(f"pre{j}", [128, TM + 4], BF16) for j in range(3)]
                B_pre = [Buf(f"pre{j}") for j in range(3)]
                act_T = [sb(f"actT{j}", [128, TM], BF16) for j in range(3)]
                B_act = [Buf(f"actT{j}") for j in range(3)]
                zT = sb("zT", [128, TM], BF16)
                B_zT = Buf("zT")
                zs = sb("zs", [128, NT, 128], BF16)
                B_zs = Buf("zs")
                sq = [sb(f"sq{j}", [128, TM], BF16) for j in range(1)]
                B_sq = [Buf("sq0")]
                diag = sb("diag", [128, 12, 128], BF16)
                B_diag = Buf("diag")
                hs = sb("hs", [128, 8, NT], F32)
                B_hs = Buf("hs")
                oTs = [sb(f"oTs{i}", [128, TM], BF16) for i in range(2)]
                B_oTs = [Buf("oTs0"), Buf("oTs1")]
                NTMP = GCH

                def tmpset(nm, dt, n=NTMP):
                    ts = [sb(f"{nm}{i}", [128, 128], dt) for i in range(n)]
                    return ts, [Buf(f"{nm}{i}") for i in range(n)]
                t_dg, B_dg = tmpset("dg", F32)
                t_El, B_El = tmpset("El", F32)
                t_Eu, B_Eu = tmpset("Eu", F32)
                t_N, B_N = tmpset("N", BF16)
                t_NT, B_NT = tmpset("NT", BF16)
                t_N16, B_N16 = tmpset("N16", BF16)
                t_N16T, B_N16T = tmpset("N16T", BF16)
                def pairset(nm, n):
                    ts = [sb(f"{nm}{i}", [128, 2, 128], BF16) for i in range(n)]
                    return ts, [Buf(f"{nm}{i}") for i in range(n)]
                t_PP, B_PP = pairset("PP", 2 * GCH)
                t_TU, B_TU = pairset("TU", 2 * GCH)
                t_XX, B_XX = pairset("XX", GCH)
                t_r, B_r = tmpset("r", BF16, 2)
                t_vn, B_vn = tmpset("vn", BF16, 2)
                t_o1, B_o1 = tmpset("o1", F32, 2)
                t_so1, B_so1 = tmpset("so1", F32, 2)
                r_op = sb("r_op", [128, min(NT, RRES), 128], F32)
                B_rop = [Buf(f"rop{t}") for t in range(NT)]
                t_og, B_og = tmpset("og", BF16, 2)
                t_junk, B_junk = tmpset("junk", BF16, 2)
                t_ms = sb("t_ms", [128, 2], F32)
                B_ms = [Buf("ms0"), Buf("ms1")]
                hn = sb("hn", [128, 4, NT], F32)
                B_hn = Buf("hn")
                r_U = sb("r_U", [128, min(NT, RRES), 128], BF16)
                r_at = sb("r_at", [128, min(NT, RRES), 128], BF16)
                r_kd = sb("r_kd", [128, min(NT, RRES), 128], BF16)
                r_vb = sb("r_vb", [128, min(NT, RRES), 128], BF16)
                B_rU = [Buf(f"rU{t}") for t in range(NT)]
                B_rat = [Buf(f"rat{t}") for t in range(NT)]
                B_rkd = [Buf(f"rkd{t}") for t in range(NT)]
                B_rvb = [Buf(f"rvb{t}") for t in range(NT)]
                HS = lambda i: hs[:, i, :]

                def head_groups(h):
                    return ([(0, h), (1, 8 + h), (2, 16 + h)] if main else [(1, 8 + h), (2, 16 + h)])

                def inproj_gen(h):
                    for j, g in head_groups(h) + ([(3, 24 + h)] if main else []):
                        w, bw = load_w(wr, w_fm[g])
                        dst, bdst = (pre[j], B_pre[j]) if j < 3 else (zT, B_zT)
                        off = 3 if j < 3 else 0
                        yield
                        for tb in range(NTB):
                            ps, bps = bigR.next()
                            for c in range(NCH):
                                cx.op('pe', lambda e: e.matmul(ps[:, 0:TB], lhsT=w[:, c, :], rhs=xnT[:, c, tb * TB:(tb + 1) * TB],
                                                               start=(c == 0), stop=(c == NCH - 1)),
                                      reads=[bw] + B_xnT[tb * TB // 128:(tb + 1) * TB // 128], writes=[bps],
                                      signal=(c == NCH - 1))
                                if c in (3, 7, 11):
                                    yield
                            cx.op('act', lambda e: e.activation(out=dst[:, off + tb * TB: off + (tb + 1) * TB], in_=ps[:, 0:TB],
                                                                func=AF.Copy), reads=[bps], writes=[bdst])
                            yield
                        if j < 3:
                            ps, bps = smR.next()
                            for c in range(NCH):
                                cx.op('pe', lambda e: e.matmul(ps[:, 0:3], lhsT=w[:, c, :], rhs=xh[:, c, 0:3],
                                                               start=(c == 0), stop=(c == NCH - 1)),
                                      reads=[bw, B_xh], writes=[bps], signal=(c == NCH - 1))
                            cx.op('act', lambda e: e.activation(out=dst[:, 0:3], in_=ps[:, 0:3], func=AF.Copy),
                                  reads=[bps], writes=[bdst])
                            yield

                for _ in inproj_gen(0):
                    pass
                for h in range(NH):
                    groups = head_groups(h)
                    for j, g in groups:
                        for k in range(4):
                            cx.op('pool', lambda e: e.tensor_scalar(out=diag[:, j * 4 + k, :], in0=idf,
                                                                    scalar1=convw_t[:, g, k:k + 1], scalar2=None,
                                                                    op0=ALU.mult),
                                  reads=[B_cst, B_cw], writes=[B_diag])
                    for j, g in groups:
                        for tb in range(NTB):
                            ps, bps = bigR.next()
                            for k in range(4):
                                cx.op('pe', lambda e: e.matmul(ps[:, 0:TB], lhsT=diag[:, j * 4 + k, :],
                                                               rhs=pre[j][:, tb * TB + k: tb * TB + k + TB],
                                                               start=(k == 0), stop=(k == 3)),
                                      reads=[B_diag, B_pre[j]], writes=[bps], signal=(k == 3))
                            cx.op('act', lambda e: e.activation(out=act_T[j][:, tb * TB:(tb + 1) * TB], in_=ps[:, 0:TB],
                                                                func=AF.Silu), reads=[bps], writes=[B_act[j]])
                    if not main and h == 0:
                        chk(4)
                    if dbg and main and h == 0:
                        for j, nm in enumerate(["d_q", "d_k", "d_v"]):
                            tmpf = sb(f"dbgf{j}", [128, TM], F32)
                            bt = Buf("dbgf")
                            cx.op('dve', lambda e: e.tensor_copy(out=tmpf[:], in_=act_T[j][:]), reads=[B_act[j]], writes=[bt])
                            cx.dma(dbg_out[nm], tmpf[:], reads=[bt])
                    for j in ([0, 1] if main else [1]):
                        s_t, bs_t = sq[0], B_sq[0]
                        cx.op('pool', lambda e: e.tensor_tensor(out=s_t[:], in0=act_T[j][:], in1=act_T[j][:], op=ALU.mult),
                              reads=[B_act[j]], writes=[bs_t])
                        ps, bps = smR.next()
                        for ti in range(NT):
                            cx.op('pe', lambda e: e.matmul(ps[:, ti:ti + 1], lhsT=s_t[:, ti * 128:(ti + 1) * 128],
                                                           rhs=onesb[:, 0:1], start=True, stop=True),
                                  reads=[bs_t, B_cst], writes=[bps], signal=(ti == NT - 1))
                        rsqrt(HS(j), ps[:, 0:NT], 1.0, [bps], [B_hs])
                    bh = beta_all[:, :, h]
                    cx.op('dve', lambda e: e.tensor_tensor(out=HS(3), in0=HS(1), in1=bh, op=ALU.mult),
                          reads=[B_hs, B_tok], writes=[B_hs])
                    cx.op('dve', lambda e: e.tensor_tensor(out=HS(7), in0=HS(3), in1=HS(1), op=ALU.mult),
                          reads=[B_hs], writes=[B_hs])
                    cx.op('dve', lambda e: e.tensor_tensor(out=HS(2), in0=HS(7), in1=egc_all[:, :, h], op=ALU.mult),
                          reads=[B_hs, B_tok], writes=[B_hs])
                    cx.op('dve', lambda e: e.reciprocal(out=HS(4), in_=HS(1)), reads=[B_hs], writes=[B_hs])
                    cx.op('dve', lambda e: e.tensor_tensor(out=HS(5), in0=HS(1), in1=kds_all[:, :, h], op=ALU.mult),
                          reads=[B_hs, B_tok], writes=[B_hs])
                    if main:
                        cx.op('dve', lambda e: e.tensor_scalar(out=HS(0), in0=HS(0), scalar1=float(128 ** -0.5), scalar2=None,
                                                               op0=ALU.mult), reads=[B_hs], writes=[B_hs])
                        for ti in range(NT):
                            tp, btp = smR.next()
                            cx.op('pe', lambda e: e.matmul(tp, lhsT=zT[:, ti * 128:(ti + 1) * 128], rhs=idb, start=True, stop=True),
                                  reads=[B_zT, B_cst], writes=[btp])
                            cx.op('act', lambda e: e.activation(out=zs[:, ti, :], in_=tp, func=AF.Silu),
                                  reads=[btp], writes=[B_zs])
                        cx.op('pool', lambda e: e.tensor_tensor(out=zs[:], in0=zs[:],
                                                                in1=hnw_t[:].unsqueeze(1).to_broadcast([128, NT, 128]), op=ALU.mult),
                              reads=[B_zs, B_hnw], writes=[B_zs])
                        cx.op('dve', lambda e: e.scalar_tensor_tensor(out=hn[:, 2, :], in0=HS(0), scalar=1.0 / 128, in1=HS(0),
                                                                      op0=ALU.mult, op1=ALU.mult), reads=[B_hs], writes=[B_hn])
                        cx.op('act', lambda e: e.activation(out=hn[:, 3, :], in_=HS(0), func=AF.Ln), reads=[B_hs], writes=[B_hn])
                    ot, bot = oTs[h % 2], B_oTs[h % 2]
                    if not main and h == 0:
                        chk(51)
                    cx.op('dve', lambda e: e.tensor_scalar(out=hn[:, 0, :], in0=HS(7), scalar1=-1.0, scalar2=None, op0=ALU.mult),
                          reads=[B_hs], writes=[B_hn])
                    cx.op('dve', lambda e: e.tensor_scalar(out=hn[:, 1, :], in0=HS(2), scalar1=-1.0, scalar2=None, op0=ALU.mult),
                          reads=[B_hs], writes=[B_hn])

                    def mm(lhsT, blh, rhs, brh):
                        ps, bps = smR.next()
                        cx.op('pe', lambda e: e.matmul(ps, lhsT=lhsT, rhs=rhs, start=True, stop=True),
                              reads=[blh, brh], writes=[bps])
                        return ps, bps

                    def evac_copy(dst, bdst, ps, bps):
                        cx.op('act', lambda e: e.activation(out=dst, in_=ps, func=AF.Copy), reads=[bps], writes=[bdst])

                    def evac_add(dst, bdst, ps, bps, addend, badd):
                        cx.op('dve', lambda e: e.scalar_tensor_tensor(out=dst, in0=ps, scalar=1.0, in1=addend,
                                                                      op0=ALU.mult, op1=ALU.add),
                              reads=[bps, badd], writes=[bdst])

                    def evac_mask(dst, bdst, ps, bps, mi):
                        cx.op('dve', lambda e: e.scalar_tensor_tensor(out=dst, in0=ps, scalar=1.0, in1=mk(mi),
                                                                      op0=ALU.mult, op1=ALU.mult),
                              reads=[bps, B_cst], writes=[bdst])

                    def prep_gen(ti, pp, h=h):
                        sl = slice(ti * 128, (ti + 1) * 128)
                        qT_t, kT_t, vT_t = act_T[0][:, sl], act_T[1][:, sl], act_T[2][:, sl]
                        col = lambda i: hs[:, i, ti:ti + 1]
                        gcc = gc_all[:, ti, h:h + 1]
                        t_ad_, B_ad_ = t_dg[pp], B_dg[pp]
                        t_E_, B_E_ = t_dg[pp], B_dg[pp]
                        cx.op('pool', lambda e: e.tensor_scalar(out=t_dg[pp][:], in0=idf, scalar1=gcc, scalar2=None, op0=ALU.mult),
                              reads=[B_cst, B_tok], writes=[B_dg[pp]])
                        yield
                        psG, bG = smR.next()
                        cx.op('pe', lambda e: e.matmul(psG, lhsT=onesf, rhs=t_dg[pp][:], start=True, stop=True),
                              reads=[B_cst, B_dg[pp]], writes=[bG])
                        cx.op('act', lambda e: e.activation(out=t_ad_[:], in_=psG, func=AF.Abs, bias=ngc_all[:, ti, h:h + 1]),
                              reads=[bG, B_tok], writes=[B_ad_])
                        cx.op('dve', lambda e: e.tensor_tensor(out=t_El[pp][:], in0=t_ad_[:], in1=mk(C_PLOW), op=ALU.add),
                              reads=[B_ad_, B_cst], writes=[B_El[pp]])
                        cx.op('act', lambda e: e.activation(out=t_El[pp][:], in_=t_El[pp][:], func=AF.Exp, scale=-1.0),
                              reads=[B_El[pp]], writes=[B_El[pp]])
                        if main:
                            cx.op('dve', lambda e: e.tensor_tensor(out=t_Eu[pp][:], in0=t_ad_[:], in1=mk(C_PUP), op=ALU.add),
                                  reads=[B_ad_, B_cst], writes=[B_Eu[pp]])
                            cx.op('act', lambda e: e.activation(out=t_Eu[pp][:], in_=t_Eu[pp][:], func=AF.Exp, scale=-1.0),
                                  reads=[B_Eu[pp]], writes=[B_Eu[pp]])
                        psKK, bKK = smR.next()
                        cx.op('pe', lambda e: e.matmul(psKK, lhsT=kT_t, rhs=kT_t, start=True, stop=True),
                              reads=[B_act[1]], writes=[bKK])
                        cx.op('dve', lambda e: e.scalar_tensor_tensor(out=t_N[pp][:], in0=psKK, scalar=hn[:, 0, ti:ti + 1], in1=t_El[pp][:],
                                                                      op0=ALU.mult, op1=ALU.mult),
                              reads=[bKK, B_hn, B_El[pp]], writes=[B_N[pp]])
                        cx.op('pool', lambda e: e.tensor_tensor(out=t_N16[pp][:], in0=t_N[pp][:], in1=mk(C_BD16), op=ALU.mult),
                              reads=[B_N[pp], B_cst], writes=[B_N16[pp]])
                        if main:
                            psQK, bQK = smR.next()
                            cx.op('pe', lambda e: e.matmul(psQK, lhsT=kT_t, rhs=qT_t, start=True, stop=True),
                                  reads=[B_act[1], B_act[0]], writes=[bQK])
                            cx.op('dve', lambda e: e.scalar_tensor_tensor(out=r_at[:, ti % RRES, :], in0=psQK, scalar=col(1), in1=t_Eu[pp][:],
                                                                          op0=ALU.mult, op1=ALU.mult),
                                  reads=[bQK, B_hs, B_Eu[pp]], writes=[B_rat[ti % RRES]])
                        yield
                        tpN, btpN = smR.next()
                        cx.op('pe', lambda e: e.matmul(tpN, lhsT=t_N[pp][:], rhs=idb, start=True, stop=True),
                              reads=[B_N[pp], B_cst], writes=[btpN])
                        cx.op('act', lambda e: e.activation(out=t_NT[pp][:], in_=tpN, func=AF.Copy), reads=[btpN], writes=[B_NT[pp]])
                        cx.op('pool', lambda e: e.tensor_tensor(out=t_N16T[pp][:], in0=t_NT[pp][:], in1=mk(C_BD16), op=ALU.mult),
                              reads=[B_NT[pp], B_cst], writes=[B_N16T[pp]])
                        PP = lambda i: (t_PP[pp * 2 + i], B_PP[pp * 2 + i])
                        TU = lambda i: (t_TU[pp * 2 + i], B_TU[pp * 2 + i])
                        XX = (t_XX[pp], B_XX[pp])
                        tu0, btu0 = TU(0)
                        cx.op('pool', lambda e: e.tensor_tensor(out=tu0[:, 0, :], in0=t_N16[pp][:], in1=idf, op=ALU.add),
                              reads=[B_N16[pp], B_cst], writes=[btu0])
                        cx.op('pool', lambda e: e.tensor_tensor(out=tu0[:, 1, :], in0=t_N16T[pp][:], in1=idf, op=ALU.add),
                              reads=[B_N16T[pp], B_cst], writes=[btu0])
                        Pk, PkT, bPk = t_N16[pp][:], t_N16T[pp][:], [B_N16[pp], B_N16T[pp]]
                        cur = 0
                        for lvl in range(3):
                            last = (lvl == 2)
                            npp, bnpp = PP(lvl % 2)
                            yield
                            pst, bpst = sm_full()
                            cx.op('pe', lambda e: e.matmul(pst[:, 0, :], lhsT=PkT, rhs=Pk, start=True, stop=True),
                                  reads=bPk, writes=[bpst], signal=last)
                            if not last:
                                cx.op('pe', lambda e: e.matmul(pst[:, 1, :], lhsT=Pk, rhs=PkT, start=True, stop=True),
                                      reads=bPk, writes=[bpst])
                                cx.op('act', lambda e: e.activation(out=npp[:, 0:2, :], in_=pst[:, 0:2, :], func=AF.Copy),
                                      reads=[bpst], writes=[bnpp])
                            else:
                                cx.op('act', lambda e: e.activation(out=npp[:, 0, :], in_=pst[:, 0, :], func=AF.Copy),
                                      reads=[bpst], writes=[bnpp])
                            t0, bt0 = TU(cur)
                            t1, bt1 = TU(1 - cur)
                            yield
                            pst, bpst = sm_full()
                            cx.op('pe', lambda e: e.matmul(pst[:, 0, :], lhsT=t0[:, 1, :], rhs=npp[:, 0, :], start=True, stop=False),
                                  reads=[bt0, bnpp], writes=[bpst], signal=False)
                            cx.op('pe', lambda e: e.matmul(pst[:, 0, :], lhsT=idb, rhs=t0[:, 0, :], start=False, stop=True),
                                  reads=[bt0, B_cst], writes=[bpst], signal=False)
                            cx.op('pe', lambda e: e.matmul(pst[:, 1, :], lhsT=npp[:, 0, :], rhs=t0[:, 1, :], start=True, stop=False),
                                  reads=[bt0, bnpp], writes=[bpst], signal=False)
                            cx.op('pe', lambda e: e.matmul(pst[:, 1, :], lhsT=idb, rhs=t0[:, 1, :], start=False, stop=True),
                                  reads=[bt0, B_cst], writes=[bpst])
                            cx.op('act', lambda e: e.activation(out=t1[:, 0:2, :], in_=pst[:, 0:2, :], func=AF.Copy),
                                  reads=[bpst], writes=[bt1])
                            cur = 1 - cur
                            if not last:
                                Pk, PkT, bPk = npp[:, 0, :], npp[:, 1, :], [bnpp]
                        for mi_, lastm in [(C_C32, False), (C_C64, False), (C_C128, True)]:
                            t0, bt0 = TU(cur)
                            t1, bt1 = TU(1 - cur)
                            xx, bxx = XX
                            yield
                            pst, bpst = sm_full()
                            if not lastm:
                                cx.op('pe', lambda e: e.matmul(pst[:, 0, :], lhsT=t_NT[pp][:], rhs=t0[:, 0, :], start=True, stop=True),
                                      reads=[B_NT[pp], bt0], writes=[bpst], signal=False)
                            cx.op('pe', lambda e: e.matmul(pst[:, 1, :], lhsT=t_N[pp][:], rhs=t0[:, 1, :], start=True, stop=True),
                                  reads=[B_N[pp], bt0], writes=[bpst])
                            if not lastm:
                                cx.op('dve', lambda e: e.scalar_tensor_tensor(out=xx[:, 0:2, :], in0=pst[:, 0:2, :], scalar=1.0,
                                                                              in1=mk(mi_).unsqueeze(1).to_broadcast([128, 2, 128]),
                                                                              op0=ALU.mult, op1=ALU.mult),
                                      reads=[bpst, B_cst], writes=[bxx])
                            else:
                                cx.op('dve', lambda e: e.scalar_tensor_tensor(out=xx[:, 1, :], in0=pst[:, 1, :], scalar=1.0, in1=mk(mi_),
                                                                              op0=ALU.mult, op1=ALU.mult),
                                      reads=[bpst, B_cst], writes=[bxx])
                            yield
                            pst, bpst = sm_full()
                            if not lastm:
                                cx.op('pe', lambda e: e.matmul(pst[:, 0, :], lhsT=t0[:, 1, :], rhs=xx[:, 0, :], start=True, stop=True),
                                      reads=[bt0, bxx], writes=[bpst], signal=False)
                            cx.op('pe', lambda e: e.matmul(pst[:, 1, :], lhsT=t0[:, 0, :], rhs=xx[:, 1, :], start=True, stop=True),
                                  reads=[bt0, bxx], writes=[bpst])
                            if not lastm:
                                cx.op('dve', lambda e: e.scalar_tensor_tensor(out=t1[:, 0:2, :], in0=pst[:, 0:2, :], scalar=1.0, in1=t0[:, 0:2, :],
                                                                              op0=ALU.mult, op1=ALU.add),
                                      reads=[bpst, bt0], writes=[bt1])
                            else:
                                cx.op('dve', lambda e: e.scalar_tensor_tensor(out=r_U[:, ti % RRES, :], in0=pst[:, 1, :], scalar=1.0, in1=t0[:, 1, :],
                                                                              op0=ALU.mult, op1=ALU.add),
                                      reads=[bpst, bt0], writes=[B_rU[ti % RRES]])
                            cur = 1 - cur
                        tpk, btpk = smR.next()
                        cx.op('pe', lambda e: e.matmul(tpk, lhsT=kT_t, rhs=idb, start=True, stop=True), reads=[B_act[1], B_cst], writes=[btpk])
                        cx.op('act', lambda e: e.activation(out=r_kd[:, ti % RRES, :], in_=tpk, func=AF.Copy, scale=col(5)),
                              reads=[btpk, B_hs], writes=[B_rkd[ti % RRES]])
                        tpv, btpv = smR.next()
                        cx.op('pe', lambda e: e.matmul(tpv, lhsT=vT_t, rhs=idb, start=True, stop=True), reads=[B_act[2], B_cst], writes=[btpv])
                        cx.op('act', lambda e: e.activation(out=r_vb[:, ti % RRES, :], in_=tpv, func=AF.Copy, scale=col(3)),
                              reads=[btpv, B_hs], writes=[B_rvb[ti % RRES]])

                    def state_gen(ti, h=h):
                        pp = ti % 2
                        sl = slice(ti * 128, (ti + 1) * 128)
                        qT_t, kT_t = act_T[0][:, sl], act_T[1][:, sl]
                        col = lambda i: hs[:, i, ti:ti + 1]
                        Sb_h = S_b[:, h, :]
                        Sf_h = S_f[:, h, :]
                        psA, bA = mm(kT_t, B_act[1], Sb_h, B_S[h])
                        cx.op('dve', lambda e: e.scalar_tensor_tensor(out=t_r[pp][:], in0=psA, scalar=hn[:, 1, ti:ti + 1], in1=r_vb[:, ti % RRES, :],
                                                                      op0=ALU.mult, op1=ALU.add),
                              reads=[bA, B_hn, B_rvb[ti % RRES]], writes=[B_r[pp]])
                        if main:
                            psO1, bO1 = mm(qT_t, B_act[0], Sb_h, B_S[h])
                            cx.op('act', lambda e: e.activation(out=t_so1[pp][:], in_=psO1, func=AF.Copy,
                                                                scale=egc_all[:, ti, h:h + 1]),
                                  reads=[bO1, B_tok], writes=[B_so1[pp]])
                        yield
                        psB, bB = mm(r_U[:, ti % RRES, :], B_rU[ti % RRES], t_r[pp][:], B_r[pp])
                        cx.op('act', lambda e: e.activation(out=t_vn[pp][:], in_=psB, func=AF.Copy, scale=col(4)),
                              reads=[bB, B_hs], writes=[B_vn[pp]])
                        yield
                        psS, bS = mm(r_kd[:, ti % RRES, :], B_rkd[ti % RRES], t_vn[pp][:], B_vn[pp])
                        cx.op('dve', lambda e: e.scalar_tensor_tensor(out=Sf_h, in0=Sf_h, scalar=edec_all[:, ti, h:h + 1], in1=psS,
                                                                      op0=ALU.mult, op1=ALU.add),
                              reads=[bS, B_tok, B_S[h]], writes=[B_S[h]])
                        cx.op('act', lambda e: e.activation(out=Sb_h, in_=Sf_h, func=AF.Copy), reads=[B_S[h]], writes=[B_S[h]])
                        if main:
                            psO2, bO2 = mm(r_at[:, ti % RRES, :], B_rat[ti % RRES], t_vn[pp][:], B_vn[pp])
                            cx.op('dve', lambda e: e.scalar_tensor_tensor(out=r_op[:, ti % RRES, :], in0=psO2, scalar=1.0, in1=t_so1[pp][:],
                                                                          op0=ALU.mult, op1=ALU.add),
                                  reads=[bO2, B_so1[pp]], writes=[B_rop[ti % RRES]])

                    def post_gen(ti, h=h):
                        pp = ti % 2
                        sl = slice(ti * 128, (ti + 1) * 128)
                        opre, bopre = r_op[:, ti % RRES, :], B_rop[ti % RRES]
                        ms = t_ms[:, pp:pp + 1]
                        cx.op('act', lambda e: e.activation(out=t_junk[pp][:], in_=opre, func=AF.Square, accum_out=ms),
                              reads=[bopre], writes=[B_junk[pp], B_ms[pp]])
                        cx.op('act', lambda e: e.activation(out=ms, in_=ms, func=AF.Ln, scale=hn[:, 2, ti:ti + 1], bias=eps_c),
                              reads=[B_ms[pp], B_hn, B_small], writes=[B_ms[pp]])
                        cx.op('act', lambda e: e.activation(out=ms, in_=ms, func=AF.Exp, scale=-0.5, bias=hn[:, 3, ti:ti + 1]),
                              reads=[B_ms[pp], B_hn], writes=[B_ms[pp]])
                        yield
                        cx.op('pool', lambda e: e.tensor_scalar(out=t_o1[pp][:], in0=opre, scalar1=ms, scalar2=None, op0=ALU.mult),
                              reads=[bopre, B_ms[pp]], writes=[B_o1[pp]])
                        cx.op('pool', lambda e: e.tensor_tensor(out=t_og[pp][:], in0=t_o1[pp][:], in1=zs[:, ti, :], op=ALU.mult),
                              reads=[B_o1[pp], B_zs], writes=[B_og[pp]])
                        yield
                        tpo, btpo = smR.next()
                        cx.op('pe', lambda e: e.matmul(tpo, lhsT=t_og[pp][:], rhs=idb, start=True, stop=True),
                              reads=[B_og[pp], B_cst], writes=[btpo])
                        cx.op('act', lambda e: e.activation(out=ot[:, sl], in_=tpo, func=AF.Copy), reads=[btpo], writes=[bot])

                    active = []
                    inproj_chain = [['inproj', -1, -1, inproj_gen(h + 1)]] if h + 1 < NH else []
                    free_slots = list(range(GCH))
                    next_prep, next_state, state_on = 0, 0, False
                    next_post, post_on = (0 if main else NT), False
                    prep_done = set()
                    while next_state < NT or next_post < NT or active or inproj_chain:
                        if (not post_on) and next_post < NT and next_post < next_state:
                            active.append(['post', next_post, -1, post_gen(next_post)])
                            post_on = True
                        while free_slots and next_prep < NT and next_prep < min(next_state, next_post if main else next_state) + RRES:
                            slot = free_slots.pop(0)
                            active.append(['prep', next_prep, slot, prep_gen(next_prep, slot)])
                            next_prep += 1
                        if (not state_on) and next_state < NT and next_state in prep_done:
                            active.append(['state', next_state, -1, state_gen(next_state)])
                            state_on = True
                        for a in list(active) + list(inproj_chain):
                            try:
                                next(a[3])
                            except StopIteration:
                                if a[0] == 'inproj':
                                    inproj_chain.remove(a)
                                    continue
                                active.remove(a)
                                if a[0] == 'prep':
                                    prep_done.add(a[1])
                                    free_slots.append(a[2])
                                elif a[0] == 'state':
                                    state_on = False
                                    next_state += 1
                                elif a[0] == 'post':
                                    post_on = False
                                    next_post += 1
                    if main:
                        cx.dma(oT_d[h], ot[:], reads=[bot], writes=[B_oTd[h]])
                if not main:
                    cx.op('pool', lambda e: e.tensor_copy(out=xh[:, :, 0:3], in_=xnT[:, :, TM - 3:TM]),
                          reads=[B_xnT[NT - 1]], writes=[B_xh])
                if dbg and main:
                    cx.dma(dbg_out["d_S"], S_f[:], reads=B_S)
                cx.barrier()
            cx.cur = es

        phase('P')
        chk(6)
        phase('M')
        chk(7)

        with ExitStack() as sB:
            cx.cur = sB
            vn_all = sb("vn_all", [128, NT, 1024], BF16)
            B_vnall = [Buf(f"vnall{t}") for t in range(NT)]
            with ExitStack() as sB1:
                cx.cur = sB1
                wvb_b = sb("wvb_b", [128, NCH, 1024], BF16)
                B_wvb = Buf("wvb")
                stg = [sb(f"stgB{i}", [128, NCH, 128], F32) for i in range(2)]
                B_stg = [Buf("stgB0"), Buf("stgB1")]
                for c4 in range(8):
                    f, bfb = stg[c4 % 2], B_stg[c4 % 2]
                    cx.dma(f[:], w_vb[:, :, c4 * 128:(c4 + 1) * 128], writes=[bfb])
                    cx.op('pool' if c4 % 2 == 0 else 'dve',
                          lambda e: e.tensor_tensor(out=wvb_b[:, :, c4 * 128:(c4 + 1) * 128], in0=f[:],
                                                    in1=normw_t[:].unsqueeze(2).to_broadcast([128, NCH, 128]), op=ALU.mult),
                          reads=[bfb, B_nw], writes=[B_wvb])
                lnw_t = sb("lnw_t", [128, 1024], F32)
                lnb_t = sb("lnb_t", [128, 1024], F32)
                B_ln = Buf("ln")
                cx.dma(lnw_t[:], lnw.partition_broadcast(128), writes=[B_ln])
                cx.dma(lnb_t[:], lnb.partition_broadcast(128), writes=[B_ln])
                vtmp = [sb(f"vtmp{i}", [128, 1024], F32) for i in range(2)]
                B_vtmp = [Buf("vtmp0"), Buf("vtmp1")]
                for ti in range(NT):
                    pss = []
                    for hf in range(2):
                        ps, bps = bigR.next()
                        for c in range(NCH):
                            cx.op('pe', lambda e: e.matmul(ps[:, :], lhsT=xnT[:, c, ti * 128:(ti + 1) * 128],
                                                           rhs=wvb_b[:, c, hf * 512:(hf + 1) * 512],
                                                           start=(c == 0), stop=(c == NCH - 1)),
                                  reads=[B_xnT[ti], B_wvb], writes=[bps], signal=(c == NCH - 1))
                        pss.append((ps, bps))
                    vt, bvt = vtmp[ti % 2], B_vtmp[ti % 2]
                    bs = B_st[2 + ti % 2]
                    s0 = 8 + (ti % 2) * 8
                    for hf in range(2):
                        ps, bps = pss[hf]
                        cx.op('act', lambda e: e.activation(out=vt[:, hf * 512:(hf + 1) * 512], in_=ps[:, :], func=AF.Copy,
                                                            accum_out=st16[:, s0 + hf:s0 + hf + 1]),
                              reads=[bps], writes=[bvt, bs])
                        cx.op('act', lambda e: e.activation(out=vn_all[:, ti, hf * 512:(hf + 1) * 512], in_=ps[:, :], func=AF.Square,
                                                            accum_out=st16[:, s0 + 2 + hf:s0 + 3 + hf]),
                              reads=[bps], writes=[B_vnall[ti], bs])
                    mean = st16[:, s0 + 4:s0 + 5]
                    var = st16[:, s0 + 5:s0 + 6]
                    m2 = st16[:, s0 + 6:s0 + 7]
                    cx.op('dve', lambda e: e.tensor_scalar(out=mean, in0=st16[:, s0:s0 + 1], scalar1=st16[:, s0 + 1:s0 + 2],
                                                           scalar2=1.0 / 1024, op0=ALU.add, op1=ALU.mult), reads=[bs], writes=[bs])
                    cx.op('dve', lambda e: e.tensor_scalar(out=var, in0=st16[:, s0 + 2:s0 + 3], scalar1=st16[:, s0 + 3:s0 + 4],
                                                           scalar2=1.0 / 1024, op0=ALU.add, op1=ALU.mult), reads=[bs], writes=[bs])
                    cx.op('dve', lambda e: e.tensor_tensor(out=m2, in0=mean, in1=mean, op=ALU.mult), reads=[bs], writes=[bs])
                    cx.op('dve', lambda e: e.tensor_tensor(out=var, in0=var, in1=m2, op=ALU.subtract), reads=[bs], writes=[bs])
                    rsqrt(var, var, 1.0, [bs], [bs])
                    cx.op('dve', lambda e: e.tensor_scalar(out=vt[:], in0=vt[:], scalar1=mean, scalar2=var,
                                                           op0=ALU.subtract, op1=ALU.mult), reads=[bvt, bs], writes=[bvt])
                    cx.op('pool', lambda e: e.tensor_tensor(out=vt[:], in0=vt[:], in1=lnw_t[:], op=ALU.mult),
                          reads=[bvt, B_ln], writes=[bvt])
                    cx.op('pool', lambda e: e.tensor_tensor(out=vn_all[:, ti, :], in0=vt[:], in1=lnb_t[:], op=ALU.add),
                          reads=[bvt, B_ln], writes=[B_vnall[ti]])
                cx.barrier()
            cx.cur = sB
            wr = WRing(cx, 2, 4)
            bsp_t = sb("bsp_t", [128, 8, 128], F32)
            B_bsp = Buf("bsp")
            cx.dma(bsp_t[:].rearrange("p a b -> p (a b)"), bsp.partition_broadcast(128), writes=[B_bsp])
            wsp_f = sb("wsp_f", [128, 8, 128], F32)
            wsp_m = sb("wsp_m", [128, 8, 128], BF16)
            Rsp = sb("Rsp", [128, 8, 128], BF16)
            B_wsp = Buf("wsp")
            B_R = Buf("Rsp")
            cx.dma(wsp_f[:], wsp, writes=[B_wsp])
            cx.op('dve', lambda e: e.tensor_tensor(out=wsp_m[:], in0=wsp_f[:],
                                                   in1=cst_f[:, C_TRIL:C_TRIL + 1, :].to_broadcast([128, 8, 128]), op=ALU.mult),
                  reads=[B_wsp, B_cst], writes=[B_wsp])
            for g in range(8):
                tp, btp = smR.next()
                cx.op('pe', lambda e: e.matmul(tp, lhsT=wsp_m[:, g, :], rhs=idb, start=True, stop=True),
                      reads=[B_wsp, B_cst], writes=[btp])
                cx.op('act', lambda e: e.activation(out=Rsp[:, g, :], in_=tp, func=AF.Copy), reads=[btp], writes=[B_R])
            uT = sb("uT", [128, TM], BF16)
            szT = sb("szT", [128, TM], BF16)
            buT, bszT = Buf("uT"), Buf("szT")
            oTs = [sb(f"oTsB{i}", [128, TM], BF16) for i in range(2)]
            B_oTs = [Buf("oTsB0"), Buf("oTsB1")]
            tA = [sb(f"tA{i}", [128, 512], F32) for i in range(2)]
            tBb = [sb(f"tB{i}", [128, 512], F32) for i in range(2)]
            B_tA = [Buf("tA0"), Buf("tA1")]
            B_tB = [Buf("tB0"), Buf("tB1")]
            nxt = (load_w(wr, w_fm[32]), load_w(wr, w_fm[40]))
            for g in range(8):
                (wu, bwu), (wz, bwz) = nxt
                if g + 1 < 8:
                    nxt = (load_w(wr, w_fm[32 + g + 1]), load_w(wr, w_fm[40 + g + 1]))
                inproj_fm(wu, bwu, lambda tb: uT[:, tb * TB:(tb + 1) * TB], buT, AF.Copy)
                inproj_fm(wz, bwz, lambda tb: szT[:, tb * TB:(tb + 1) * TB], bszT, AF.Silu)
                ot, bot = oTs[g % 2], B_oTs[g % 2]
                NB4 = min(4, NT)
                for t4 in range(NT // NB4):
                    pst, bpst = sm_full()
                    for j in range(NB4):
                        ti = t4 * NB4 + j
                        cx.op('pe', lambda e: e.matmul(pst[:, j, :], lhsT=vn_all[:, ti, g * 128:(g + 1) * 128], rhs=Rsp[:, g, :],
                                                       start=True, stop=True), reads=[B_vnall[ti], B_R], writes=[bpst],
                              signal=(j == NB4 - 1))
                    pp = t4 % 2
                    sl4 = slice(t4 * NB4 * 128, (t4 + 1) * NB4 * 128)
                    v3 = lambda t: t[:, 0:NB4 * 128].rearrange("p (a b) -> p a b", a=NB4)
                    cx.op('dve', lambda e: e.tensor_tensor(out=v3(tA[pp]), in0=pst[:, 0:NB4, :],
                                                           in1=bsp_t[:, g:g + 1, :].to_broadcast([128, NB4, 128]), op=ALU.add),
                          reads=[bpst, B_bsp], writes=[B_tA[pp]])
                    cx.op('pool', lambda e: e.tensor_tensor(out=tBb[pp][:, 0:NB4 * 128], in0=tA[pp][:, 0:NB4 * 128], in1=uT[:, sl4], op=ALU.mult),
                          reads=[B_tA[pp], buT], writes=[B_tB[pp]])
                    cx.op('dve', lambda e: e.tensor_tensor(out=ot[:, sl4], in0=tBb[pp][:, 0:NB4 * 128], in1=szT[:, sl4], op=ALU.mult),
                          reads=[B_tB[pp], bszT], writes=[bot])
                cx.dma(oT_d[8 + g], ot[:], reads=[bot], writes=[B_oTd[8 + g]])
            cx.barrier()
        cx.cur = es

        chk(8)
        with ExitStack() as sO:
            cx.cur = sO
            wo_b = sb("wo_b", [128, NCH, D], BF16)
            B_wo = Buf("wo")
            stg = [sb(f"stgO{i}", [128, NCH, 128], F32) for i in range(2)]
            B_stg = [Buf("stgO0"), Buf("stgO1")]
            for c4 in range(16):
                f, bfb = stg[c4 % 2], B_stg[c4 % 2]
                cx.dma(f[:], w_o[:, :, c4 * 128:(c4 + 1) * 128], writes=[bfb], q=('sp' if c4 % 2 == 0 else 'act'))
                cx.op('pool' if c4 % 2 == 0 else 'dve',
                      lambda e: e.tensor_copy(out=wo_b[:, :, c4 * 128:(c4 + 1) * 128], in_=f[:]),
                      reads=[bfb], writes=[B_wo])
            fnw_t = sb("fnw_t", [128, D], F32)
            B_fnw = Buf("fnw")
            cx.dma(fnw_t[:], fnw.partition_broadcast(128), writes=[B_fnw])
            for c in range(NCH):
                cx.dma(xnT[:, c, :], oT_d[c], reads=[B_oTd[c]], writes=B_xnT, q=('sp' if c % 2 == 0 else 'act'))
            if dbg:
                with ExitStack() as sd:
                    cx.cur = sd
                    dtmp = sb("dbg_oT", [128, NCH, TM], F32)
                    bt = Buf("dbgoT")
                    cx.op('dve', lambda e: e.tensor_copy(out=dtmp[:], in_=xnT[:]), reads=B_xnT, writes=[bt])
                    cx.dma(dbg_out["d_oT"], dtmp[:], reads=[bt])
                    cx.barrier()
                cx.cur = sO
            xin = [sb(f"xinO{i}", [128, D], F32) for i in range(2)]
            B_xin = [Buf("xinO0"), Buf("xinO1")]
            junk = sb("junkO", [128, D], BF16)
            B_junk = Buf("junkO")
            for ti in range(NT):
                xi, bxi = xin[ti % 2], B_xin[ti % 2]
                cx.dma(xi[:], xcat[TM + ti * 128: TM + (ti + 1) * 128, :], writes=[bxi])
                bs = B_st[ti % 2]
                s0 = 32 + (ti % 2) * 8
                for n4 in range(4):
                    ps, bps = bigR.next()
                    for c in range(NCH):
                        cx.op('pe', lambda e: e.matmul(ps[:, :], lhsT=xnT[:, c, ti * 128:(ti + 1) * 128],
                                                       rhs=wo_b[:, c, n4 * 512:(n4 + 1) * 512],
                                                       start=(c == 0), stop=(c == NCH - 1)),
                              reads=[B_xnT[ti], B_wo], writes=[bps], signal=(c == NCH - 1))
                    cx.op('dve', lambda e: e.tensor_tensor(out=xi[:, n4 * 512:(n4 + 1) * 512], in0=ps[:, :],
                                                           in1=xi[:, n4 * 512:(n4 + 1) * 512], op=ALU.add),
                          reads=[bps, bxi], writes=[bxi])
                ssq = st16[:, s0:s0 + 1]
                cx.op('act', lambda e: e.activation(out=junk[:], in_=xi[:], func=AF.Square, accum_out=ssq),
                      reads=[bxi], writes=[B_junk, bs])
                rsqrt(ssq, ssq, 1.0 / D, [bs], [bs])
                cx.op('dve', lambda e: e.scalar_tensor_tensor(out=xi[:], in0=xi[:], scalar=ssq, in1=fnw_t[:],
                                                               op0=ALU.mult, op1=ALU.mult),
                      reads=[bxi, bs, B_fnw], writes=[bxi])
                cx.dma(out[ti * 128:(ti + 1) * 128, :], xi[:], reads=[bxi])
            cx.finish()
        print("instructions:", cx.nins, {k: v for k, v in cx.cnt.items()})
    except StopBuild:
        print("stopped early at", STOP[0])
    return nc


def prep_inputs(inp, NT, seq, nb):
    TM = NT * 128
    x = np.asarray(inp["x"], np.float32)
    w_in = np.asarray(inp["w_in"], np.float32)[0]
    w_out = np.asarray(inp["w_out"], np.float32)[0]

    def fm(cols):
        return np.ascontiguousarray(w_in[:, cols].reshape(NCH, 128, 128).transpose(1, 0, 2))
    groups = []
    for sec in (0, 1024, 2048, 3072, 4112, 6160):
        for g in range(8):
            groups.append(fm(slice(sec + g * 128, sec + (g + 1) * 128)))
    w_fm = np.stack(groups, 0)
    w_vb = np.ascontiguousarray(w_in[:, 5136:6160].reshape(NCH, 128, 1024).transpose(1, 0, 2))
    w_ba = np.ascontiguousarray(w_in[:, 4096:4112].reshape(NCH, 128, 16).transpose(1, 0, 2))
    w_o = np.ascontiguousarray(w_out.reshape(NCH, 128, D).transpose(1, 0, 2))
    convw = np.ascontiguousarray(np.asarray(inp["conv_w"], np.float32)[0].reshape(4, 24, 128).transpose(2, 1, 0))
    normw = np.ascontiguousarray(np.asarray(inp["norm_w"], np.float32)[0].reshape(NCH, 128).T)
    common = dict(
        w_fm=w_fm, w_vb=w_vb, w_ba=w_ba, w_o=w_o, convw=convw, normw=normw,
        alog=np.asarray(inp["a_log"], np.float32).reshape(1, 8),
        dtb=np.asarray(inp["dt_bias"], np.float32).reshape(1, 8),
        hnw=np.asarray(inp["head_norm_w"], np.float32).reshape(1, 128),
        lnw=np.asarray(inp["sgu_ln_w"], np.float32).reshape(1, 1024),
        lnb=np.asarray(inp["sgu_ln_b"], np.float32).reshape(1, 1024),
        wsp=np.ascontiguousarray(np.asarray(inp["w_spatial"], np.float32)[0].transpose(1, 0, 2)),
        bsp=np.asarray(inp["b_spatial"], np.float32).reshape(1, 1024),
        fnw=np.asarray(inp["final_norm_w"], np.float32).reshape(1, D),
        cst=make_consts(),
    )
    maps, ids = [], []
    nhalf = seq // TM
    for b in range(nb):
        for hf in range(nhalf):
            xc = np.zeros((2 * TM, D), np.float32)
            if hf > 0:
                xc[:TM] = x[b, (hf - 1) * TM: hf * TM]
            xc[TM:] = x[b, hf * TM:(hf + 1) * TM]
            m = dict(common)
            m["xcat"] = xc
            maps.append(m)
            ids.append((b, hf))
    return maps, ids


_CACHE = {}


def kernel(**inputs):
    NT = 16
    if NT not in _CACHE:
        _CACHE[NT] = build_program(NT)
    nc = _CACHE[NT]
    maps, ids = prep_inputs(inputs, NT, 4096, 4)
    res = run_bass_kernel_spmd(nc, maps, core_ids=list(range(8)))
    outp = np.zeros((4, 4096, D), np.float32)
    TM = NT * 128
    for (b, hf), r in zip(ids, res.results):
        outp[b, hf * TM:(hf + 1) * TM] = r["out"]
    return outp
```

```python
import numpy as np
from contextlib import ExitStack
import concourse.bass as bass
import concourse.mybir as mybir
from concourse.bass_utils import run_bass_kernel_spmd

F32 = mybir.dt.float32
BF16 = mybir.dt.bfloat16
ALU = mybir.AluOpType
AF = mybir.ActivationFunctionType

D = 2048
NCH = 16
NH = 8
EPS = 1e-6
SAME_SYNC = True
NDS = 8
GCH = 6
RRES = 8

C_ID, C_LOW, C_UP, C_BD16, C_C32, C_C64, C_C128, C_ONES, C_TRIL, C_PLOW, C_PUP = range(11)
NCST = 11


def make_consts():
    p = np.arange(128)[:, None]
    f = np.arange(128)[None, :]
    c = np.zeros((128, NCST, 128), np.float32)
    c[:, C_ID] = (p == f)
    c[:, C_LOW] = (f < p)
    c[:, C_UP] = (f >= p)
    bd = lambda b: ((p // b) == (f // b)).astype(np.float32)
    c[:, C_BD16] = bd(16)
    c[:, C_C32] = bd(32) - bd(16)
    c[:, C_C64] = bd(64) - bd(32)
    c[:, C_C128] = 1.0 - bd(64)
    c[:, C_ONES] = 1.0
    c[:, C_TRIL] = (f <= p)
    c[:, C_PLOW] = 30000.0 * (f >= p)
    c[:, C_PUP] = 30000.0 * (f < p)
    return c


class StopBuild(Exception):
    pass


STOP = [0]


class Buf:
    __slots__ = ("name", "lw", "rd", "excl")

    def __init__(self, name, excl=False):
        self.name = name
        self.lw = None
        self.rd = {}
        self.excl = excl


class Ctx:
    def __init__(self, nc, es):
        self.nc = nc
        self.es = es
        self.eng = {'pe': nc.tensor, 'act': nc.scalar, 'dve': nc.vector, 'pool': nc.gpsimd, 'sp': nc.sync}
        self.sem = {k: es.enter_context(nc.semaphore("s_" + k)) for k in ['pe', 'act', 'dve', 'pool']}
        self.cnt = {k: 0 for k in self.sem}
        self.dsem = [es.enter_context(nc.semaphore(f"dq{i}")) for i in range(NDS)]
        self.dcnt = [0] * NDS
        self.dn = 0
        self.waited = {}
        self.nins = 0
        self.cur = es
        self.nsb = 0

    def sb(self, name, shape, dt):
        self.nsb += 1
        return self.cur.enter_context(self.nc.sbuf_tensor(f"{name}_{self.nsb}", shape, dt))

    def barrier(self):
        for e in ['pe', 'act', 'dve', 'pool', 'sp']:
            for k in self.sem:
                if k != e and self.cnt[k] > 0:
                    self._wait(e, k, self.cnt[k])
            for i in range(NDS):
                if self.dcnt[i] > 0:
                    self._wait(e, ('d', i), self.dcnt[i])

    def _wait(self, e, key, val):
        if self.waited.get((e, key), 0) >= val:
            return
        sem = self.sem[key] if isinstance(key, str) else self.dsem[key[1]]
        self.eng[e].wait_ge(sem, val)
        self.waited[(e, key)] = val

    def _deps(self, e, reads, writes):
        need = {}
        for b in reads:
            if b.lw is not None and need.get(b.lw[0], 0) < b.lw[1]:
                need[b.lw[0]] = b.lw[1]
        same = need.get(e, 0)
        for b in writes:
            if b.lw is not None and b.lw[0] != e and need.get(b.lw[0], 0) < b.lw[1]:
                need[b.lw[0]] = b.lw[1]
            for k, v in b.rd.items():
                if k != e and need.get(k, 0) < v:
                    need[k] = v
        for k, v in need.items():
            if k == e and (e == 'pe' or not SAME_SYNC):
                continue
            self._wait(e, k, v)

    def op(self, e, fn, reads=(), writes=(), signal=True):
        ex = [b for b in reads if b.excl]
        if ex:
            reads = [b for b in reads if not b.excl]
            writes = list(writes) + ex
        self._deps(e, reads, writes)
        ins = fn(self.eng[e])
        self.nins += 1
        if signal:
            self.cnt[e] += 1
            ins.then_inc(self.sem[e], 1)
            v = self.cnt[e]
        else:
            v = self.cnt[e] + 1
        for b in reads:
            if b.rd.get(e, 0) < v:
                b.rd[e] = v
        for b in writes:
            b.lw = (e, v)
            b.rd = {}

    def dma(self, out_ap, in_ap, reads=(), writes=(), q='sp'):
        i = self.dn % NDS
        self.dn += 1
        key = ('d', i)
        if self.dcnt[i] > 0:
            self._wait(q, key, self.dcnt[i])
        self._deps(q, reads, writes)
        self.dcnt[i] += 16
        self.eng[q].dma_start(out=out_ap, in_=in_ap).then_inc(self.dsem[i], 16)
        self.nins += 1
        v = self.dcnt[i]
        for b in reads:
            if b.rd.get(key, 0) < v:
                b.rd[key] = v
        for b in writes:
            b.lw = (key, v)
            b.rd = {}

    def finish(self, q='sp'):
        for i in range(NDS):
            if self.dcnt[i] > 0:
                self.eng[q].wait_ge(self.dsem[i], self.dcnt[i])


class Ring:
    def __init__(self, items):
        self.items = items
        self.i = 0

    def next(self):
        it = self.items[self.i % len(self.items)]
        self.i += 1
        return it


class WRing:
    def __init__(self, cx, nws, nwb):
        self.cx = cx
        self.f = [cx.sb(f"wst_f{i}", [128, NCH, 128], F32) for i in range(nws)]
        self.bf = [Buf(f"wstf{i}") for i in range(nws)]
        self.w = [cx.sb(f"wst_b{i}", [128, NCH, 128], BF16) for i in range(nwb)]
        self.bw = [Buf(f"wstb{i}") for i in range(nwb)]
        self.i = 0


def build_program(NT, dbg=False):
    TM = NT * 128
    TB = min(512, TM)
    NTB = TM // TB
    nc = bass.Bass("TRN2", target_bir_lowering=False)
    dt_in = lambda n, s: nc.dram_tensor(n, s, F32, kind="ExternalInput").ap()
    xcat = dt_in("xcat", [2 * TM, D])
    w_fm = dt_in("w_fm", [48, 128, NCH, 128])
    w_vb = dt_in("w_vb", [128, NCH, 1024])
    w_ba = dt_in("w_ba", [128, NCH, 16])
    w_o = dt_in("w_o", [128, NCH, D])
    convw = dt_in("convw", [128, 24, 4])
    normw = dt_in("normw", [128, NCH])
    alog = dt_in("alog", [1, 8])
    dtb = dt_in("dtb", [1, 8])
    hnw = dt_in("hnw", [1, 128])
    lnw = dt_in("lnw", [1, 1024])
    lnb = dt_in("lnb", [1, 1024])
    wsp = dt_in("wsp", [128, 8, 128])
    bsp = dt_in("bsp", [1, 1024])
    fnw = dt_in("fnw", [1, D])
    cst = dt_in("cst", [128, NCST, 128])
    out = nc.dram_tensor("out", [TM, D], F32, kind="ExternalOutput").ap()
    oT_d = nc.dram_tensor("oT_d", [NCH, 128, TM], BF16, kind="Internal").ap()
    B_oTd = [Buf(f"oTd{c}") for c in range(NCH)]
    dbg_out = {}
    if dbg:
        for nm, shp in [("d_xnT", [128, NCH, TM]), ("d_q", [128, TM]), ("d_k", [128, TM]), ("d_v", [128, TM]),
                        ("d_beta", [128, NT, 8]), ("d_g", [128, NT, 8]), ("d_U", [128, 128]), ("d_N", [128, 128]),
                        ("d_S", [128, 8, 128]), ("d_oT", [128, NCH, TM]), ("d_E", [128, 128]),
                        ("d_opre", [128, 128]), ("d_vnew", [128, 128])]:
            dbg_out[nm] = nc.dram_tensor(nm, shp, F32, kind="ExternalOutput").ap()

    try:
      with ExitStack() as es:
        cx = Ctx(nc, es)
        sb = cx.sb

        def chk(n):
            if STOP[0] == n:
                cx.barrier()
                cx.finish()
                raise StopBuild()
        big = [es.enter_context(nc.psum_tensor(f"pbig{i}", [128, 512], F32)) for i in range(2)]
        bigR = Ring([(t, Buf(f"pbig{i}", True)) for i, t in enumerate(big)])
        NSM = 6
        smt = [es.enter_context(nc.psum_tensor(f"psm{i}", [128, 4, 128], F32)) for i in range(NSM)]
        smB = [Buf(f"psm{i}", True) for i in range(NSM)]
        smR = Ring([(smt[i][:, 0, :], smB[i]) for i in range(NSM)])

        def sm_full():
            i = smR.i % NSM
            smR.i += 1
            return smt[i], smB[i]

        cst_f = sb("cst_f", [128, NCST, 128], F32)
        cst_b = sb("cst_b", [128, 2, 128], BF16)
        B_cst = Buf("cst")
        cx.dma(cst_f[:], cst, writes=[B_cst])
        cx.op('dve', lambda e: e.tensor_copy(out=cst_b[:, 0, :], in_=cst_f[:, C_ID, :]), reads=[B_cst], writes=[B_cst])
        cx.op('dve', lambda e: e.tensor_copy(out=cst_b[:, 1, :], in_=cst_f[:, C_ONES, :]), reads=[B_cst], writes=[B_cst])
        idb = cst_b[:, 0, :]
        idf = cst_f[:, C_ID, :]
        onesb = cst_b[:, 1, :]
        onesf = cst_f[:, C_ONES, :]

        def mk(i):
            return cst_f[:, i, :]

        normw_t = sb("normw_t", [128, NCH], F32)
        B_nw = Buf("nw")
        cx.dma(normw_t[:], normw, writes=[B_nw])
        convw_t = sb("convw_t", [128, 24, 4], F32)
        B_cw = Buf("cw")
        cx.dma(convw_t[:], convw, writes=[B_cw])
        small = sb("small", [128, 64], F32)
        B_small = Buf("small")
        cx.dma(small[:, 0:8], alog.partition_broadcast(128), writes=[B_small])
        cx.dma(small[:, 8:16], dtb.partition_broadcast(128), writes=[B_small])
        negA = small[:, 16:24]
        cx.op('act', lambda e: e.activation(out=negA, in_=small[:, 0:8], func=AF.Exp), reads=[B_small], writes=[B_small])
        cx.op('dve', lambda e: e.tensor_scalar(out=negA, in0=negA, scalar1=-1.0, scalar2=None, op0=ALU.mult),
              reads=[B_small], writes=[B_small])
        dtb_t = small[:, 8:16]
        eps_c = small[:, 24:25]
        one_c = small[:, 25:26]
        cx.op('pool', lambda e: e.memset(small[:, 24:25], EPS), writes=[B_small])
        cx.op('pool', lambda e: e.memset(small[:, 25:26], 1.0), writes=[B_small])

        def rsqrt(out_ap, in_ap, scale, rd, wr):
            cx.op('act', lambda e: e.activation(out=out_ap, in_=in_ap, func=AF.Ln, scale=scale, bias=eps_c),
                  reads=list(rd) + [B_small], writes=wr)
            cx.op('act', lambda e: e.activation(out=out_ap, in_=out_ap, func=AF.Exp, scale=-0.5), reads=wr, writes=wr)
        hnw_t = sb("hnw_t", [128, 128], F32)
        B_hnw = Buf("hnw")
        cx.dma(hnw_t[:], hnw.partition_broadcast(128), writes=[B_hnw])
        wba_f = sb("wba_f", [128, NCH, 16], F32)
        wba_b = sb("wba_b", [128, NCH, 16], BF16)
        B_wba = Buf("wba")
        cx.dma(wba_f[:], w_ba, writes=[B_wba])
        cx.op('dve', lambda e: e.tensor_tensor(out=wba_b[:], in0=wba_f[:],
                                               in1=normw_t[:].unsqueeze(2).to_broadcast([128, NCH, 16]), op=ALU.mult),
              reads=[B_wba, B_nw], writes=[B_wba])
        xnT = sb("xnT", [128, NCH, TM], BF16)
        B_xnT = [Buf(f"xnT{t}") for t in range(NT)]
        xh = sb("xh", [128, NCH, 4], BF16)
        B_xh = Buf("xh")
        st16 = sb("st16", [128, 64], F32)
        B_st = [Buf(f"st{i}") for i in range(4)]
        beta_all = sb("beta_all", [128, NT, 8], F32)
        g_all = sb("g_all", [128, NT, 8], F32)
        gc_all = sb("gc_all", [128, NT, 8], F32)
        egc_all = sb("egc_all", [128, NT, 8], F32)
        kds_all = sb("kds_all", [128, NT, 8], F32)
        edec_all = sb("edec_all", [128, NT, 8], F32)
        ngc_all = sb("ngc_all", [128, NT, 8], F32)
        B_tok = Buf("tokscal")
        S_f = sb("S_f", [128, NH, 128], F32)
        S_b = sb("S_b", [128, NH, 128], BF16)
        B_S = [Buf(f"S{h}") for h in range(NH)]
        cx.op('pool', lambda e: e.memset(S_f[:], 0.0), writes=B_S)
        cx.op('pool', lambda e: e.memset(S_b[:], 0.0), writes=B_S)
        cx.op('pool', lambda e: e.memset(xh[:], 0.0), writes=[B_xh])
        fl = lambda t: t[:].rearrange("p a b -> p (a b)")
        chk(1)

        def load_w(wr, src_ap, fold=True):
            i = wr.i
            wr.i += 1
            f, bfb = wr.f[i % len(wr.f)], wr.bf[i % len(wr.f)]
            w, bwb = wr.w[i % len(wr.w)], wr.bw[i % len(wr.w)]
            cx.dma(f[:], src_ap, writes=[bfb])
            eng = 'pool' if (i % 2 == 0) else 'dve'
            if fold:
                cx.op(eng, lambda e: e.tensor_tensor(out=w[:], in0=f[:],
                                                     in1=normw_t[:].unsqueeze(2).to_broadcast([128, NCH, 128]), op=ALU.mult),
                      reads=[bfb, B_nw], writes=[bwb])
            else:
                cx.op(eng, lambda e: e.tensor_copy(out=w[:], in_=f[:]), reads=[bfb], writes=[bwb])
            return w, bwb

        def inproj_fm(w, bw, dst_fn, bdst, func):
            for tb in range(NTB):
                ps, bps = bigR.next()
                for c in range(NCH):
                    cx.op('pe', lambda e: e.matmul(ps[:, 0:TB], lhsT=w[:, c, :], rhs=xnT[:, c, tb * TB:(tb + 1) * TB],
                                                   start=(c == 0), stop=(c == NCH - 1)),
                          reads=[bw] + B_xnT[tb * TB // 128:(tb + 1) * TB // 128], writes=[bps],
                          signal=(c == NCH - 1))
                cx.op('act', lambda e: e.activation(out=dst_fn(tb), in_=ps[:, 0:TB], func=func), reads=[bps], writes=[bdst])

        def phase(ph):
            main = (ph == 'M')
            tok0 = TM if main else 0
            with ExitStack() as sx:
                cx.cur = sx
                xin = [sb(f"xin{i}", [128, D], F32) for i in range(2)]
                B_xin = [Buf("xin0"), Buf("xin1")]
                xs = [sb(f"xs{i}", [128, D], BF16) for i in range(2)]
                B_xs = [Buf("xs0"), Buf("xs1")]
                for ti in range(NT):
                    xi, bxi = xin[ti % 2], B_xin[ti % 2]
                    xsi, bxs = xs[ti % 2], B_xs[ti % 2]
                    cx.dma(xi[:], xcat[tok0 + ti * 128: tok0 + (ti + 1) * 128, :], writes=[bxi])
                    ssq = st16[:, ti % 2: ti % 2 + 1]
                    rstd = st16[:, 2 + ti % 2: 3 + ti % 2]
                    bs = B_st[ti % 2]
                    cx.op('act', lambda e: e.activation(out=xsi[:], in_=xi[:], func=AF.Square, accum_out=ssq),
                          reads=[bxi], writes=[bxs, bs])
                    rsqrt(rstd, ssq, 1.0 / D, [bs], [bs])
                    cx.op('act', lambda e: e.activation(out=xsi[:], in_=xi[:], func=AF.Copy, scale=rstd),
                          reads=[bxi, bs], writes=[bxs])
                    for q4 in range(4):
                        pst, bpst = sm_full()
                        for j in range(4):
                            c = q4 * 4 + j
                            cx.op('pe', lambda e: e.matmul(pst[:, j, :], lhsT=xsi[:, c * 128:(c + 1) * 128], rhs=idb,
                                                           start=True, stop=True),
                                  reads=[bxs, B_cst], writes=[bpst], signal=(j == 3))
                        cx.op('dve' if q4 % 2 == 0 else 'act',
                              (lambda e: e.tensor_copy(out=xnT[:, q4 * 4:(q4 + 1) * 4, ti * 128:(ti + 1) * 128], in_=pst[:]))
                              if q4 % 2 == 0 else
                              (lambda e: e.activation(out=xnT[:, q4 * 4:(q4 + 1) * 4, ti * 128:(ti + 1) * 128], in_=pst[:], func=AF.Copy)),
                              reads=[bpst], writes=[B_xnT[ti]])
                cx.barrier()
                if not main:
                    chk(2)
            cx.cur = es
            for ti in range(NT):
                ps, bps = smR.next()
                for c in range(NCH):
                    cx.op('pe', lambda e: e.matmul(ps[:, 0:16], lhsT=xnT[:, c, ti * 128:(ti + 1) * 128],
                                                   rhs=wba_b[:, c, :], start=(c == 0), stop=(c == NCH - 1)),
                          reads=[B_xnT[ti], B_wba], writes=[bps], signal=(c == NCH - 1))
                cx.op('act', lambda e: e.activation(out=beta_all[:, ti, :], in_=ps[:, 0:8], func=AF.Exp, scale=-1.0),
                      reads=[bps], writes=[B_tok])
                cx.op('dve', lambda e: e.tensor_copy(out=g_all[:, ti, :], in_=ps[:, 8:16]), reads=[bps], writes=[B_tok])
            TA = NT * 8
            cx.op('dve', lambda e: e.tensor_scalar(out=fl(beta_all), in0=fl(beta_all), scalar1=1.0, scalar2=None,
                                                   op0=ALU.add), reads=[B_tok], writes=[B_tok])
            cx.op('dve', lambda e: e.reciprocal(out=fl(beta_all), in_=fl(beta_all)), reads=[B_tok], writes=[B_tok])
            cx.op('dve', lambda e: e.tensor_tensor(out=g_all[:], in0=g_all[:],
                                                   in1=dtb_t.unsqueeze(1).to_broadcast([128, NT, 8]), op=ALU.add),
                  reads=[B_tok, B_small], writes=[B_tok])
            cx.op('act', lambda e: e.activation(out=fl(g_all), in_=fl(g_all), func=AF.Exp), reads=[B_tok], writes=[B_tok])
            cx.op('act', lambda e: e.activation(out=fl(g_all), in_=fl(g_all), func=AF.Ln, bias=one_c),
                  reads=[B_tok, B_small], writes=[B_tok])
            cx.op('dve', lambda e: e.tensor_tensor(out=g_all[:], in0=g_all[:],
                                                   in1=negA.unsqueeze(1).to_broadcast([128, NT, 8]), op=ALU.mult),
                  reads=[B_tok, B_small], writes=[B_tok])
            for a0 in range(0, TA, 128):
                a1 = min(TA, a0 + 128)
                ps, bps = smR.next()
                cx.op('pe', lambda e: e.matmul(ps[:, 0:a1 - a0], lhsT=mk(C_UP), rhs=fl(g_all)[:, a0:a1],
                                               start=True, stop=True), reads=[B_tok, B_cst], writes=[bps])
                cx.op('dve', lambda e: e.tensor_copy(out=fl(gc_all)[:, a0:a1], in_=ps[:, 0:a1 - a0]),
                      reads=[bps], writes=[B_tok])
                ps2, bps2 = smR.next()
                cx.op('pe', lambda e: e.matmul(ps2[:, 0:a1 - a0], lhsT=onesf, rhs=fl(g_all)[:, a0:a1],
                                               start=True, stop=True), reads=[B_tok, B_cst], writes=[bps2])
                cx.op('dve', lambda e: e.tensor_tensor(out=fl(kds_all)[:, a0:a1], in0=ps2[:, 0:a1 - a0],
                                                       in1=fl(gc_all)[:, a0:a1], op=ALU.subtract),
                      reads=[bps2, B_tok], writes=[B_tok])
                cx.op('dve', lambda e: e.tensor_copy(out=fl(edec_all)[:, a0:a1], in_=ps2[:, 0:a1 - a0]),
                      reads=[bps2, B_tok], writes=[B_tok])
                cx.op('act', lambda e: e.activation(out=fl(edec_all)[:, a0:a1], in_=fl(edec_all)[:, a0:a1], func=AF.Exp),
                      reads=[B_tok], writes=[B_tok])
            cx.op('act', lambda e: e.activation(out=fl(kds_all), in_=fl(kds_all), func=AF.Exp), reads=[B_tok], writes=[B_tok])
            cx.op('act', lambda e: e.activation(out=fl(egc_all), in_=fl(gc_all), func=AF.Exp), reads=[B_tok], writes=[B_tok])
            cx.op('dve', lambda e: e.tensor_scalar(out=fl(ngc_all), in0=fl(gc_all), scalar1=-1.0, scalar2=None, op0=ALU.mult),
                  reads=[B_tok], writes=[B_tok])
            if not main:
                chk(3)
            if dbg and main:
                cx.dma(dbg_out["d_beta"], beta_all[:], reads=[B_tok])
                cx.dma(dbg_out["d_g"], g_all[:], reads=[B_tok])
                with ExitStack() as sd:
                    cx.cur = sd
                    dtmp = sb("dbg_xnT", [128, NCH, TM], F32)
                    bt = Buf("dbgx")
                    cx.op('dve', lambda e: e.tensor_copy(out=dtmp[:], in_=xnT[:]), reads=B_xnT, writes=[bt])
                    cx.dma(dbg_out["d_xnT"], dtmp[:], reads=[bt])
                    cx.barrier()
                cx.cur = es

            with ExitStack() as shd:
                cx.cur = shd
                wr = WRing(cx, 2, 2)
                pre = [sb(f"pre{j}", [128, TM + 4], BF16) for j in range(3)]
                B_pre = [Buf(f"pre{j}") for j in range(3)]
                act_T = [sb(f"actT{j}", [128, TM], BF16) for j in range(3)]
                B_act = [Buf(f"actT{j}") for j in range(3)]
                zT = sb("zT", [128, TM], BF16)
                B_zT = Buf("zT")
                zs = sb("zs", [128, NT, 128], BF16)
                B_zs = Buf("zs")
                sq = [sb(f"sq{j}", [128, TM], BF16) for j in range(1)]
                B_sq = [Buf("sq0")]
                diag = sb("diag", [128, 12, 128], BF16)
                B_diag = Buf("diag")
                hs = sb("hs", [128, 8, NT], F32)
                B_hs = Buf("hs")
                oTs = [sb(f"oTs{i}", [128, TM], BF16) for i in range(2)]
                B_oTs = [Buf("oTs0"), Buf("oTs1")]
                NTMP = GCH

                def tmpset(nm, dt, n=NTMP):
                    ts = [sb(f"{nm}{i}", [128, 128], dt) for i in range(n)]
                    return ts, [Buf(f"{nm}{i}") for i in range(n)]
                t_dg, B_dg = tmpset("dg", F32)
                t_El, B_El = tmpset("El", F32)
                t_Eu, B_Eu = tmpset("Eu", F32)
                t_N, B_N = tmpset("N", BF16)
                t_NT, B_NT = tmpset("NT", BF16)
                t_N16, B_N16 = tmpset("N16", BF16)
                t_N16T, B_N16T = tmpset("N16T", BF16)
                def pairset(nm, n):
                    ts = [sb(f"{nm}{i}", [128, 2, 128], BF16) for i in range(n)]
                    return ts, [Buf(f"{nm}{i}") for i in range(n)]
                t_PP, B_PP = pairset("PP", 2 * GCH)
                t_TU, B_TU = pairset("TU", 2 * GCH)
                t_XX, B_XX = pairset("XX", GCH)
                t_r, B_r = tmpset("r", BF16, 2)
                t_vn, B_vn = tmpset("vn", BF16, 2)
                t_o1, B_o1 = tmpset("o1", F32, 2)
                t_so1, B_so1 = tmpset("so1", F32, 2)
                r_op = sb("r_op", [128, min(NT, RRES), 128], F32)
                B_rop = [Buf(f"rop{t}") for t in range(NT)]
                t_og, B_og = tmpset("og", BF16, 2)
                t_junk, B_junk = tmpset("junk", BF16, 2)
                t_ms = sb("t_ms", [128, 2], F32)
                B_ms = [Buf("ms0"), Buf("ms1")]
                hn = sb("hn", [128, 4, NT], F32)
                B_hn = Buf("hn")
                r_U = sb("r_U", [128, min(NT, RRES), 128], BF16)
                r_at = sb("r_at", [128, min(NT, RRES), 128], BF16)
                r_kd = sb("r_kd", [128, min(NT, RRES), 128], BF16)
                r_vb = sb("r_vb", [128, min(NT, RRES), 128], BF16)
                B_rU = [Buf(f"rU{t}") for t in range(NT)]
                B_rat = [Buf(f"rat{t}") for t in range(NT)]
                B_rkd = [Buf(f"rkd{t}") for t in range(NT)]
                B_rvb = [Buf(f"rvb{t}") for t in range(NT)]
                HS = lambda i: hs[:, i, :]

                def head_groups(h):
                    return ([(0, h), (1, 8 + h), (2, 16 + h)] if main else [(1, 8 + h), (2, 16 + h)])

                def inproj_gen(h):
                    for j, g in head_groups(h) + ([(3, 24 + h)] if main else []):
                        w, bw = load_w(wr, w_fm[g])
                        dst, bdst = (pre[j], B_pre[j]) if j < 3 else (zT, B_zT)
                        off = 3 if j < 3 else 0
                        yield
                        for tb in range(NTB):
                            ps, bps = bigR.next()
                            for c in range(NCH):
                                cx.op('pe', lambda e: e.matmul(ps[:, 0:TB], lhsT=w[:, c, :], rhs=xnT[:, c, tb * TB:(tb + 1) * TB],
                                                               start=(c == 0), stop=(c == NCH - 1)),
                                      reads=[bw] + B_xnT[tb * TB // 128:(tb + 1) * TB // 128], writes=[bps],
                                      signal=(c == NCH - 1))
                                if c in (3, 7, 11):
                                    yield
                            cx.op('act', lambda e: e.activation(out=dst[:, off + tb * TB: off + (tb + 1) * TB], in_=ps[:, 0:TB],
                                                                func=AF.Copy), reads=[bps], writes=[bdst])
                            yield
                        if j < 3:
                            ps, bps = smR.next()
                            for c in range(NCH):
                                cx.op('pe', lambda e: e.matmul(ps[:, 0:3], lhsT=w[:, c, :], rhs=xh[:, c, 0:3],
                                                               start=(c == 0), stop=(c == NCH - 1)),
                                      reads=[bw, B_xh], writes=[bps], signal=(c == NCH - 1))
                            cx.op('act', lambda e: e.activation(out=dst[:, 0:3], in_=ps[:, 0:3], func=AF.Copy),
                                  reads=[bps], writes=[bdst])
                            yield

                for _ in inproj_gen(0):
                    pass
                for h in range(NH):
                    groups = head_groups(h)
                    for j, g in groups:
                        for k in range(4):
                            cx.op('pool', lambda e: e.tensor_scalar(out=diag[:, j * 4 + k, :], in0=idf,
                                                                    scalar1=convw_t[:, g, k:k + 1], scalar2=None,
                                                                    op0=ALU.mult),
                                  reads=[B_cst, B_cw], writes=[B_diag])
                    for j, g in groups:
                        for tb in range(NTB):
                            ps, bps = bigR.next()
                            for k in range(4):
                                cx.op('pe', lambda e: e.matmul(ps[:, 0:TB], lhsT=diag[:, j * 4 + k, :],
                                                               rhs=pre[j][:, tb * TB + k: tb * TB + k + TB],
                                                               start=(k == 0), stop=(k == 3)),
                                      reads=[B_diag, B_pre[j]], writes=[bps], signal=(k == 3))
                            cx.op('act', lambda e: e.activation(out=act_T[j][:, tb * TB:(tb + 1) * TB], in_=ps[:, 0:TB],
                                                                func=AF.Silu), reads=[bps], writes=[B_act[j]])
                    if not main and h == 0:
                        chk(4)
                    if dbg and main and h == 0:
                        for j, nm in enumerate(["d_q", "d_k", "d_v"]):
                            tmpf = sb(f"dbgf{j}", [128, TM], F32)
                            bt = Buf("dbgf")
                            cx.op('dve', lambda e: e.tensor_copy(out=tmpf[:], in_=act_T[j][:]), reads=[B_act[j]], writes=[bt])
                            cx.dma(dbg_out[nm], tmpf[:], reads=[bt])
                    for j in ([0, 1] if main else [1]):
                        s_t, bs_t = sq[0], B_sq[0]
                        cx.op('pool', lambda e: e.tensor_tensor(out=s_t[:], in0=act_T[j][:], in1=act_T[j][:], op=ALU.mult),
                              reads=[B_act[j]], writes=[bs_t])
                        ps, bps = smR.next()
                        for ti in range(NT):
                            cx.op('pe', lambda e: e.matmul(ps[:, ti:ti + 1], lhsT=s_t[:, ti * 128:(ti + 1) * 128],
                                                           rhs=onesb[:, 0:1], start=True, stop=True),
                                  reads=[bs_t, B_cst], writes=[bps], signal=(ti == NT - 1))
                        rsqrt(HS(j), ps[:, 0:NT], 1.0, [bps], [B_hs])
                    bh = beta_all[:, :, h]
                    cx.op('dve', lambda e: e.tensor_tensor(out=HS(3), in0=HS(1), in1=bh, op=ALU.mult),
                          reads=[B_hs, B_tok], writes=[B_hs])
                    cx.op('dve', lambda e: e.tensor_tensor(out=HS(7), in0=HS(3), in1=HS(1), op=ALU.mult),
                          reads=[B_hs], writes=[B_hs])
                    cx.op('dve', lambda e: e.tensor_tensor(out=HS(2), in0=HS(7), in1=egc_all[:, :, h], op=ALU.mult),
                          reads=[B_hs, B_tok], writes=[B_hs])
                    cx.op('dve', lambda e: e.reciprocal(out=HS(4), in_=HS(1)), reads=[B_hs], writes=[B_hs])
                    cx.op('dve', lambda e: e.tensor_tensor(out=HS(5), in0=HS(1), in1=kds_all[:, :, h], op=ALU.mult),
                          reads=[B_hs, B_tok], writes=[B_hs])
                    if main:
                        cx.op('dve', lambda e: e.tensor_scalar(out=HS(0), in0=HS(0), scalar1=float(128 ** -0.5), scalar2=None,
                                                               op0=ALU.mult), reads=[B_hs], writes=[B_hs])
                        for ti in range(NT):
                            tp, btp = smR.next()
                            cx.op('pe', lambda e: e.matmul(tp, lhsT=zT[:, ti * 128:(ti + 1) * 128], rhs=idb, start=True, stop=True),
                                  reads=[B_zT, B_cst], writes=[btp])
                            cx.op('act', lambda e: e.activation(out=zs[:, ti, :], in_=tp, func=AF.Silu),
                                  reads=[btp], writes=[B_zs])
                        cx.op('pool', lambda e: e.tensor_tensor(out=zs[:], in0=zs[:],
                                                                in1=hnw_t[:].unsqueeze(1).to_broadcast([128, NT, 128]), op=ALU.mult),
                              reads=[B_zs, B_hnw], writes=[B_zs])
                        cx.op('dve', lambda e: e.scalar_tensor_tensor(out=hn[:, 2, :], in0=HS(0), scalar=1.0 / 128, in1=HS(0),
                                                                      op0=ALU.mult, op1=ALU.mult), reads=[B_hs], writes=[B_hn])
                        cx.op('act', lambda e: e.activation(out=hn[:, 3, :], in_=HS(0), func=AF.Ln), reads=[B_hs], writes=[B_hn])
                    ot, bot = oTs[h % 2], B_oTs[h % 2]
                    if not main and h == 0:
                        chk(51)
                    cx.op('dve', lambda e: e.tensor_scalar(out=hn[:, 0, :], in0=HS(7), scalar1=-1.0, scalar2=None, op0=ALU.mult),
                          reads=[B_hs], writes=[B_hn])
                    cx.op('dve', lambda e: e.tensor_scalar(out=hn[:, 1, :], in0=HS(2), scalar1=-1.0, scalar2=None, op0=ALU.mult),
                          reads=[B_hs], writes=[B_hn])

                    def mm(lhsT, blh, rhs, brh):
                        ps, bps = smR.next()
                        cx.op('pe', lambda e: e.matmul(ps, lhsT=lhsT, rhs=rhs, start=True, stop=True),
                              reads=[blh, brh], writes=[bps])
                        return ps, bps

                    def evac_copy(dst, bdst, ps, bps):
                        cx.op('act', lambda e: e.activation(out=dst, in_=ps, func=AF.Copy), reads=[bps], writes=[bdst])

                    def evac_add(dst, bdst, ps, bps, addend, badd):
                        cx.op('dve', lambda e: e.scalar_tensor_tensor(out=dst, in0=ps, scalar=1.0, in1=addend,
                                                                      op0=ALU.mult, op1=ALU.add),
                              reads=[bps, badd], writes=[bdst])

                    def evac_mask(dst, bdst, ps, bps, mi):
                        cx.op('dve', lambda e: e.scalar_tensor_tensor(out=dst, in0=ps, scalar=1.0, in1=mk(mi),
                                                                      op0=ALU.mult, op1=ALU.mult),
                              reads=[bps, B_cst], writes=[bdst])

                    def prep_gen(ti, pp, h=h):
                        sl = slice(ti * 128, (ti + 1) * 128)
                        qT_t, kT_t, vT_t = act_T[0][:, sl], act_T[1][:, sl], act_T[2][:, sl]
                        col = lambda i: hs[:, i, ti:ti + 1]
                        gcc = gc_all[:, ti, h:h + 1]
                        t_ad_, B_ad_ = t_dg[pp], B_dg[pp]
                        t_E_, B_E_ = t_dg[pp], B_dg[pp]
                        cx.op('pool', lambda e: e.tensor_scalar(out=t_dg[pp][:], in0=idf, scalar1=gcc, scalar2=None, op0=ALU.mult),
                              reads=[B_cst, B_tok], writes=[B_dg[pp]])
                        yield
                        psG, bG = smR.next()
                        cx.op('pe', lambda e: e.matmul(psG, lhsT=onesf, rhs=t_dg[pp][:], start=True, stop=True),
                              reads=[B_cst, B_dg[pp]], writes=[bG])
                        cx.op('act', lambda e: e.activation(out=t_ad_[:], in_=psG, func=AF.Abs, bias=ngc_all[:, ti, h:h + 1]),
                              reads=[bG, B_tok], writes=[B_ad_])
                        cx.op('dve', lambda e: e.tensor_tensor(out=t_El[pp][:], in0=t_ad_[:], in1=mk(C_PLOW), op=ALU.add),
                              reads=[B_ad_, B_cst], writes=[B_El[pp]])
                        cx.op('act', lambda e: e.activation(out=t_El[pp][:], in_=t_El[pp][:], func=AF.Exp, scale=-1.0),
                              reads=[B_El[pp]], writes=[B_El[pp]])
                        if main:
                            cx.op('dve', lambda e: e.tensor_tensor(out=t_Eu[pp][:], in0=t_ad_[:], in1=mk(C_PUP), op=ALU.add),
                                  reads=[B_ad_, B_cst], writes=[B_Eu[pp]])
                            cx.op('act', lambda e: e.activation(out=t_Eu[pp][:], in_=t_Eu[pp][:], func=AF.Exp, scale=-1.0),
                                  reads=[B_Eu[pp]], writes=[B_Eu[pp]])
                        psKK, bKK = smR.next()
                        cx.op('pe', lambda e: e.matmul(psKK, lhsT=kT_t, rhs=kT_t, start=True, stop=True),
                              reads=[B_act[1]], writes=[bKK])
                        cx.op('dve', lambda e: e.scalar_tensor_tensor(out=t_N[pp][:], in0=psKK, scalar=hn[:, 0, ti:ti + 1], in1=t_El[pp][:],
                                                                      op0=ALU.mult, op1=ALU.mult),
                              reads=[bKK, B_hn, B_El[pp]], writes=[B_N[pp]])
                        cx.op('pool', lambda e: e.tensor_tensor(out=t_N16[pp][:], in0=t_N[pp][:], in1=mk(C_BD16), op=ALU.mult),
                              reads=[B_N[pp], B_cst], writes=[B_N16[pp]])
                        if main:
                            psQK, bQK = smR.next()
                            cx.op('pe', lambda e: e.matmul(psQK, lhsT=kT_t, rhs=qT_t, start=True, stop=True),
                                  reads=[B_act[1], B_act[0]], writes=[bQK])
                            cx.op('dve', lambda e: e.scalar_tensor_tensor(out=r_at[:, ti % RRES, :], in0=psQK, scalar=col(1), in1=t_Eu[pp][:],
                                                                          op0=ALU.mult, op1=ALU.mult),
                                  reads=[bQK, B_hs, B_Eu[pp]], writes=[B_rat[ti % RRES]])
                        yield
                        tpN, btpN = smR.next()
                        cx.op('pe', lambda e: e.matmul(tpN, lhsT=t_N[pp][:], rhs=idb, start=True, stop=True),
                              reads=[B_N[pp], B_cst], writes=[btpN])
                        cx.op('dve', lambda e: e.tensor_copy(out=t_NT[pp][:], in_=tpN), reads=[btpN], writes=[B_NT[pp]])
                        cx.op('pool', lambda e: e.tensor_tensor(out=t_N16T[pp][:], in0=t_NT[pp][:], in1=mk(C_BD16), op=ALU.mult),
                              reads=[B_NT[pp], B_cst], writes=[B_N16T[pp]])
                        PP = lambda i: (t_PP[pp * 2 + i], B_PP[pp * 2 + i])
                        TU = lambda i: (t_TU[pp * 2 + i], B_TU[pp * 2 + i])
                        XX = (t_XX[pp], B_XX[pp])
                        tu0, btu0 = TU(0)
                        cx.op('pool', lambda e: e.tensor_tensor(out=tu0[:, 0, :], in0=t_N16[pp][:], in1=idf, op=ALU.add),
                              reads=[B_N16[pp], B_cst], writes=[btu0])
                        cx.op('pool', lambda e: e.tensor_tensor(out=tu0[:, 1, :], in0=t_N16T[pp][:], in1=idf, op=ALU.add),
                              reads=[B_N16T[pp], B_cst], writes=[btu0])
                        Pk, PkT, bPk = t_N16[pp][:], t_N16T[pp][:], [B_N16[pp], B_N16T[pp]]
                        cur = 0
                        for lvl in range(3):
                            last = (lvl == 2)
                            npp, bnpp = PP(lvl % 2)
                            yield
                            pst, bpst = sm_full()
                            cx.op('pe', lambda e: e.matmul(pst[:, 0, :], lhsT=PkT, rhs=Pk, start=True, stop=True),
                                  reads=bPk, writes=[bpst], signal=last)
                            if not last:
                                cx.op('pe', lambda e: e.matmul(pst[:, 1, :], lhsT=Pk, rhs=PkT, start=True, stop=True),
                                      reads=bPk, writes=[bpst])
                                cx.op('dve', lambda e: e.tensor_copy(out=npp[:, 0:2, :], in_=pst[:, 0:2, :]),
                                      reads=[bpst], writes=[bnpp])
                            else:
                                cx.op('dve', lambda e: e.tensor_copy(out=npp[:, 0, :], in_=pst[:, 0, :]),
                                      reads=[bpst], writes=[bnpp])
                            t0, bt0 = TU(cur)
                            t1, bt1 = TU(1 - cur)
                            yield
                            pst, bpst = sm_full()
                            cx.op('pe', lambda e: e.matmul(pst[:, 0, :], lhsT=t0[:, 1, :], rhs=npp[:, 0, :], start=True, stop=True),
                                  reads=[bt0, bnpp], writes=[bpst], signal=False)
                            cx.op('pe', lambda e: e.matmul(pst[:, 1, :], lhsT=npp[:, 0, :], rhs=t0[:, 1, :], start=True, stop=True),
                                  reads=[bt0, bnpp], writes=[bpst])
                            cx.op('dve', lambda e: e.scalar_tensor_tensor(out=t1[:, 0:2, :], in0=pst[:, 0:2, :], scalar=1.0, in1=t0[:, 0:2, :],
                                                                          op0=ALU.mult, op1=ALU.add),
                                  reads=[bpst, bt0], writes=[bt1])
                            cur = 1 - cur
                            if not last:
                                Pk, PkT, bPk = npp[:, 0, :], npp[:, 1, :], [bnpp]
                        for mi_, lastm in [(C_C32, False), (C_C64, False), (C_C128, True)]:
                            t0, bt0 = TU(cur)
                            t1, bt1 = TU(1 - cur)
                            xx, bxx = XX
                            yield
                            pst, bpst = sm_full()
                            if not lastm:
                                cx.op('pe', lambda e: e.matmul(pst[:, 0, :], lhsT=t_NT[pp][:], rhs=t0[:, 0, :], start=True, stop=True),
                                      reads=[B_NT[pp], bt0], writes=[bpst], signal=False)
                            cx.op('pe', lambda e: e.matmul(pst[:, 1, :], lhsT=t_N[pp][:], rhs=t0[:, 1, :], start=True, stop=True),
                                  reads=[B_N[pp], bt0], writes=[bpst])
                            if not lastm:
                                cx.op('dve', lambda e: e.scalar_tensor_tensor(out=xx[:, 0:2, :], in0=pst[:, 0:2, :], scalar=1.0,
                                                                              in1=mk(mi_).unsqueeze(1).to_broadcast([128, 2, 128]),
                                                                              op0=ALU.mult, op1=ALU.mult),
                                      reads=[bpst, B_cst], writes=[bxx])
                            else:
                                cx.op('dve', lambda e: e.scalar_tensor_tensor(out=xx[:, 1, :], in0=pst[:, 1, :], scalar=1.0, in1=mk(mi_),
                                                                              op0=ALU.mult, op1=ALU.mult),
                                      reads=[bpst, B_cst], writes=[bxx])
                            yield
                            pst, bpst = sm_full()
                            if not lastm:
                                cx.op('pe', lambda e: e.matmul(pst[:, 0, :], lhsT=t0[:, 1, :], rhs=xx[:, 0, :], start=True, stop=True),
                                      reads=[bt0, bxx], writes=[bpst], signal=False)
                            cx.op('pe', lambda e: e.matmul(pst[:, 1, :], lhsT=t0[:, 0, :], rhs=xx[:, 1, :], start=True, stop=True),
                                  reads=[bt0, bxx], writes=[bpst])
                            if not lastm:
                                cx.op('dve', lambda e: e.scalar_tensor_tensor(out=t1[:, 0:2, :], in0=pst[:, 0:2, :], scalar=1.0, in1=t0[:, 0:2, :],
                                                                              op0=ALU.mult, op1=ALU.add),
                                      reads=[bpst, bt0], writes=[bt1])
                            else:
                                cx.op('dve', lambda e: e.scalar_tensor_tensor(out=r_U[:, ti % RRES, :], in0=pst[:, 1, :], scalar=1.0, in1=t0[:, 1, :],
                                                                              op0=ALU.mult, op1=ALU.add),
                                      reads=[bpst, bt0], writes=[B_rU[ti % RRES]])
                            cur = 1 - cur
                        tpk, btpk = smR.next()
                        cx.op('pe', lambda e: e.matmul(tpk, lhsT=kT_t, rhs=idb, start=True, stop=True), reads=[B_act[1], B_cst], writes=[btpk])
                        cx.op('act', lambda e: e.activation(out=r_kd[:, ti % RRES, :], in_=tpk, func=AF.Copy, scale=col(5)),
                              reads=[btpk, B_hs], writes=[B_rkd[ti % RRES]])
                        tpv, btpv = smR.next()
                        cx.op('pe', lambda e: e.matmul(tpv, lhsT=vT_t, rhs=idb, start=True, stop=True), reads=[B_act[2], B_cst], writes=[btpv])
                        cx.op('act', lambda e: e.activation(out=r_vb[:, ti % RRES, :], in_=tpv, func=AF.Copy, scale=col(3)),
                              reads=[btpv, B_hs], writes=[B_rvb[ti % RRES]])

                    def state_gen(ti, h=h):
                        pp = ti % 2
                        sl = slice(ti * 128, (ti + 1) * 128)
                        qT_t, kT_t = act_T[0][:, sl], act_T[1][:, sl]
                        col = lambda i: hs[:, i, ti:ti + 1]
                        Sb_h = S_b[:, h, :]
                        Sf_h = S_f[:, h, :]
                        psA, bA = mm(kT_t, B_act[1], Sb_h, B_S[h])
                        cx.op('dve', lambda e: e.scalar_tensor_tensor(out=t_r[pp][:], in0=psA, scalar=hn[:, 1, ti:ti + 1], in1=r_vb[:, ti % RRES, :],
                                                                      op0=ALU.mult, op1=ALU.add),
                              reads=[bA, B_hn, B_rvb[ti % RRES]], writes=[B_r[pp]])
                        if main:
                            psO1, bO1 = mm(qT_t, B_act[0], Sb_h, B_S[h])
                            cx.op('act', lambda e: e.activation(out=t_so1[pp][:], in_=psO1, func=AF.Copy,
                                                                scale=egc_all[:, ti, h:h + 1]),
                                  reads=[bO1, B_tok], writes=[B_so1[pp]])
                        yield
                        psB, bB = mm(r_U[:, ti % RRES, :], B_rU[ti % RRES], t_r[pp][:], B_r[pp])
                        cx.op('act', lambda e: e.activation(out=t_vn[pp][:], in_=psB, func=AF.Copy, scale=col(4)),
                              reads=[bB, B_hs], writes=[B_vn[pp]])
                        yield
                        psS, bS = mm(r_kd[:, ti % RRES, :], B_rkd[ti % RRES], t_vn[pp][:], B_vn[pp])
                        cx.op('dve', lambda e: e.scalar_tensor_tensor(out=Sf_h, in0=Sf_h, scalar=edec_all[:, ti, h:h + 1], in1=psS,
                                                                      op0=ALU.mult, op1=ALU.add),
                              reads=[bS, B_tok, B_S[h]], writes=[B_S[h]])
                        cx.op('act', lambda e: e.activation(out=Sb_h, in_=Sf_h, func=AF.Copy), reads=[B_S[h]], writes=[B_S[h]])
                        if main:
                            psO2, bO2 = mm(r_at[:, ti % RRES, :], B_rat[ti % RRES], t_vn[pp][:], B_vn[pp])
                            cx.op('dve', lambda e: e.scalar_tensor_tensor(out=r_op[:, ti % RRES, :], in0=psO2, scalar=1.0, in1=t_so1[pp][:],
                                                                          op0=ALU.mult, op1=ALU.add),
                                  reads=[bO2, B_so1[pp]], writes=[B_rop[ti % RRES]])

                    def post_gen(ti, h=h):
                        pp = ti % 2
                        sl = slice(ti * 128, (ti + 1) * 128)
                        opre, bopre = r_op[:, ti % RRES, :], B_rop[ti % RRES]
                        ms = t_ms[:, pp:pp + 1]
                        cx.op('act', lambda e: e.activation(out=t_junk[pp][:], in_=opre, func=AF.Square, accum_out=ms),
                              reads=[bopre], writes=[B_junk[pp], B_ms[pp]])
                        cx.op('act', lambda e: e.activation(out=ms, in_=ms, func=AF.Ln, scale=hn[:, 2, ti:ti + 1], bias=eps_c),
                              reads=[B_ms[pp], B_hn, B_small], writes=[B_ms[pp]])
                        cx.op('act', lambda e: e.activation(out=ms, in_=ms, func=AF.Exp, scale=-0.5, bias=hn[:, 3, ti:ti + 1]),
                              reads=[B_ms[pp], B_hn], writes=[B_ms[pp]])
                        yield
                        cx.op('pool', lambda e: e.tensor_scalar(out=t_o1[pp][:], in0=opre, scalar1=ms, scalar2=None, op0=ALU.mult),
                              reads=[bopre, B_ms[pp]], writes=[B_o1[pp]])
                        cx.op('pool', lambda e: e.tensor_tensor(out=t_og[pp][:], in0=t_o1[pp][:], in1=zs[:, ti, :], op=ALU.mult),
                              reads=[B_o1[pp], B_zs], writes=[B_og[pp]])
                        yield
                        tpo, btpo = smR.next()
                        cx.op('pe', lambda e: e.matmul(tpo, lhsT=t_og[pp][:], rhs=idb, start=True, stop=True),
                              reads=[B_og[pp], B_cst], writes=[btpo])
                        cx.op('act', lambda e: e.activation(out=ot[:, sl], in_=tpo, func=AF.Copy), reads=[btpo], writes=[bot])

                    active = []
                    inproj_chain = [['inproj', -1, -1, inproj_gen(h + 1)]] if h + 1 < NH else []
                    free_slots = list(range(GCH))
                    next_prep, next_state, state_on = 0, 0, False
                    next_post, post_on = (0 if main else NT), False
                    prep_done = set()
                    while next_state < NT or next_post < NT or active or inproj_chain:
                        if (not post_on) and next_post < NT and next_post < next_state:
                            active.append(['post', next_post, -1, post_gen(next_post)])
                            post_on = True
                        while free_slots and next_prep < NT and next_prep < min(next_state, next_post if main else next_state) + RRES:
                            slot = free_slots.pop(0)
                            active.append(['prep', next_prep, slot, prep_gen(next_prep, slot)])
                            next_prep += 1
                        if (not state_on) and next_state < NT and next_state in prep_done:
                            active.append(['state', next_state, -1, state_gen(next_state)])
                            state_on = True
                        for a in list(active) + list(inproj_chain):
                            try:
                                next(a[3])
                            except StopIteration:
                                if a[0] == 'inproj':
                                    inproj_chain.remove(a)
                                    continue
                                active.remove(a)
                                if a[0] == 'prep':
                                    prep_done.add(a[1])
                                    free_slots.append(a[2])
                                elif a[0] == 'state':
                                    state_on = False
                                    next_state += 1
                                elif a[0] == 'post':
                                    post_on = False
                                    next_post += 1
                    if main:
                        cx.dma(oT_d[h], ot[:], reads=[bot], writes=[B_oTd[h]])
                if not main:
                    cx.op('pool', lambda e: e.tensor_copy(out=xh[:, :, 0:3], in_=xnT[:, :, TM - 3:TM]),
                          reads=[B_xnT[NT - 1]], writes=[B_xh])
                if dbg and main:
                    cx.dma(dbg_out["d_S"], S_f[:], reads=B_S)
                cx.barrier()
            cx.cur = es

        phase('P')
        chk(6)
        phase('M')
        chk(7)

        with ExitStack() as sB:
            cx.cur = sB
            vn_all = sb("vn_all", [128, NT, 1024], BF16)
            B_vnall = [Buf(f"vnall{t}") for t in range(NT)]
            with ExitStack() as sB1:
                cx.cur = sB1
                wvb_b = sb("wvb_b", [128, NCH, 1024], BF16)
                B_wvb = Buf("wvb")
                stg = [sb(f"stgB{i}", [128, NCH, 128], F32) for i in range(2)]
                B_stg = [Buf("stgB0"), Buf("stgB1")]
                for c4 in range(8):
                    f, bfb = stg[c4 % 2], B_stg[c4 % 2]
                    cx.dma(f[:], w_vb[:, :, c4 * 128:(c4 + 1) * 128], writes=[bfb])
                    cx.op('pool' if c4 % 2 == 0 else 'dve',
                          lambda e: e.tensor_tensor(out=wvb_b[:, :, c4 * 128:(c4 + 1) * 128], in0=f[:],
                                                    in1=normw_t[:].unsqueeze(2).to_broadcast([128, NCH, 128]), op=ALU.mult),
                          reads=[bfb, B_nw], writes=[B_wvb])
                lnw_t = sb("lnw_t", [128, 1024], F32)
                lnb_t = sb("lnb_t", [128, 1024], F32)
                B_ln = Buf("ln")
                cx.dma(lnw_t[:], lnw.partition_broadcast(128), writes=[B_ln])
                cx.dma(lnb_t[:], lnb.partition_broadcast(128), writes=[B_ln])
                vtmp = [sb(f"vtmp{i}", [128, 1024], F32) for i in range(2)]
                B_vtmp = [Buf("vtmp0"), Buf("vtmp1")]
                for ti in range(NT):
                    pss = []
                    for hf in range(2):
                        ps, bps = bigR.next()
                        for c in range(NCH):
                            cx.op('pe', lambda e: e.matmul(ps[:, :], lhsT=xnT[:, c, ti * 128:(ti + 1) * 128],
                                                           rhs=wvb_b[:, c, hf * 512:(hf + 1) * 512],
                                                           start=(c == 0), stop=(c == NCH - 1)),
                                  reads=[B_xnT[ti], B_wvb], writes=[bps], signal=(c == NCH - 1))
                        pss.append((ps, bps))
                    vt, bvt = vtmp[ti % 2], B_vtmp[ti % 2]
                    bs = B_st[2 + ti % 2]
                    s0 = 8 + (ti % 2) * 8
                    for hf in range(2):
                        ps, bps = pss[hf]
                        cx.op('act', lambda e: e.activation(out=vt[:, hf * 512:(hf + 1) * 512], in_=ps[:, :], func=AF.Copy,
                                                            accum_out=st16[:, s0 + hf:s0 + hf + 1]),
                              reads=[bps], writes=[bvt, bs])
                        cx.op('act', lambda e: e.activation(out=vn_all[:, ti, hf * 512:(hf + 1) * 512], in_=ps[:, :], func=AF.Square,
                                                            accum_out=st16[:, s0 + 2 + hf:s0 + 3 + hf]),
                              reads=[bps], writes=[B_vnall[ti], bs])
                    mean = st16[:, s0 + 4:s0 + 5]
                    var = st16[:, s0 + 5:s0 + 6]
                    m2 = st16[:, s0 + 6:s0 + 7]
                    cx.op('dve', lambda e: e.tensor_scalar(out=mean, in0=st16[:, s0:s0 + 1], scalar1=st16[:, s0 + 1:s0 + 2],
                                                           scalar2=1.0 / 1024, op0=ALU.add, op1=ALU.mult), reads=[bs], writes=[bs])
                    cx.op('dve', lambda e: e.tensor_scalar(out=var, in0=st16[:, s0 + 2:s0 + 3], scalar1=st16[:, s0 + 3:s0 + 4],
                                                           scalar2=1.0 / 1024, op0=ALU.add, op1=ALU.mult), reads=[bs], writes=[bs])
                    cx.op('dve', lambda e: e.tensor_tensor(out=m2, in0=mean, in1=mean, op=ALU.mult), reads=[bs], writes=[bs])
                    cx.op('dve', lambda e: e.tensor_tensor(out=var, in0=var, in1=m2, op=ALU.subtract), reads=[bs], writes=[bs])
                    rsqrt(var, var, 1.0, [bs], [bs])
                    cx.op('dve', lambda e: e.tensor_scalar(out=vt[:], in0=vt[:], scalar1=mean, scalar2=var,
                                                           op0=ALU.subtract, op1=ALU.mult), reads=[bvt, bs], writes=[bvt])
                    cx.op('pool', lambda e: e.tensor_tensor(out=vt[:], in0=vt[:], in1=lnw_t[:], op=ALU.mult),
                          reads=[bvt, B_ln], writes=[bvt])
                    cx.op('pool', lambda e: e.tensor_tensor(out=vn_all[:, ti, :], in0=vt[:], in1=lnb_t[:], op=ALU.add),
                          reads=[bvt, B_ln], writes=[B_vnall[ti]])
                cx.barrier()
            cx.cur = sB
            wr = WRing(cx, 2, 4)
            bsp_t = sb("bsp_t", [128, 8, 128], F32)
            B_bsp = Buf("bsp")
            cx.dma(bsp_t[:].rearrange("p a b -> p (a b)"), bsp.partition_broadcast(128), writes=[B_bsp])
            wsp_f = sb("wsp_f", [128, 8, 128], F32)
            wsp_m = sb("wsp_m", [128, 8, 128], BF16)
            Rsp = sb("Rsp", [128, 8, 128], BF16)
            B_wsp = Buf("wsp")
            B_R = Buf("Rsp")
            cx.dma(wsp_f[:], wsp, writes=[B_wsp])
            cx.op('dve', lambda e: e.tensor_tensor(out=wsp_m[:], in0=wsp_f[:],
                                                   in1=cst_f[:, C_TRIL:C_TRIL + 1, :].to_broadcast([128, 8, 128]), op=ALU.mult),
                  reads=[B_wsp, B_cst], writes=[B_wsp])
            for g in range(8):
                tp, btp = smR.next()
                cx.op('pe', lambda e: e.matmul(tp, lhsT=wsp_m[:, g, :], rhs=idb, start=True, stop=True),
                      reads=[B_wsp, B_cst], writes=[btp])
                cx.op('act', lambda e: e.activation(out=Rsp[:, g, :], in_=tp, func=AF.Copy), reads=[btp], writes=[B_R])
            uT = sb("uT", [128, TM], BF16)
            szT = sb("szT", [128, TM], BF16)
            buT, bszT = Buf("uT"), Buf("szT")
            oTs = [sb(f"oTsB{i}", [128, TM], BF16) for i in range(2)]
            B_oTs = [Buf("oTsB0"), Buf("oTsB1")]
            tA = [sb(f"tA{i}", [128, 512], F32) for i in range(2)]
            tBb = [sb(f"tB{i}", [128, 512], F32) for i in range(2)]
            B_tA = [Buf("tA0"), Buf("tA1")]
            B_tB = [Buf("tB0"), Buf("tB1")]
            nxt = (load_w(wr, w_fm[32]), load_w(wr, w_fm[40]))
            for g in range(8):
                (wu, bwu), (wz, bwz) = nxt
                if g + 1 < 8:
                    nxt = (load_w(wr, w_fm[32 + g + 1]), load_w(wr, w_fm[40 + g + 1]))
                inproj_fm(wu, bwu, lambda tb: uT[:, tb * TB:(tb + 1) * TB], buT, AF.Copy)
                inproj_fm(wz, bwz, lambda tb: szT[:, tb * TB:(tb + 1) * TB], bszT, AF.Silu)
                ot, bot = oTs[g % 2], B_oTs[g % 2]
                NB4 = min(4, NT)
                for t4 in range(NT // NB4):
                    pst, bpst = sm_full()
                    for j in range(NB4):
                        ti = t4 * NB4 + j
                        cx.op('pe', lambda e: e.matmul(pst[:, j, :], lhsT=vn_all[:, ti, g * 128:(g + 1) * 128], rhs=Rsp[:, g, :],
                                                       start=True, stop=True), reads=[B_vnall[ti], B_R], writes=[bpst],
                              signal=(j == NB4 - 1))
                    pp = t4 % 2
                    sl4 = slice(t4 * NB4 * 128, (t4 + 1) * NB4 * 128)
                    v3 = lambda t: t[:, 0:NB4 * 128].rearrange("p (a b) -> p a b", a=NB4)
                    cx.op('dve', lambda e: e.tensor_tensor(out=v3(tA[pp]), in0=pst[:, 0:NB4, :],
                                                           in1=bsp_t[:, g:g + 1, :].to_broadcast([128, NB4, 128]), op=ALU.add),
                          reads=[bpst, B_bsp], writes=[B_tA[pp]])
                    cx.op('pool', lambda e: e.tensor_tensor(out=tBb[pp][:, 0:NB4 * 128], in0=tA[pp][:, 0:NB4 * 128], in1=uT[:, sl4], op=ALU.mult),
                          reads=[B_tA[pp], buT], writes=[B_tB[pp]])
                    cx.op('dve', lambda e: e.tensor_tensor(out=ot[:, sl4], in0=tBb[pp][:, 0:NB4 * 128], in1=szT[:, sl4], op=ALU.mult),
                          reads=[B_tB[pp], bszT], writes=[bot])
                cx.dma(oT_d[8 + g], ot[:], reads=[bot], writes=[B_oTd[8 + g]])
            cx.barrier()
        cx.cur = es

        chk(8)
        with ExitStack() as sO:
            cx.cur = sO
            wo_b = sb("wo_b", [128, NCH, D], BF16)
            B_wo = Buf("wo")
            stg = [sb(f"stgO{i}", [128, NCH, 128], F32) for i in range(2)]
            B_stg = [Buf("stgO0"), Buf("stgO1")]
            for c4 in range(16):
                f, bfb = stg[c4 % 2], B_stg[c4 % 2]
                cx.dma(f[:], w_o[:, :, c4 * 128:(c4 + 1) * 128], writes=[bfb], q=('sp' if c4 % 2 == 0 else 'act'))
                cx.op('pool' if c4 % 2 == 0 else 'dve',
                      lambda e: e.tensor_copy(out=wo_b[:, :, c4 * 128:(c4 + 1) * 128], in_=f[:]),
                      reads=[bfb], writes=[B_wo])
            fnw_t = sb("fnw_t", [128, D], F32)
            B_fnw = Buf("fnw")
            cx.dma(fnw_t[:], fnw.partition_broadcast(128), writes=[B_fnw])
            for c in range(NCH):
                cx.dma(xnT[:, c, :], oT_d[c], reads=[B_oTd[c]], writes=B_xnT, q=('sp' if c % 2 == 0 else 'act'))
            if dbg:
                with ExitStack() as sd:
                    cx.cur = sd
                    dtmp = sb("dbg_oT", [128, NCH, TM], F32)
                    bt = Buf("dbgoT")
                    cx.op('dve', lambda e: e.tensor_copy(out=dtmp[:], in_=xnT[:]), reads=B_xnT, writes=[bt])
                    cx.dma(dbg_out["d_oT"], dtmp[:], reads=[bt])
                    cx.barrier()
                cx.cur = sO
            xin = [sb(f"xinO{i}", [128, D], F32) for i in range(2)]
            B_xin = [Buf("xinO0"), Buf("xinO1")]
            junk = sb("junkO", [128, D], BF16)
            B_junk = Buf("junkO")
            for ti in range(NT):
                xi, bxi = xin[ti % 2], B_xin[ti % 2]
                cx.dma(xi[:], xcat[TM + ti * 128: TM + (ti + 1) * 128, :], writes=[bxi])
                bs = B_st[ti % 2]
                s0 = 32 + (ti % 2) * 8
                for n4 in range(4):
                    ps, bps = bigR.next()
                    for c in range(NCH):
                        cx.op('pe', lambda e: e.matmul(ps[:, :], lhsT=xnT[:, c, ti * 128:(ti + 1) * 128],
                                                       rhs=wo_b[:, c, n4 * 512:(n4 + 1) * 512],
                                                       start=(c == 0), stop=(c == NCH - 1)),
                              reads=[B_xnT[ti], B_wo], writes=[bps], signal=(c == NCH - 1))
                    cx.op('dve', lambda e: e.tensor_tensor(out=xi[:, n4 * 512:(n4 + 1) * 512], in0=ps[:, :],
                                                           in1=xi[:, n4 * 512:(n4 + 1) * 512], op=ALU.add),
                          reads=[bps, bxi], writes=[bxi])
                ssq = st16[:, s0:s0 + 1]
                cx.op('act', lambda e: e.activation(out=junk[:], in_=xi[:], func=AF.Square, accum_out=ssq),
                      reads=[bxi], writes=[B_junk, bs])
                rsqrt(ssq, ssq, 1.0 / D, [bs], [bs])
                cx.op('dve', lambda e: e.scalar_tensor_tensor(out=xi[:], in0=xi[:], scalar=ssq, in1=fnw_t[:],
                                                               op0=ALU.mult, op1=ALU.mult),
                      reads=[bxi, bs, B_fnw], writes=[bxi])
                cx.dma(out[ti * 128:(ti + 1) * 128, :], xi[:], reads=[bxi])
            cx.finish()
        print("instructions:", cx.nins, {k: v for k, v in cx.cnt.items()})
    except StopBuild:
        print("stopped early at", STOP[0])
    return nc


def prep_inputs(inp, NT, seq, nb):
    TM = NT * 128
    x = np.asarray(inp["x"], np.float32)
    w_in = np.asarray(inp["w_in"], np.float32)[0]
    w_out = np.asarray(inp["w_out"], np.float32)[0]

    def fm(cols):
        return np.ascontiguousarray(w_in[:, cols].reshape(NCH, 128, 128).transpose(1, 0, 2))
    groups = []
    for sec in (0, 1024, 2048, 3072, 4112, 6160):
        for g in range(8):
            groups.append(fm(slice(sec + g * 128, sec + (g + 1) * 128)))
    w_fm = np.stack(groups, 0)
    w_vb = np.ascontiguousarray(w_in[:, 5136:6160].reshape(NCH, 128, 1024).transpose(1, 0, 2))
    w_ba = np.ascontiguousarray(w_in[:, 4096:4112].reshape(NCH, 128, 16).transpose(1, 0, 2))
    w_o = np.ascontiguousarray(w_out.reshape(NCH, 128, D).transpose(1, 0, 2))
    convw = np.ascontiguousarray(np.asarray(inp["conv_w"], np.float32)[0].reshape(4, 24, 128).transpose(2, 1, 0))
    normw = np.ascontiguousarray(np.asarray(inp["norm_w"], np.float32)[0].reshape(NCH, 128).T)
    common = dict(
        w_fm=w_fm, w_vb=w_vb, w_ba=w_ba, w_o=w_o, convw=convw, normw=normw,
        alog=np.asarray(inp["a_log"], np.float32).reshape(1, 8),
        dtb=np.asarray(inp["dt_bias"], np.float32).reshape(1, 8),
        hnw=np.asarray(inp["head_norm_w"], np.float32).reshape(1, 128),
        lnw=np.asarray(inp["sgu_ln_w"], np.float32).reshape(1, 1024),
        lnb=np.asarray(inp["sgu_ln_b"], np.float32).reshape(1, 1024),
        wsp=np.ascontiguousarray(np.asarray(inp["w_spatial"], np.float32)[0].transpose(1, 0, 2)),
        bsp=np.asarray(inp["b_spatial"], np.float32).reshape(1, 1024),
        fnw=np.asarray(inp["final_norm_w"], np.float32).reshape(1, D),
        cst=make_consts(),
    )
    maps, ids = [], []
    nhalf = seq // TM
    for b in range(nb):
        for hf in range(nhalf):
            xc = np.zeros((2 * TM, D), np.float32)
            if hf > 0:
                xc[:TM] = x[b, (hf - 1) * TM: hf * TM]
            xc[TM:] = x[b, hf * TM:(hf + 1) * TM]
            m = dict(common)
            m["xcat"] = xc
            maps.append(m)
            ids.append((b, hf))
    return maps, ids


_CACHE = {}


def kernel(**inputs):
    NT = 16
    if NT not in _CACHE:
        _CACHE[NT] = build_program(NT)
    nc = _CACHE[NT]
    maps, ids = prep_inputs(inputs, NT, 4096, 4)
    res = run_bass_kernel_spmd(nc, maps, core_ids=list(range(8)))
    outp = np.zeros((4, 4096, D), np.float32)
    TM = NT * 128
    for (b, hf), r in zip(ids, res.results):
        outp[b, hf * TM:(hf + 1) * TM] = r["out"]
    return outp
```

```python
import numpy as np
from contextlib import ExitStack
import concourse.bass as bass
import concourse.mybir as mybir
from concourse.bass_utils import run_bass_kernel_spmd

F32 = mybir.dt.float32
BF16 = mybir.dt.bfloat16
ALU = mybir.AluOpType
AF = mybir.ActivationFunctionType

D = 2048
NCH = 16
NH = 8
EPS = 1e-6
SAME_SYNC = True
NDS = 8
GCH = 5
RRES = 8

C_ID, C_LOW, C_UP, C_BD16, C_C32, C_C64, C_C128, C_ONES, C_TRIL, C_PLOW, C_PUP = range(11)
NCST = 11


def make_consts():
    p = np.arange(128)[:, None]
    f = np.arange(128)[None, :]
    c = np.zeros((128, NCST, 128), np.float32)
    c[:, C_ID] = (p == f)
    c[:, C_LOW] = (f < p)
    c[:, C_UP] = (f >= p)
    bd = lambda b: ((p // b) == (f // b)).astype(np.float32)
    c[:, C_BD16] = bd(16)
    c[:, C_C32] = bd(32) - bd(16)
    c[:, C_C64] = bd(64) - bd(32)
    c[:, C_C128] = 1.0 - bd(64)
    c[:, C_ONES] = 1.0
    c[:, C_TRIL] = (f <= p)
    c[:, C_PLOW] = 30000.0 * (f >= p)
    c[:, C_PUP] = 30000.0 * (f < p)
    return c


class StopBuild(Exception):
    pass


STOP = [0]


class Buf:
    __slots__ = ("name", "lw", "rd", "excl")

    def __init__(self, name, excl=False):
        self.name = name
        self.lw = None
        self.rd = {}
        self.excl = excl


class Ctx:
    def __init__(self, nc, es):
        self.nc = nc
        self.es = es
        self.eng = {'pe': nc.tensor, 'act': nc.scalar, 'dve': nc.vector, 'pool': nc.gpsimd, 'sp': nc.sync}
        self.sem = {k: es.enter_context(nc.semaphore("s_" + k)) for k in ['pe', 'act', 'dve', 'pool']}
        self.cnt = {k: 0 for k in self.sem}
        self.dsem = [es.enter_context(nc.semaphore(f"dq{i}")) for i in range(NDS)]
        self.dcnt = [0] * NDS
        self.dn = 0
        self.waited = {}
        self.nins = 0
        self.cur = es
        self.nsb = 0

    def sb(self, name, shape, dt):
        self.nsb += 1
        return self.cur.enter_context(self.nc.sbuf_tensor(f"{name}_{self.nsb}", shape, dt))

    def barrier(self):
        for e in ['pe', 'act', 'dve', 'pool', 'sp']:
            for k in self.sem:
                if k != e and self.cnt[k] > 0:
                    self._wait(e, k, self.cnt[k])
            for i in range(NDS):
                if self.dcnt[i] > 0:
                    self._wait(e, ('d', i), self.dcnt[i])

    def _wait(self, e, key, val):
        if self.waited.get((e, key), 0) >= val:
            return
        sem = self.sem[key] if isinstance(key, str) else self.dsem[key[1]]
        self.eng[e].wait_ge(sem, val)
        self.waited[(e, key)] = val

    def _deps(self, e, reads, writes):
        need = {}
        for b in reads:
            if b.lw is not None and need.get(b.lw[0], 0) < b.lw[1]:
                need[b.lw[0]] = b.lw[1]
        same = need.get(e, 0)
        for b in writes:
            if b.lw is not None and b.lw[0] != e and need.get(b.lw[0], 0) < b.lw[1]:
                need[b.lw[0]] = b.lw[1]
            for k, v in b.rd.items():
                if k != e and need.get(k, 0) < v:
                    need[k] = v
        for k, v in need.items():
            if k == e and (e == 'pe' or not SAME_SYNC):
                continue
            self._wait(e, k, v)

    def op(self, e, fn, reads=(), writes=(), signal=True):
        ex = [b for b in reads if b.excl]
        if ex:
            reads = [b for b in reads if not b.excl]
            writes = list(writes) + ex
        self._deps(e, reads, writes)
        ins = fn(self.eng[e])
        self.nins += 1
        if signal:
            self.cnt[e] += 1
            ins.then_inc(self.sem[e], 1)
            v = self.cnt[e]
        else:
            v = self.cnt[e] + 1
        for b in reads:
            if b.rd.get(e, 0) < v:
                b.rd[e] = v
        for b in writes:
            b.lw = (e, v)
            b.rd = {}

    def dma(self, out_ap, in_ap, reads=(), writes=(), q='sp'):
        i = self.dn % NDS
        self.dn += 1
        key = ('d', i)
        if self.dcnt[i] > 0:
            self._wait(q, key, self.dcnt[i])
        self._deps(q, reads, writes)
        self.dcnt[i] += 16
        self.eng[q].dma_start(out=out_ap, in_=in_ap).then_inc(self.dsem[i], 16)
        self.nins += 1
        v = self.dcnt[i]
        for b in reads:
            if b.rd.get(key, 0) < v:
                b.rd[key] = v
        for b in writes:
            b.lw = (key, v)
            b.rd = {}

    def finish(self, q='sp'):
        for i in range(NDS):
            if self.dcnt[i] > 0:
                self.eng[q].wait_ge(self.dsem[i], self.dcnt[i])


class Ring:
    def __init__(self, items):
        self.items = items
        self.i = 0

    def next(self):
        it = self.items[self.i % len(self.items)]
        self.i += 1
        return it


class WRing:
    def __init__(self, cx, nws, nwb):
        self.cx = cx
        self.f = [cx.sb(f"wst_f{i}", [128, NCH, 128], F32) for i in range(nws)]
        self.bf = [Buf(f"wstf{i}") for i in range(nws)]
        self.w = [cx.sb(f"wst_b{i}", [128, NCH, 128], BF16) for i in range(nwb)]
        self.bw = [Buf(f"wstb{i}") for i in range(nwb)]
        self.i = 0


def build_program(NT, dbg=False):
    TM = NT * 128
    TB = min(512, TM)
    NTB = TM // TB
    nc = bass.Bass("TRN2", target_bir_lowering=False)
    dt_in = lambda n, s: nc.dram_tensor(n, s, F32, kind="ExternalInput").ap()
    xcat = dt_in("xcat", [2 * TM, D])
    w_fm = dt_in("w_fm", [48, 128, NCH, 128])
    w_vb = dt_in("w_vb", [128, NCH, 1024])
    w_ba = dt_in("w_ba", [128, NCH, 16])
    w_o = dt_in("w_o", [128, NCH, D])
    convw = dt_in("convw", [128, 24, 4])
    normw = dt_in("normw", [128, NCH])
    alog = dt_in("alog", [1, 8])
    dtb = dt_in("dtb", [1, 8])
    hnw = dt_in("hnw", [1, 128])
    lnw = dt_in("lnw", [1, 1024])
    lnb = dt_in("lnb", [1, 1024])
    wsp = dt_in("wsp", [128, 8, 128])
    bsp = dt_in("bsp", [1, 1024])
    fnw = dt_in("fnw", [1, D])
    cst = dt_in("cst", [128, NCST, 128])
    out = nc.dram_tensor("out", [TM, D], F32, kind="ExternalOutput").ap()
    oT_d = nc.dram_tensor("oT_d", [NCH, 128, TM], BF16, kind="Internal").ap()
    B_oTd = [Buf(f"oTd{c}") for c in range(NCH)]
    dbg_out = {}
    if dbg:
        for nm, shp in [("d_xnT", [128, NCH, TM]), ("d_q", [128, TM]), ("d_k", [128, TM]), ("d_v", [128, TM]),
                        ("d_beta", [128, NT, 8]), ("d_g", [128, NT, 8]), ("d_U", [128, 128]), ("d_N", [128, 128]),
                        ("d_S", [128, 8, 128]), ("d_oT", [128, NCH, TM]), ("d_E", [128, 128]),
                        ("d_opre", [128, 128]), ("d_vnew", [128, 128])]:
            dbg_out[nm] = nc.dram_tensor(nm, shp, F32, kind="ExternalOutput").ap()

    try:
      with ExitStack() as es:
        cx = Ctx(nc, es)
        sb = cx.sb

        def chk(n):
            if STOP[0] == n:
                cx.barrier()
                cx.finish()
                raise StopBuild()
        big = [es.enter_context(nc.psum_tensor(f"pbig{i}", [128, 512], F32)) for i in range(2)]
        bigR = Ring([(t, Buf(f"pbig{i}", True)) for i, t in enumerate(big)])
        NSM = 6
        smt = [es.enter_context(nc.psum_tensor(f"psm{i}", [128, 4, 128], F32)) for i in range(NSM)]
        smB = [Buf(f"psm{i}", True) for i in range(NSM)]
        smR = Ring([(smt[i][:, 0, :], smB[i]) for i in range(NSM)])

        def sm_full():
            i = smR.i % NSM
            smR.i += 1
            return smt[i], smB[i]

        cst_f = sb("cst_f", [128, NCST, 128], F32)
        cst_b = sb("cst_b", [128, 2, 128], BF16)
        B_cst = Buf("cst")
        cx.dma(cst_f[:], cst, writes=[B_cst])
        cx.op('dve', lambda e: e.tensor_copy(out=cst_b[:, 0, :], in_=cst_f[:, C_ID, :]), reads=[B_cst], writes=[B_cst])
        cx.op('dve', lambda e: e.tensor_copy(out=cst_b[:, 1, :], in_=cst_f[:, C_ONES, :]), reads=[B_cst], writes=[B_cst])
        idb = cst_b[:, 0, :]
        idf = cst_f[:, C_ID, :]
        onesb = cst_b[:, 1, :]
        onesf = cst_f[:, C_ONES, :]

        def mk(i):
            return cst_f[:, i, :]

        normw_t = sb("normw_t", [128, NCH], F32)
        B_nw = Buf("nw")
        cx.dma(normw_t[:], normw, writes=[B_nw])
        convw_t = sb("convw_t", [128, 24, 4], F32)
        B_cw = Buf("cw")
        cx.dma(convw_t[:], convw, writes=[B_cw])
        small = sb("small", [128, 64], F32)
        B_small = Buf("small")
        cx.dma(small[:, 0:8], alog.partition_broadcast(128), writes=[B_small])
        cx.dma(small[:, 8:16], dtb.partition_broadcast(128), writes=[B_small])
        negA = small[:, 16:24]
        cx.op('act', lambda e: e.activation(out=negA, in_=small[:, 0:8], func=AF.Exp), reads=[B_small], writes=[B_small])
        cx.op('dve', lambda e: e.tensor_scalar(out=negA, in0=negA, scalar1=-1.0, scalar2=None, op0=ALU.mult),
              reads=[B_small], writes=[B_small])
        dtb_t = small[:, 8:16]
        eps_c = small[:, 24:25]
        one_c = small[:, 25:26]
        cx.op('pool', lambda e: e.memset(small[:, 24:25], EPS), writes=[B_small])
        cx.op('pool', lambda e: e.memset(small[:, 25:26], 1.0), writes=[B_small])

        def rsqrt(out_ap, in_ap, scale, rd, wr):
            cx.op('act', lambda e: e.activation(out=out_ap, in_=in_ap, func=AF.Ln, scale=scale, bias=eps_c),
                  reads=list(rd) + [B_small], writes=wr)
            cx.op('act', lambda e: e.activation(out=out_ap, in_=out_ap, func=AF.Exp, scale=-0.5), reads=wr, writes=wr)
        hnw_t = sb("hnw_t", [128, 128], F32)
        B_hnw = Buf("hnw")
        cx.dma(hnw_t[:], hnw.partition_broadcast(128), writes=[B_hnw])
        wba_f = sb("wba_f", [128, NCH, 16], F32)
        wba_b = sb("wba_b", [128, NCH, 16], BF16)
        B_wba = Buf("wba")
        cx.dma(wba_f[:], w_ba, writes=[B_wba])
        cx.op('dve', lambda e: e.tensor_tensor(out=wba_b[:], in0=wba_f[:],
                                               in1=normw_t[:].unsqueeze(2).to_broadcast([128, NCH, 16]), op=ALU.mult),
              reads=[B_wba, B_nw], writes=[B_wba])
        xnT = sb("xnT", [128, NCH, TM], BF16)
        B_xnT = [Buf(f"xnT{t}") for t in range(NT)]
        xh = sb("xh", [128, NCH, 4], BF16)
        B_xh = Buf("xh")
        st16 = sb("st16", [128, 64], F32)
        B_st = [Buf(f"st{i}") for i in range(4)]
        beta_all = sb("beta_all", [128, NT, 8], F32)
        g_all = sb("g_all", [128, NT, 8], F32)
        gc_all = sb("gc_all", [128, NT, 8], F32)
        egc_all = sb("egc_all", [128, NT, 8], F32)
        kds_all = sb("kds_all", [128, NT, 8], F32)
        edec_all = sb("edec_all", [128, NT, 8], F32)
        ngc_all = sb("ngc_all", [128, NT, 8], F32)
        B_tok = Buf("tokscal")
        S_f = sb("S_f", [128, NH, 128], F32)
        S_b = sb("S_b", [128, NH, 128], BF16)
        B_S = [Buf(f"S{h}") for h in range(NH)]
        cx.op('pool', lambda e: e.memset(S_f[:], 0.0), writes=B_S)
        cx.op('pool', lambda e: e.memset(S_b[:], 0.0), writes=B_S)
        cx.op('pool', lambda e: e.memset(xh[:], 0.0), writes=[B_xh])
        fl = lambda t: t[:].rearrange("p a b -> p (a b)")
        chk(1)

        def load_w(wr, src_ap, fold=True):
            i = wr.i
            wr.i += 1
            f, bfb = wr.f[i % len(wr.f)], wr.bf[i % len(wr.f)]
            w, bwb = wr.w[i % len(wr.w)], wr.bw[i % len(wr.w)]
            cx.dma(f[:], src_ap, writes=[bfb])
            eng = 'pool' if (i % 2 == 0) else 'dve'
            if fold:
                cx.op(eng, lambda e: e.tensor_tensor(out=w[:], in0=f[:],
                                                     in1=normw_t[:].unsqueeze(2).to_broadcast([128, NCH, 128]), op=ALU.mult),
                      reads=[bfb, B_nw], writes=[bwb])
            else:
                cx.op(eng, lambda e: e.tensor_copy(out=w[:], in_=f[:]), reads=[bfb], writes=[bwb])
            return w, bwb

        def inproj_fm(w, bw, dst_fn, bdst, func):
            for tb in range(NTB):
                ps, bps = bigR.next()
                for c in range(NCH):
                    cx.op('pe', lambda e: e.matmul(ps[:, 0:TB], lhsT=w[:, c, :], rhs=xnT[:, c, tb * TB:(tb + 1) * TB],
                                                   start=(c == 0), stop=(c == NCH - 1)),
                          reads=[bw] + B_xnT[tb * TB // 128:(tb + 1) * TB // 128], writes=[bps],
                          signal=(c == NCH - 1))
                cx.op('act', lambda e: e.activation(out=dst_fn(tb), in_=ps[:, 0:TB], func=func), reads=[bps], writes=[bdst])

        def phase(ph):
            main = (ph == 'M')
            tok0 = TM if main else 0
            with ExitStack() as sx:
                cx.cur = sx
                xin = [sb(f"xin{i}", [128, D], F32) for i in range(2)]
                B_xin = [Buf("xin0"), Buf("xin1")]
                xs = [sb(f"xs{i}", [128, D], BF16) for i in range(2)]
                B_xs = [Buf("xs0"), Buf("xs1")]
                def ba_tile(ti):
                    ps, bps = smR.next()
                    for c in range(NCH):
                        cx.op('pe', lambda e: e.matmul(ps[:, 0:16], lhsT=xnT[:, c, ti * 128:(ti + 1) * 128],
                                                       rhs=wba_b[:, c, :], start=(c == 0), stop=(c == NCH - 1)),
                              reads=[B_xnT[ti], B_wba], writes=[bps], signal=(c == NCH - 1))
                    cx.op('act', lambda e: e.activation(out=beta_all[:, ti, :], in_=ps[:, 0:8], func=AF.Exp, scale=-1.0),
                          reads=[bps], writes=[B_tok])
                    cx.op('dve', lambda e: e.tensor_copy(out=g_all[:, ti, :], in_=ps[:, 8:16]), reads=[bps], writes=[B_tok])

                for ti in range(NT):
                    if ti >= 1:
                        ba_tile(ti - 1)
                    xi, bxi = xin[ti % 2], B_xin[ti % 2]
                    xsi, bxs = xs[ti % 2], B_xs[ti % 2]
                    cx.dma(xi[:], xcat[tok0 + ti * 128: tok0 + (ti + 1) * 128, :], writes=[bxi])
                    ssq = st16[:, ti % 2: ti % 2 + 1]
                    rstd = st16[:, 2 + ti % 2: 3 + ti % 2]
                    bs = B_st[ti % 2]
                    cx.op('act', lambda e: e.activation(out=xsi[:], in_=xi[:], func=AF.Square, accum_out=ssq),
                          reads=[bxi], writes=[bxs, bs])
                    rsqrt(rstd, ssq, 1.0 / D, [bs], [bs])
                    cx.op('dve', lambda e: e.tensor_scalar(out=xsi[:], in0=xi[:], scalar1=rstd, scalar2=None, op0=ALU.mult),
                          reads=[bxi, bs], writes=[bxs])
                    for q4 in range(4):
                        pst, bpst = sm_full()
                        for j in range(4):
                            c = q4 * 4 + j
                            cx.op('pe', lambda e: e.matmul(pst[:, j, :], lhsT=xsi[:, c * 128:(c + 1) * 128], rhs=idb,
                                                           start=True, stop=True),
                                  reads=[bxs, B_cst], writes=[bpst], signal=(j == 3))
                        cx.op('dve' if q4 % 2 == 0 else 'act',
                              (lambda e: e.tensor_copy(out=xnT[:, q4 * 4:(q4 + 1) * 4, ti * 128:(ti + 1) * 128], in_=pst[:]))
                              if q4 % 2 == 0 else
                              (lambda e: e.activation(out=xnT[:, q4 * 4:(q4 + 1) * 4, ti * 128:(ti + 1) * 128], in_=pst[:], func=AF.Copy)),
                              reads=[bpst], writes=[B_xnT[ti]])
                ba_tile(NT - 1)
                cx.barrier()
                if not main:
                    chk(2)
            cx.cur = es
            TA = NT * 8
            cx.op('dve', lambda e: e.tensor_scalar(out=fl(beta_all), in0=fl(beta_all), scalar1=1.0, scalar2=None,
                                                   op0=ALU.add), reads=[B_tok], writes=[B_tok])
            cx.op('dve', lambda e: e.reciprocal(out=fl(beta_all), in_=fl(beta_all)), reads=[B_tok], writes=[B_tok])
            cx.op('dve', lambda e: e.tensor_tensor(out=g_all[:], in0=g_all[:],
                                                   in1=dtb_t.unsqueeze(1).to_broadcast([128, NT, 8]), op=ALU.add),
                  reads=[B_tok, B_small], writes=[B_tok])
            cx.op('act', lambda e: e.activation(out=fl(g_all), in_=fl(g_all), func=AF.Exp), reads=[B_tok], writes=[B_tok])
            cx.op('act', lambda e: e.activation(out=fl(g_all), in_=fl(g_all), func=AF.Ln, bias=one_c),
                  reads=[B_tok, B_small], writes=[B_tok])
            cx.op('dve', lambda e: e.tensor_tensor(out=g_all[:], in0=g_all[:],
                                                   in1=negA.unsqueeze(1).to_broadcast([128, NT, 8]), op=ALU.mult),
                  reads=[B_tok, B_small], writes=[B_tok])
            for a0 in range(0, TA, 128):
                a1 = min(TA, a0 + 128)
                ps, bps = smR.next()
                cx.op('pe', lambda e: e.matmul(ps[:, 0:a1 - a0], lhsT=mk(C_UP), rhs=fl(g_all)[:, a0:a1],
                                               start=True, stop=True), reads=[B_tok, B_cst], writes=[bps])
                cx.op('dve', lambda e: e.tensor_copy(out=fl(gc_all)[:, a0:a1], in_=ps[:, 0:a1 - a0]),
                      reads=[bps], writes=[B_tok])
                ps2, bps2 = smR.next()
                cx.op('pe', lambda e: e.matmul(ps2[:, 0:a1 - a0], lhsT=onesf, rhs=fl(g_all)[:, a0:a1],
                                               start=True, stop=True), reads=[B_tok, B_cst], writes=[bps2])
                cx.op('dve', lambda e: e.tensor_tensor(out=fl(kds_all)[:, a0:a1], in0=ps2[:, 0:a1 - a0],
                                                       in1=fl(gc_all)[:, a0:a1], op=ALU.subtract),
                      reads=[bps2, B_tok], writes=[B_tok])
                cx.op('dve', lambda e: e.tensor_copy(out=fl(edec_all)[:, a0:a1], in_=ps2[:, 0:a1 - a0]),
                      reads=[bps2, B_tok], writes=[B_tok])
                cx.op('act', lambda e: e.activation(out=fl(edec_all)[:, a0:a1], in_=fl(edec_all)[:, a0:a1], func=AF.Exp),
                      reads=[B_tok], writes=[B_tok])
            cx.op('act', lambda e: e.activation(out=fl(kds_all), in_=fl(kds_all), func=AF.Exp), reads=[B_tok], writes=[B_tok])
            cx.op('act', lambda e: e.activation(out=fl(egc_all), in_=fl(gc_all), func=AF.Exp), reads=[B_tok], writes=[B_tok])
            cx.op('dve', lambda e: e.tensor_scalar(out=fl(ngc_all), in0=fl(gc_all), scalar1=-1.0, scalar2=None, op0=ALU.mult),
                  reads=[B_tok], writes=[B_tok])
            if not main:
                chk(3)
            if dbg and main:
                cx.dma(dbg_out["d_beta"], beta_all[:], reads=[B_tok])
                cx.dma(dbg_out["d_g"], g_all[:], reads=[B_tok])
                with ExitStack() as sd:
                    cx.cur = sd
                    dtmp = sb("dbg_xnT", [128, NCH, TM], F32)
                    bt = Buf("dbgx")
                    cx.op('dve', lambda e: e.tensor_copy(out=dtmp[:], in_=xnT[:]), reads=B_xnT, writes=[bt])
                    cx.dma(dbg_out["d_xnT"], dtmp[:], reads=[bt])
                    cx.barrier()
                cx.cur = es

            with ExitStack() as shd:
                cx.cur = shd
                wr = WRing(cx, 2, 2)
                pre = [sb(f"pre{j}", [128, TM + 4], BF16) for j in range(3)]
                B_pre = [Buf(f"pre{j}") for j in range(3)]
                act_T = [sb(f"actT{j}", [128, TM], BF16) for j in range(3)]
                B_act = [Buf(f"actT{j}") for j in range(3)]
                zT = sb("zT", [128, TM], BF16)
                B_zT = Buf("zT")
                zs = sb("zs", [128, NT, 128], BF16)
                B_zs = Buf("zs")
                sq = [sb(f"sq{j}", [128, TM], BF16) for j in range(1)]
                B_sq = [Buf("sq0")]
                diag = sb("diag", [128, 12, 128], BF16)
                B_diag = Buf("diag")
                hs = sb("hs", [128, 8, NT], F32)
                B_hs = Buf("hs")
                oTs = [sb(f"oTs{i}", [128, TM], BF16) for i in range(2)]
                B_oTs = [Buf("oTs0"), Buf("oTs1")]
                NTMP = GCH

                def tmpset(nm, dt, n=NTMP):
                    ts = [sb(f"{nm}{i}", [128, 128], dt) for i in range(n)]
                    return ts, [Buf(f"{nm}{i}") for i in range(n)]
                t_dg, B_dg = tmpset("dg", F32)
                t_El, B_El = tmpset("El", F32)
                t_Eu, B_Eu = tmpset("Eu", F32)
                t_N, B_N = tmpset("N", BF16)
                t_NT, B_NT = tmpset("NT", BF16)
                t_N16, B_N16 = tmpset("N16", BF16)
                t_N16T, B_N16T = tmpset("N16T", BF16)
                def pairset(nm, n):
                    ts = [sb(f"{nm}{i}", [128, 2, 128], BF16) for i in range(n)]
                    return ts, [Buf(f"{nm}{i}") for i in range(n)]
                t_PP, B_PP = pairset("PP", 2 * GCH)
                t_TU, B_TU = pairset("TU", 2 * GCH)
                t_XX, B_XX = pairset("XX", GCH)
                t_r, B_r = tmpset("r", BF16, 2)
                t_vn, B_vn = tmpset("vn", BF16, 2)
                t_o1, B_o1 = tmpset("o1", F32, 2)
                t_so1, B_so1 = tmpset("so1", F32, 2)
                r_op = sb("r_op", [128, min(NT, RRES), 128], F32)
                B_rop = [Buf(f"rop{t}") for t in range(NT)]
                t_og, B_og = tmpset("og", BF16, 2)
                t_junk, B_junk = tmpset("junk", BF16, 2)
                t_ms = sb("t_ms", [128, 2], F32)
                B_ms = [Buf("ms0"), Buf("ms1")]
                hn = sb("hn", [128, 4, NT], F32)
                B_hn = Buf("hn")
                r_U = sb("r_U", [128, min(NT, RRES), 128], BF16)
                r_at = sb("r_at", [128, min(NT, RRES), 128], BF16)
                r_kd = sb("r_kd", [128, min(NT, RRES), 128], BF16)
                r_vb = sb("r_vb", [128, min(NT, RRES), 128], BF16)
                B_rU = [Buf(f"rU{t}") for t in range(NT)]
                B_rat = [Buf(f"rat{t}") for t in range(NT)]
                B_rkd = [Buf(f"rkd{t}") for t in range(NT)]
                B_rvb = [Buf(f"rvb{t}") for t in range(NT)]
                HS = lambda i: hs[:, i, :]

                def head_groups(h):
                    return ([(0, h), (1, 8 + h), (2, 16 + h)] if main else [(1, 8 + h), (2, 16 + h)])

                def inproj_gen(h):
                    for j, g in head_groups(h) + ([(3, 24 + h)] if main else []):
                        w, bw = load_w(wr, w_fm[g])
                        dst, bdst = (pre[j], B_pre[j]) if j < 3 else (zT, B_zT)
                        off = 3 if j < 3 else 0
                        yield
                        for tb in range(NTB):
                            ps, bps = bigR.next()
                            for c in range(NCH):
                                cx.op('pe', lambda e: e.matmul(ps[:, 0:TB], lhsT=w[:, c, :], rhs=xnT[:, c, tb * TB:(tb + 1) * TB],
                                                               start=(c == 0), stop=(c == NCH - 1)),
                                      reads=[bw] + B_xnT[tb * TB // 128:(tb + 1) * TB // 128], writes=[bps],
                                      signal=(c == NCH - 1))
                                if c in (3, 7, 11):
                                    yield
                            cx.op('act', lambda e: e.activation(out=dst[:, off + tb * TB: off + (tb + 1) * TB], in_=ps[:, 0:TB],
                                                                func=AF.Copy), reads=[bps], writes=[bdst])
                            yield
                        if j < 3:
                            ps, bps = smR.next()
                            for c in range(NCH):
                                cx.op('pe', lambda e: e.matmul(ps[:, 0:3], lhsT=w[:, c, :], rhs=xh[:, c, 0:3],
                                                               start=(c == 0), stop=(c == NCH - 1)),
                                      reads=[bw, B_xh], writes=[bps], signal=(c == NCH - 1))
                            cx.op('act', lambda e: e.activation(out=dst[:, 0:3], in_=ps[:, 0:3], func=AF.Copy),
                                  reads=[bps], writes=[bdst])
                            yield

                for _ in inproj_gen(0):
                    pass
                for h in range(NH):
                    groups = head_groups(h)
                    for j, g in groups:
                        for k in range(4):
                            cx.op('pool', lambda e: e.tensor_scalar(out=diag[:, j * 4 + k, :], in0=idf,
                                                                    scalar1=convw_t[:, g, k:k + 1], scalar2=None,
                                                                    op0=ALU.mult),
                                  reads=[B_cst, B_cw], writes=[B_diag])
                    for j, g in groups:
                        for tb in range(NTB):
                            ps, bps = bigR.next()
                            for k in range(4):
                                cx.op('pe', lambda e: e.matmul(ps[:, 0:TB], lhsT=diag[:, j * 4 + k, :],
                                                               rhs=pre[j][:, tb * TB + k: tb * TB + k + TB],
                                                               start=(k == 0), stop=(k == 3)),
                                      reads=[B_diag, B_pre[j]], writes=[bps], signal=(k == 3))
                            cx.op('act', lambda e: e.activation(out=act_T[j][:, tb * TB:(tb + 1) * TB], in_=ps[:, 0:TB],
                                                                func=AF.Silu), reads=[bps], writes=[B_act[j]])
                    if not main and h == 0:
                        chk(4)
                    if dbg and main and h == 0:
                        for j, nm in enumerate(["d_q", "d_k", "d_v"]):
                            tmpf = sb(f"dbgf{j}", [128, TM], F32)
                            bt = Buf("dbgf")
                            cx.op('dve', lambda e: e.tensor_copy(out=tmpf[:], in_=act_T[j][:]), reads=[B_act[j]], writes=[bt])
                            cx.dma(dbg_out[nm], tmpf[:], reads=[bt])
                    for j in ([0, 1] if main else [1]):
                        s_t, bs_t = sq[0], B_sq[0]
                        cx.op('pool', lambda e: e.tensor_tensor(out=s_t[:], in0=act_T[j][:], in1=act_T[j][:], op=ALU.mult),
                              reads=[B_act[j]], writes=[bs_t])
                        ps, bps = smR.next()
                        for ti in range(NT):
                            cx.op('pe', lambda e: e.matmul(ps[:, ti:ti + 1], lhsT=s_t[:, ti * 128:(ti + 1) * 128],
                                                           rhs=onesb[:, 0:1], start=True, stop=True),
                                  reads=[bs_t, B_cst], writes=[bps], signal=(ti == NT - 1))
                        rsqrt(HS(j), ps[:, 0:NT], 1.0, [bps], [B_hs])
                    bh = beta_all[:, :, h]
                    cx.op('dve', lambda e: e.tensor_tensor(out=HS(3), in0=HS(1), in1=bh, op=ALU.mult),
                          reads=[B_hs, B_tok], writes=[B_hs])
                    cx.op('dve', lambda e: e.tensor_tensor(out=HS(7), in0=HS(3), in1=HS(1), op=ALU.mult),
                          reads=[B_hs], writes=[B_hs])
                    cx.op('dve', lambda e: e.tensor_tensor(out=HS(2), in0=HS(7), in1=egc_all[:, :, h], op=ALU.mult),
                          reads=[B_hs, B_tok], writes=[B_hs])
                    cx.op('dve', lambda e: e.reciprocal(out=HS(4), in_=HS(1)), reads=[B_hs], writes=[B_hs])
                    cx.op('dve', lambda e: e.tensor_tensor(out=HS(5), in0=HS(1), in1=kds_all[:, :, h], op=ALU.mult),
                          reads=[B_hs, B_tok], writes=[B_hs])
                    if main:
                        cx.op('dve', lambda e: e.tensor_scalar(out=HS(0), in0=HS(0), scalar1=float(128 ** -0.5), scalar2=None,
                                                               op0=ALU.mult), reads=[B_hs], writes=[B_hs])
                        for ti in range(NT):
                            tp, btp = smR.next()
                            cx.op('pe', lambda e: e.matmul(tp, lhsT=zT[:, ti * 128:(ti + 1) * 128], rhs=idb, start=True, stop=True),
                                  reads=[B_zT, B_cst], writes=[btp])
                            cx.op('act', lambda e: e.activation(out=zs[:, ti, :], in_=tp, func=AF.Silu),
                                  reads=[btp], writes=[B_zs])
                        cx.op('pool', lambda e: e.tensor_tensor(out=zs[:], in0=zs[:],
                                                                in1=hnw_t[:].unsqueeze(1).to_broadcast([128, NT, 128]), op=ALU.mult),
                              reads=[B_zs, B_hnw], writes=[B_zs])
                        cx.op('dve', lambda e: e.scalar_tensor_tensor(out=hn[:, 2, :], in0=HS(0), scalar=1.0 / 128, in1=HS(0),
                                                                      op0=ALU.mult, op1=ALU.mult), reads=[B_hs], writes=[B_hn])
                        cx.op('act', lambda e: e.activation(out=hn[:, 3, :], in_=HS(0), func=AF.Ln), reads=[B_hs], writes=[B_hn])
                    ot, bot = oTs[h % 2], B_oTs[h % 2]
                    if not main and h == 0:
                        chk(51)
                    cx.op('dve', lambda e: e.tensor_scalar(out=hn[:, 0, :], in0=HS(7), scalar1=-1.0, scalar2=None, op0=ALU.mult),
                          reads=[B_hs], writes=[B_hn])
                    cx.op('dve', lambda e: e.tensor_scalar(out=hn[:, 1, :], in0=HS(2), scalar1=-1.0, scalar2=None, op0=ALU.mult),
                          reads=[B_hs], writes=[B_hn])

                    def mm(lhsT, blh, rhs, brh):
                        ps, bps = smR.next()
                        cx.op('pe', lambda e: e.matmul(ps, lhsT=lhsT, rhs=rhs, start=True, stop=True),
                              reads=[blh, brh], writes=[bps])
                        return ps, bps

                    def evac_copy(dst, bdst, ps, bps):
                        cx.op('act', lambda e: e.activation(out=dst, in_=ps, func=AF.Copy), reads=[bps], writes=[bdst])

                    def evac_add(dst, bdst, ps, bps, addend, badd):
                        cx.op('dve', lambda e: e.scalar_tensor_tensor(out=dst, in0=ps, scalar=1.0, in1=addend,
                                                                      op0=ALU.mult, op1=ALU.add),
                              reads=[bps, badd], writes=[bdst])

                    def evac_mask(dst, bdst, ps, bps, mi):
                        cx.op('dve', lambda e: e.scalar_tensor_tensor(out=dst, in0=ps, scalar=1.0, in1=mk(mi),
                                                                      op0=ALU.mult, op1=ALU.mult),
                              reads=[bps, B_cst], writes=[bdst])

                    def prep_gen(ti, pp, h=h):
                        sl = slice(ti * 128, (ti + 1) * 128)
                        qT_t, kT_t, vT_t = act_T[0][:, sl], act_T[1][:, sl], act_T[2][:, sl]
                        col = lambda i: hs[:, i, ti:ti + 1]
                        gcc = gc_all[:, ti, h:h + 1]
                        t_ad_, B_ad_ = t_dg[pp], B_dg[pp]
                        t_E_, B_E_ = t_dg[pp], B_dg[pp]
                        cx.op('pool', lambda e: e.tensor_scalar(out=t_dg[pp][:], in0=idf, scalar1=gcc, scalar2=None, op0=ALU.mult),
                              reads=[B_cst, B_tok], writes=[B_dg[pp]])
                        yield
                        psG, bG = smR.next()
                        cx.op('pe', lambda e: e.matmul(psG, lhsT=onesf, rhs=t_dg[pp][:], start=True, stop=True),
                              reads=[B_cst, B_dg[pp]], writes=[bG])
                        cx.op('act', lambda e: e.activation(out=t_ad_[:], in_=psG, func=AF.Abs, bias=ngc_all[:, ti, h:h + 1]),
                              reads=[bG, B_tok], writes=[B_ad_])
                        cx.op('dve', lambda e: e.tensor_tensor(out=t_El[pp][:], in0=t_ad_[:], in1=mk(C_PLOW), op=ALU.add),
                              reads=[B_ad_, B_cst], writes=[B_El[pp]])
                        cx.op('act', lambda e: e.activation(out=t_El[pp][:], in_=t_El[pp][:], func=AF.Exp, scale=-1.0),
                              reads=[B_El[pp]], writes=[B_El[pp]])
                        if main:
                            cx.op('dve', lambda e: e.tensor_tensor(out=t_Eu[pp][:], in0=t_ad_[:], in1=mk(C_PUP), op=ALU.add),
                                  reads=[B_ad_, B_cst], writes=[B_Eu[pp]])
                            cx.op('act', lambda e: e.activation(out=t_Eu[pp][:], in_=t_Eu[pp][:], func=AF.Exp, scale=-1.0),
                                  reads=[B_Eu[pp]], writes=[B_Eu[pp]])
                        psKK, bKK = smR.next()
                        cx.op('pe', lambda e: e.matmul(psKK, lhsT=kT_t, rhs=kT_t, start=True, stop=True),
                              reads=[B_act[1]], writes=[bKK])
                        cx.op('dve', lambda e: e.scalar_tensor_tensor(out=t_N[pp][:], in0=psKK, scalar=hn[:, 0, ti:ti + 1], in1=t_El[pp][:],
                                                                      op0=ALU.mult, op1=ALU.mult),
                              reads=[bKK, B_hn, B_El[pp]], writes=[B_N[pp]])
                        cx.op('pool', lambda e: e.tensor_tensor(out=t_N16[pp][:], in0=t_N[pp][:], in1=mk(C_BD16), op=ALU.mult),
                              reads=[B_N[pp], B_cst], writes=[B_N16[pp]])
                        if main:
                            psQK, bQK = smR.next()
                            cx.op('pe', lambda e: e.matmul(psQK, lhsT=kT_t, rhs=qT_t, start=True, stop=True),
                                  reads=[B_act[1], B_act[0]], writes=[bQK])
                            cx.op('dve', lambda e: e.scalar_tensor_tensor(out=r_at[:, ti % RRES, :], in0=psQK, scalar=col(1), in1=t_Eu[pp][:],
                                                                          op0=ALU.mult, op1=ALU.mult),
                                  reads=[bQK, B_hs, B_Eu[pp]], writes=[B_rat[ti % RRES]])
                        yield
                        tpN, btpN = smR.next()
                        cx.op('pe', lambda e: e.matmul(tpN, lhsT=t_N[pp][:], rhs=idb, start=True, stop=True),
                              reads=[B_N[pp], B_cst], writes=[btpN])
                        cx.op('act', lambda e: e.activation(out=t_NT[pp][:], in_=tpN, func=AF.Copy), reads=[btpN], writes=[B_NT[pp]])
                        cx.op('pool', lambda e: e.tensor_tensor(out=t_N16T[pp][:], in0=t_NT[pp][:], in1=mk(C_BD16), op=ALU.mult),
                              reads=[B_NT[pp], B_cst], writes=[B_N16T[pp]])
                        PP = lambda i: (t_PP[pp * 2 + i], B_PP[pp * 2 + i])
                        TU = lambda i: (t_TU[pp * 2 + i], B_TU[pp * 2 + i])
                        XX = (t_XX[pp], B_XX[pp])
                        tu0, btu0 = TU(0)
                        cx.op('pool', lambda e: e.tensor_tensor(out=tu0[:, 0, :], in0=t_N16[pp][:], in1=idf, op=ALU.add),
                              reads=[B_N16[pp], B_cst], writes=[btu0])
                        cx.op('pool', lambda e: e.tensor_tensor(out=tu0[:, 1, :], in0=t_N16T[pp][:], in1=idf, op=ALU.add),
                              reads=[B_N16T[pp], B_cst], writes=[btu0])
                        Pk, PkT, bPk = t_N16[pp][:], t_N16T[pp][:], [B_N16[pp], B_N16T[pp]]
                        cur = 0
                        for lvl in range(3):
                            last = (lvl == 2)
                            npp, bnpp = PP(lvl % 2)
                            yield
                            pst, bpst = sm_full()
                            cx.op('pe', lambda e: e.matmul(pst[:, 0, :], lhsT=PkT, rhs=Pk, start=True, stop=True),
                                  reads=bPk, writes=[bpst], signal=last)
                            if not last:
                                cx.op('pe', lambda e: e.matmul(pst[:, 1, :], lhsT=Pk, rhs=PkT, start=True, stop=True),
                                      reads=bPk, writes=[bpst])
                                cx.op('act', lambda e: e.activation(out=npp[:, 0:2, :], in_=pst[:, 0:2, :], func=AF.Copy),
                                      reads=[bpst], writes=[bnpp])
                            else:
                                cx.op('act', lambda e: e.activation(out=npp[:, 0, :], in_=pst[:, 0, :], func=AF.Copy),
                                      reads=[bpst], writes=[bnpp])
                            t0, bt0 = TU(cur)
                            t1, bt1 = TU(1 - cur)
                            yield
                            pst, bpst = sm_full()
                            cx.op('pe', lambda e: e.matmul(pst[:, 0, :], lhsT=t0[:, 1, :], rhs=npp[:, 0, :], start=True, stop=True),
                                  reads=[bt0, bnpp], writes=[bpst], signal=False)
                            cx.op('pe', lambda e: e.matmul(pst[:, 1, :], lhsT=npp[:, 0, :], rhs=t0[:, 1, :], start=True, stop=True),
                                  reads=[bt0, bnpp], writes=[bpst])
                            cx.op('dve', lambda e: e.scalar_tensor_tensor(out=t1[:, 0:2, :], in0=pst[:, 0:2, :], scalar=1.0, in1=t0[:, 0:2, :],
                                                                          op0=ALU.mult, op1=ALU.add),
                                  reads=[bpst, bt0], writes=[bt1])
                            cur = 1 - cur
                            if not last:
                                Pk, PkT, bPk = npp[:, 0, :], npp[:, 1, :], [bnpp]
                        for mi_, lastm in [(C_C32, False), (C_C64, False), (C_C128, True)]:
                            t0, bt0 = TU(cur)
                            t1, bt1 = TU(1 - cur)
                            xx, bxx = XX
                            yield
                            pst, bpst = sm_full()
                            if not lastm:
                                cx.op('pe', lambda e: e.matmul(pst[:, 0, :], lhsT=t_NT[pp][:], rhs=t0[:, 0, :], start=True, stop=True),
                                      reads=[B_NT[pp], bt0], writes=[bpst], signal=False)
                            cx.op('pe', lambda e: e.matmul(pst[:, 1, :], lhsT=t_N[pp][:], rhs=t0[:, 1, :], start=True, stop=True),
                                  reads=[B_N[pp], bt0], writes=[bpst])
                            if not lastm:
                                cx.op('dve', lambda e: e.scalar_tensor_tensor(out=xx[:, 0:2, :], in0=pst[:, 0:2, :], scalar=1.0,
                                                                              in1=mk(mi_).unsqueeze(1).to_broadcast([128, 2, 128]),
                                                                              op0=ALU.mult, op1=ALU.mult),
                                      reads=[bpst, B_cst], writes=[bxx])
                            else:
                                cx.op('dve', lambda e: e.scalar_tensor_tensor(out=xx[:, 1, :], in0=pst[:, 1, :], scalar=1.0, in1=mk(mi_),
                                                                              op0=ALU.mult, op1=ALU.mult),
                                      reads=[bpst, B_cst], writes=[bxx])
                            yield
                            pst, bpst = sm_full()
                            if not lastm:
                                cx.op('pe', lambda e: e.matmul(pst[:, 0, :], lhsT=t0[:, 1, :], rhs=xx[:, 0, :], start=True, stop=True),
                                      reads=[bt0, bxx], writes=[bpst], signal=False)
                            cx.op('pe', lambda e: e.matmul(pst[:, 1, :], lhsT=t0[:, 0, :], rhs=xx[:, 1, :], start=True, stop=True),
                                  reads=[bt0, bxx], writes=[bpst])
                            if not lastm:
                                cx.op('dve', lambda e: e.scalar_tensor_tensor(out=t1[:, 0:2, :], in0=pst[:, 0:2, :], scalar=1.0, in1=t0[:, 0:2, :],
                                                                              op0=ALU.mult, op1=ALU.add),
                                      reads=[bpst, bt0], writes=[bt1])
                            else:
                                cx.op('dve', lambda e: e.scalar_tensor_tensor(out=r_U[:, ti % RRES, :], in0=pst[:, 1, :], scalar=1.0, in1=t0[:, 1, :],
                                                                              op0=ALU.mult, op1=ALU.add),
                                      reads=[bpst, bt0], writes=[B_rU[ti % RRES]])
                            cur = 1 - cur
                        tpk, btpk = smR.next()
                        cx.op('pe', lambda e: e.matmul(tpk, lhsT=kT_t, rhs=idb, start=True, stop=True), reads=[B_act[1], B_cst], writes=[btpk])
                        cx.op('act', lambda e: e.activation(out=r_kd[:, ti % RRES, :], in_=tpk, func=AF.Copy, scale=col(5)),
                              reads=[btpk, B_hs], writes=[B_rkd[ti % RRES]])
                        tpv, btpv = smR.next()
                        cx.op('pe', lambda e: e.matmul(tpv, lhsT=vT_t, rhs=idb, start=True, stop=True), reads=[B_act[2], B_cst], writes=[btpv])
                        cx.op('act', lambda e: e.activation(out=r_vb[:, ti % RRES, :], in_=tpv, func=AF.Copy, scale=col(3)),
                              reads=[btpv, B_hs], writes=[B_rvb[ti % RRES]])

                    def state_gen(ti, h=h):
                        pp = ti % 2
                        sl = slice(ti * 128, (ti + 1) * 128)
                        qT_t, kT_t = act_T[0][:, sl], act_T[1][:, sl]
                        col = lambda i: hs[:, i, ti:ti + 1]
                        Sb_h = S_b[:, h, :]
                        Sf_h = S_f[:, h, :]
                        psA, bA = mm(kT_t, B_act[1], Sb_h, B_S[h])
                        cx.op('dve', lambda e: e.scalar_tensor_tensor(out=t_r[pp][:], in0=psA, scalar=hn[:, 1, ti:ti + 1], in1=r_vb[:, ti % RRES, :],
                                                                      op0=ALU.mult, op1=ALU.add),
                              reads=[bA, B_hn, B_rvb[ti % RRES]], writes=[B_r[pp]])
                        if main:
                            psO1, bO1 = mm(qT_t, B_act[0], Sb_h, B_S[h])
                            cx.op('act', lambda e: e.activation(out=t_so1[pp][:], in_=psO1, func=AF.Copy,
                                                                scale=egc_all[:, ti, h:h + 1]),
                                  reads=[bO1, B_tok], writes=[B_so1[pp]])
                        yield
                        psB, bB = mm(r_U[:, ti % RRES, :], B_rU[ti % RRES], t_r[pp][:], B_r[pp])
                        cx.op('act', lambda e: e.activation(out=t_vn[pp][:], in_=psB, func=AF.Copy, scale=col(4)),
                              reads=[bB, B_hs], writes=[B_vn[pp]])
                        yield
                        psS, bS = mm(r_kd[:, ti % RRES, :], B_rkd[ti % RRES], t_vn[pp][:], B_vn[pp])
                        cx.op('dve', lambda e: e.scalar_tensor_tensor(out=Sf_h, in0=Sf_h, scalar=edec_all[:, ti, h:h + 1], in1=psS,
                                                                      op0=ALU.mult, op1=ALU.add),
                              reads=[bS, B_tok, B_S[h]], writes=[B_S[h]])
                        cx.op('act', lambda e: e.activation(out=Sb_h, in_=Sf_h, func=AF.Copy), reads=[B_S[h]], writes=[B_S[h]])
                        if main:
                            psO2, bO2 = mm(r_at[:, ti % RRES, :], B_rat[ti % RRES], t_vn[pp][:], B_vn[pp])
                            cx.op('dve', lambda e: e.scalar_tensor_tensor(out=r_op[:, ti % RRES, :], in0=psO2, scalar=1.0, in1=t_so1[pp][:],
                                                                          op0=ALU.mult, op1=ALU.add),
                                  reads=[bO2, B_so1[pp]], writes=[B_rop[ti % RRES]])

                    def post_gen(ti, h=h):
                        pp = ti % 2
                        sl = slice(ti * 128, (ti + 1) * 128)
                        opre, bopre = r_op[:, ti % RRES, :], B_rop[ti % RRES]
                        ms = t_ms[:, pp:pp + 1]
                        cx.op('act', lambda e: e.activation(out=t_junk[pp][:], in_=opre, func=AF.Square, accum_out=ms),
                              reads=[bopre], writes=[B_junk[pp], B_ms[pp]])
                        cx.op('act', lambda e: e.activation(out=ms, in_=ms, func=AF.Ln, scale=hn[:, 2, ti:ti + 1], bias=eps_c),
                              reads=[B_ms[pp], B_hn, B_small], writes=[B_ms[pp]])
                        cx.op('act', lambda e: e.activation(out=ms, in_=ms, func=AF.Exp, scale=-0.5, bias=hn[:, 3, ti:ti + 1]),
                              reads=[B_ms[pp], B_hn], writes=[B_ms[pp]])
                        yield
                        cx.op('pool', lambda e: e.tensor_scalar(out=t_o1[pp][:], in0=opre, scalar1=ms, scalar2=None, op0=ALU.mult),
                              reads=[bopre, B_ms[pp]], writes=[B_o1[pp]])
                        cx.op('pool', lambda e: e.tensor_tensor(out=t_og[pp][:], in0=t_o1[pp][:], in1=zs[:, ti, :], op=ALU.mult),
                              reads=[B_o1[pp], B_zs], writes=[B_og[pp]])
                        yield
                        tpo, btpo = smR.next()
                        cx.op('pe', lambda e: e.matmul(tpo, lhsT=t_og[pp][:], rhs=idb, start=True, stop=True),
                              reads=[B_og[pp], B_cst], writes=[btpo])
                        cx.op('act', lambda e: e.activation(out=ot[:, sl], in_=tpo, func=AF.Copy), reads=[btpo], writes=[bot])

                    active = []
                    inproj_chain = [['inproj', -1, -1, inproj_gen(h + 1)]] if h + 1 < NH else []
                    free_slots = list(range(GCH))
                    next_prep, next_state, state_on = 0, 0, False
                    next_post, post_on = (0 if main else NT), False
                    prep_done = set()
                    while next_state < NT or next_post < NT or active or inproj_chain:
                        if (not post_on) and next_post < NT and next_post < next_state:
                            active.append(['post', next_post, -1, post_gen(next_post)])
                            post_on = True
                        while free_slots and next_prep < NT and next_prep < min(next_state, next_post if main else next_state) + RRES:
                            slot = free_slots.pop(0)
                            active.append(['prep', next_prep, slot, prep_gen(next_prep, slot)])
                            next_prep += 1
                        if (not state_on) and next_state < NT and next_state in prep_done:
                            active.append(['state', next_state, -1, state_gen(next_state)])
                            state_on = True
                        for a in list(active) + list(inproj_chain):
                            try:
                                next(a[3])
                            except StopIteration:
                                if a[0] == 'inproj':
                                    inproj_chain.remove(a)
                                    continue
                                active.remove(a)
                                if a[0] == 'prep':
                                    prep_done.add(a[1])
                                    free_slots.append(a[2])
                                elif a[0] == 'state':
                                    state_on = False
                                    next_state += 1
                                elif a[0] == 'post':
                                    post_on = False
                                    next_post += 1
                    if main:
                        cx.dma(oT_d[h], ot[:], reads=[bot], writes=[B_oTd[h]])
                if not main:
                    cx.op('pool', lambda e: e.tensor_copy(out=xh[:, :, 0:3], in_=xnT[:, :, TM - 3:TM]),
                          reads=[B_xnT[NT - 1]], writes=[B_xh])
                if dbg and main:
                    cx.dma(dbg_out["d_S"], S_f[:], reads=B_S)
                cx.barrier()
            cx.cur = es

        phase('P')
        chk(6)
        phase('M')
        chk(7)

        with ExitStack() as sB:
            cx.cur = sB
            vn_all = sb("vn_all", [128, NT, 1024], BF16)
            B_vnall = [Buf(f"vnall{t}") for t in range(NT)]
            with ExitStack() as sB1:
                cx.cur = sB1
                wvb_b = sb("wvb_b", [128, NCH, 1024], BF16)
                B_wvb = Buf("wvb")
                stg = [sb(f"stgB{i}", [128, NCH, 128], F32) for i in range(2)]
                B_stg = [Buf("stgB0"), Buf("stgB1")]
                for c4 in range(8):
                    f, bfb = stg[c4 % 2], B_stg[c4 % 2]
                    cx.dma(f[:], w_vb[:, :, c4 * 128:(c4 + 1) * 128], writes=[bfb])
                    cx.op('pool' if c4 % 2 == 0 else 'dve',
                          lambda e: e.tensor_tensor(out=wvb_b[:, :, c4 * 128:(c4 + 1) * 128], in0=f[:],
                                                    in1=normw_t[:].unsqueeze(2).to_broadcast([128, NCH, 128]), op=ALU.mult),
                          reads=[bfb, B_nw], writes=[B_wvb])
                lnw_t = sb("lnw_t", [128, 1024], F32)
                lnb_t = sb("lnb_t", [128, 1024], F32)
                B_ln = Buf("ln")
                cx.dma(lnw_t[:], lnw.partition_broadcast(128), writes=[B_ln])
                cx.dma(lnb_t[:], lnb.partition_broadcast(128), writes=[B_ln])
                vtmp = [sb(f"vtmp{i}", [128, 1024], F32) for i in range(2)]
                B_vtmp = [Buf("vtmp0"), Buf("vtmp1")]
                for ti in range(NT):
                    pss = []
                    for hf in range(2):
                        ps, bps = bigR.next()
                        for c in range(NCH):
                            cx.op('pe', lambda e: e.matmul(ps[:, :], lhsT=xnT[:, c, ti * 128:(ti + 1) * 128],
                                                           rhs=wvb_b[:, c, hf * 512:(hf + 1) * 512],
                                                           start=(c == 0), stop=(c == NCH - 1)),
                                  reads=[B_xnT[ti], B_wvb], writes=[bps], signal=(c == NCH - 1))
                        pss.append((ps, bps))
                    vt, bvt = vtmp[ti % 2], B_vtmp[ti % 2]
                    bs = B_st[2 + ti % 2]
                    s0 = 8 + (ti % 2) * 8
                    for hf in range(2):
                        ps, bps = pss[hf]
                        cx.op('act', lambda e: e.activation(out=vt[:, hf * 512:(hf + 1) * 512], in_=ps[:, :], func=AF.Copy,
                                                            accum_out=st16[:, s0 + hf:s0 + hf + 1]),
                              reads=[bps], writes=[bvt, bs])
                        cx.op('act', lambda e: e.activation(out=vn_all[:, ti, hf * 512:(hf + 1) * 512], in_=ps[:, :], func=AF.Square,
                                                            accum_out=st16[:, s0 + 2 + hf:s0 + 3 + hf]),
                              reads=[bps], writes=[B_vnall[ti], bs])
                    mean = st16[:, s0 + 4:s0 + 5]
                    var = st16[:, s0 + 5:s0 + 6]
                    m2 = st16[:, s0 + 6:s0 + 7]
                    cx.op('dve', lambda e: e.tensor_scalar(out=mean, in0=st16[:, s0:s0 + 1], scalar1=st16[:, s0 + 1:s0 + 2],
                                                           scalar2=1.0 / 1024, op0=ALU.add, op1=ALU.mult), reads=[bs], writes=[bs])
                    cx.op('dve', lambda e: e.tensor_scalar(out=var, in0=st16[:, s0 + 2:s0 + 3], scalar1=st16[:, s0 + 3:s0 + 4],
                                                           scalar2=1.0 / 1024, op0=ALU.add, op1=ALU.mult), reads=[bs], writes=[bs])
                    cx.op('dve', lambda e: e.tensor_tensor(out=m2, in0=mean, in1=mean, op=ALU.mult), reads=[bs], writes=[bs])
                    cx.op('dve', lambda e: e.tensor_tensor(out=var, in0=var, in1=m2, op=ALU.subtract), reads=[bs], writes=[bs])
                    rsqrt(var, var, 1.0, [bs], [bs])
                    cx.op('dve', lambda e: e.tensor_scalar(out=vt[:], in0=vt[:], scalar1=mean, scalar2=var,
                                                           op0=ALU.subtract, op1=ALU.mult), reads=[bvt, bs], writes=[bvt])
                    cx.op('pool', lambda e: e.tensor_tensor(out=vt[:], in0=vt[:], in1=lnw_t[:], op=ALU.mult),
                          reads=[bvt, B_ln], writes=[bvt])
                    cx.op('pool', lambda e: e.tensor_tensor(out=vn_all[:, ti, :], in0=vt[:], in1=lnb_t[:], op=ALU.add),
                          reads=[bvt, B_ln], writes=[B_vnall[ti]])
                cx.barrier()
            cx.cur = sB
            wr = WRing(cx, 2, 4)
            bsp_t = sb("bsp_t", [128, 8, 128], F32)
            B_bsp = Buf("bsp")
            cx.dma(bsp_t[:].rearrange("p a b -> p (a b)"), bsp.partition_broadcast(128), writes=[B_bsp])
            wsp_f = sb("wsp_f", [128, 8, 128], F32)
            wsp_m = sb("wsp_m", [128, 8, 128], BF16)
            Rsp = sb("Rsp", [128, 8, 128], BF16)
            B_wsp = Buf("wsp")
            B_R = Buf("Rsp")
            cx.dma(wsp_f[:], wsp, writes=[B_wsp])
            cx.op('dve', lambda e: e.tensor_tensor(out=wsp_m[:], in0=wsp_f[:],
                                                   in1=cst_f[:, C_TRIL:C_TRIL + 1, :].to_broadcast([128, 8, 128]), op=ALU.mult),
                  reads=[B_wsp, B_cst], writes=[B_wsp])
            for g in range(8):
                tp, btp = smR.next()
                cx.op('pe', lambda e: e.matmul(tp, lhsT=wsp_m[:, g, :], rhs=idb, start=True, stop=True),
                      reads=[B_wsp, B_cst], writes=[btp])
                cx.op('act', lambda e: e.activation(out=Rsp[:, g, :], in_=tp, func=AF.Copy), reads=[btp], writes=[B_R])
            uT = sb("uT", [128, TM], BF16)
            szT = sb("szT", [128, TM], BF16)
            buT, bszT = Buf("uT"), Buf("szT")
            oTs = [sb(f"oTsB{i}", [128, TM], BF16) for i in range(2)]
            B_oTs = [Buf("oTsB0"), Buf("oTsB1")]
            tA = [sb(f"tA{i}", [128, 512], F32) for i in range(2)]
            tBb = [sb(f"tB{i}", [128, 512], F32) for i in range(2)]
            B_tA = [Buf("tA0"), Buf("tA1")]
            B_tB = [Buf("tB0"), Buf("tB1")]
            nxt = (load_w(wr, w_fm[32]), load_w(wr, w_fm[40]))
            for g in range(8):
                (wu, bwu), (wz, bwz) = nxt
                if g + 1 < 8:
                    nxt = (load_w(wr, w_fm[32 + g + 1]), load_w(wr, w_fm[40 + g + 1]))
                inproj_fm(wu, bwu, lambda tb: uT[:, tb * TB:(tb + 1) * TB], buT, AF.Copy)
                inproj_fm(wz, bwz, lambda tb: szT[:, tb * TB:(tb + 1) * TB], bszT, AF.Silu)
                ot, bot = oTs[g % 2], B_oTs[g % 2]
                NB4 = min(4, NT)
                for t4 in range(NT // NB4):
                    pst, bpst = sm_full()
                    for j in range(NB4):
                        ti = t4 * NB4 + j
                        cx.op('pe', lambda e: e.matmul(pst[:, j, :], lhsT=vn_all[:, ti, g * 128:(g + 1) * 128], rhs=Rsp[:, g, :],
                                                       start=True, stop=True), reads=[B_vnall[ti], B_R], writes=[bpst],
                              signal=(j == NB4 - 1))
                    pp = t4 % 2
                    sl4 = slice(t4 * NB4 * 128, (t4 + 1) * NB4 * 128)
                    v3 = lambda t: t[:, 0:NB4 * 128].rearrange("p (a b) -> p a b", a=NB4)
                    cx.op('dve', lambda e: e.tensor_tensor(out=v3(tA[pp]), in0=pst[:, 0:NB4, :],
                                                           in1=bsp_t[:, g:g + 1, :].to_broadcast([128, NB4, 128]), op=ALU.add),
                          reads=[bpst, B_bsp], writes=[B_tA[pp]])
                    cx.op('pool', lambda e: e.tensor_tensor(out=tBb[pp][:, 0:NB4 * 128], in0=tA[pp][:, 0:NB4 * 128], in1=uT[:, sl4], op=ALU.mult),
                          reads=[B_tA[pp], buT], writes=[B_tB[pp]])
                    cx.op('dve', lambda e: e.tensor_tensor(out=ot[:, sl4], in0=tBb[pp][:, 0:NB4 * 128], in1=szT[:, sl4], op=ALU.mult),
                          reads=[B_tB[pp], bszT], writes=[bot])
                cx.dma(oT_d[8 + g], ot[:], reads=[bot], writes=[B_oTd[8 + g]])
            cx.barrier()
        cx.cur = es

        chk(8)
        with ExitStack() as sO:
            cx.cur = sO
            wo_b = sb("wo_b", [128, NCH, D], BF16)
            B_wo = Buf("wo")
            stg = [sb(f"stgO{i}", [128, NCH, 128], F32) for i in range(2)]
            B_stg = [Buf("stgO0"), Buf("stgO1")]
            for c4 in range(16):
                f, bfb = stg[c4 % 2], B_stg[c4 % 2]
                cx.dma(f[:], w_o[:, :, c4 * 128:(c4 + 1) * 128], writes=[bfb])
                cx.op('pool' if c4 % 2 == 0 else 'dve',
                      lambda e: e.tensor_copy(out=wo_b[:, :, c4 * 128:(c4 + 1) * 128], in_=f[:]),
                      reads=[bfb], writes=[B_wo])
            fnw_t = sb("fnw_t", [128, D], F32)
            B_fnw = Buf("fnw")
            cx.dma(fnw_t[:], fnw.partition_broadcast(128), writes=[B_fnw])
            for c in range(NCH):
                cx.dma(xnT[:, c, :], oT_d[c], reads=[B_oTd[c]], writes=B_xnT)
            if dbg:
                with ExitStack() as sd:
                    cx.cur = sd
                    dtmp = sb("dbg_oT", [128, NCH, TM], F32)
                    bt = Buf("dbgoT")
                    cx.op('dve', lambda e: e.tensor_copy(out=dtmp[:], in_=xnT[:]), reads=B_xnT, writes=[bt])
                    cx.dma(dbg_out["d_oT"], dtmp[:], reads=[bt])
                    cx.barrier()
                cx.cur = sO
            xin = [sb(f"xinO{i}", [128, D], F32) for i in range(2)]
            B_xin = [Buf("xinO0"), Buf("xinO1")]
            junk = sb("junkO", [128, D], BF16)
            B_junk = Buf("junkO")
            for ti in range(NT):
                xi, bxi = xin[ti % 2], B_xin[ti % 2]
                cx.dma(xi[:], xcat[TM + ti * 128: TM + (ti + 1) * 128, :], writes=[bxi])
                bs = B_st[ti % 2]
                s0 = 32 + (ti % 2) * 8
                for n4 in range(4):
                    ps, bps = bigR.next()
                    for c in range(NCH):
                        cx.op('pe', lambda e: e.matmul(ps[:, :], lhsT=xnT[:, c, ti * 128:(ti + 1) * 128],
                                                       rhs=wo_b[:, c, n4 * 512:(n4 + 1) * 512],
                                                       start=(c == 0), stop=(c == NCH - 1)),
                              reads=[B_xnT[ti], B_wo], writes=[bps], signal=(c == NCH - 1))
                    cx.op('dve', lambda e: e.tensor_tensor(out=xi[:, n4 * 512:(n4 + 1) * 512], in0=ps[:, :],
                                                           in1=xi[:, n4 * 512:(n4 + 1) * 512], op=ALU.add),
                          reads=[bps, bxi], writes=[bxi])
                ssq = st16[:, s0:s0 + 1]
                cx.op('act', lambda e: e.activation(out=junk[:], in_=xi[:], func=AF.Square, accum_out=ssq),
                      reads=[bxi], writes=[B_junk, bs])
                rsqrt(ssq, ssq, 1.0 / D, [bs], [bs])
                cx.op('dve', lambda e: e.scalar_tensor_tensor(out=xi[:], in0=xi[:], scalar=ssq, in1=fnw_t[:],
                                                               op0=ALU.mult, op1=ALU.mult),
                      reads=[bxi, bs, B_fnw], writes=[bxi])
                cx.dma(out[ti * 128:(ti + 1) * 128, :], xi[:], reads=[bxi])
            cx.finish()
        print("instructions:", cx.nins, {k: v for k, v in cx.cnt.items()})
    except StopBuild:
        print("stopped early at", STOP[0])
    return nc


def prep_inputs(inp, NT, seq, nb):
    TM = NT * 128
    x = np.asarray(inp["x"], np.float32)
    w_in = np.asarray(inp["w_in"], np.float32)[0]
    w_out = np.asarray(inp["w_out"], np.float32)[0]

    def fm(cols):
        return np.ascontiguousarray(w_in[:, cols].reshape(NCH, 128, 128).transpose(1, 0, 2))
    groups = []
    for sec in (0, 1024, 2048, 3072, 4112, 6160):
        for g in range(8):
            groups.append(fm(slice(sec + g * 128, sec + (g + 1) * 128)))
    w_fm = np.stack(groups, 0)
    w_vb = np.ascontiguousarray(w_in[:, 5136:6160].reshape(NCH, 128, 1024).transpose(1, 0, 2))
    w_ba = np.ascontiguousarray(w_in[:, 4096:4112].reshape(NCH, 128, 16).transpose(1, 0, 2))
    w_o = np.ascontiguousarray(w_out.reshape(NCH, 128, D).transpose(1, 0, 2))
    convw = np.ascontiguousarray(np.asarray(inp["conv_w"], np.float32)[0].reshape(4, 24, 128).transpose(2, 1, 0))
    normw = np.ascontiguousarray(np.asarray(inp["norm_w"], np.float32)[0].reshape(NCH, 128).T)
    common = dict(
        w_fm=w_fm, w_vb=w_vb, w_ba=w_ba, w_o=w_o, convw=convw, normw=normw,
        alog=np.asarray(inp["a_log"], np.float32).reshape(1, 8),
        dtb=np.asarray(inp["dt_bias"], np.float32).reshape(1, 8),
        hnw=np.asarray(inp["head_norm_w"], np.float32).reshape(1, 128),
        lnw=np.asarray(inp["sgu_ln_w"], np.float32).reshape(1, 1024),
        lnb=np.asarray(inp["sgu_ln_b"], np.float32).reshape(1, 1024),
        wsp=np.ascontiguousarray(np.asarray(inp["w_spatial"], np.float32)[0].transpose(1, 0, 2)),
        bsp=np.asarray(inp["b_spatial"], np.float32).reshape(1, 1024),
        fnw=np.asarray(inp["final_norm_w"], np.float32).reshape(1, D),
        cst=make_consts(),
    )
    maps, ids = [], []
    nhalf = seq // TM
    for b in range(nb):
        for hf in range(nhalf):
            xc = np.zeros((2 * TM, D), np.float32)
            if hf > 0:
                xc[:TM] = x[b, (hf - 1) * TM: hf * TM]
            xc[TM:] = x[b, hf * TM:(hf + 1) * TM]
            m = dict(common)
            m["xcat"] = xc
            maps.append(m)
            ids.append((b, hf))
    return maps, ids


_CACHE = {}


def kernel(**inputs):
    NT = 16
    if NT not in _CACHE:
        _CACHE[NT] = build_program(NT)
    nc = _CACHE[NT]
    maps, ids = prep_inputs(inputs, NT, 4096, 4)
    res = run_bass_kernel_spmd(nc, maps, core_ids=list(range(8)))
    outp = np.zeros((4, 4096, D), np.float32)
    TM = NT * 128
    for (b, hf), r in zip(ids, res.results):
        outp[b, hf * TM:(hf + 1) * TM] = r["out"]
    return outp
```
